# Optimizing a Trainium2 kernel written in Bass

```python
import jax, jax.numpy as jnp
from jax import lax
import numpy as np

D_MODEL = 1024
BATCH = 2
SEQ = 8192
DEPTH = 1
DEC_BATCH = 8
DEC_SEQ = 16
PAST_LEN = 1024

CHUNK = 64
Q_BLOCK = 128
N_HEADS = 8
QK_NOPE = 64
QK_ROPE = 32
V_HEAD = 64
Q_LORA = 256
KV_LORA = 128
MLA_WIDTH = N_HEADS * V_HEAD
CONV_WIDTH = 512
CONV_K = 3
ROPE_BASE = 10000.0
EPS = 1e-6
SM_SCALE = (QK_NOPE + QK_ROPE) ** -0.5
NEG_INF = -1e30
IN_COLS = Q_LORA + KV_LORA + QK_ROPE + MLA_WIDTH + 4 * CONV_WIDTH + 2 * D_MODEL

kernel_name = "hybrid_mla_shortconv_stream_step"


def _rms_norm(x, g):
    xf = x.astype(jnp.float32)
    y = xf * lax.rsqrt(jnp.mean(xf * xf, axis=-1, keepdims=True) + EPS)
    return (y * g.astype(jnp.float32)).astype(x.dtype)


def _rope(x, pos):
    half = QK_ROPE // 2
    inv = ROPE_BASE ** (-jnp.arange(half, dtype=jnp.float32) / half)
    ang = pos.astype(jnp.float32)[:, None] * inv[None, :]
    cos, sin = jnp.cos(ang), jnp.sin(ang)
    if x.ndim == 4:
        cos, sin = cos[:, None, :], sin[:, None, :]
    xf = x.astype(jnp.float32)
    x1, x2 = xf[..., :half], xf[..., half:]
    return jnp.concatenate([x1 * cos - x2 * sin, x2 * cos + x1 * sin], axis=-1).astype(x.dtype)


def _split_cols(p):
    sizes = (Q_LORA, KV_LORA, QK_ROPE, MLA_WIDTH, CONV_WIDTH, CONV_WIDTH, CONV_WIDTH, CONV_WIDTH, D_MODEL, D_MODEL)
    out, start = [], 0
    for s in sizes:
        out.append(p[..., start:start + s])
        start += s
    return out


def _attend(q_abs, q_pe, c_kv, k_pe, qpos, kpos):
    s = (jnp.einsum('bthc,bsc->bhts', q_abs, c_kv)
         + jnp.einsum('bthr,bsr->bhts', q_pe, k_pe)).astype(jnp.float32) * SM_SCALE
    visible = (kpos[None, :] // CHUNK) <= (qpos[:, None] // CHUNK)
    s = jnp.where(visible[None, None], s, NEG_INF)
    p = jax.nn.softmax(s, axis=-1).astype(c_kv.dtype)
    return jnp.einsum('bhts,bsc->bthc', p, c_kv)


def _layer(x, pos, past_ckv, past_kpe, past_conv, blocked,
           pre_g, w_in, q_g, w_uq, kv_g, w_uk, w_uv, w_o_mla, conv_w, w_o_conv, w_out, post_g):
    b, t, _ = x.shape
    h = _rms_norm(x, pre_g)
    q_lat, kv_lat, k_rope, g_mla, c_b, c_c, c_x, g_conv, m_mla, m_conv = _split_cols(h @ w_in)

    q = (_rms_norm(q_lat, q_g) @ w_uq).reshape(b, t, N_HEADS, QK_NOPE + QK_ROPE)
    q_nope = q[..., :QK_NOPE]
    q_pe = _rope(q[..., QK_NOPE:], pos)
    c_kv = _rms_norm(kv_lat, kv_g)
    k_pe = _rope(k_rope, pos)
    q_abs = jnp.einsum('bthd,chd->bthc', q_nope, w_uk)
    if past_ckv is None:
        keys_c, keys_r, kpos = c_kv, k_pe, pos
    else:
        keys_c = jnp.concatenate([past_ckv, c_kv], axis=1)
        keys_r = jnp.concatenate([past_kpe, k_pe], axis=1)
        kpos = jnp.concatenate([jnp.arange(past_ckv.shape[1], dtype=jnp.int32), pos])
    if blocked:
        nb = t // Q_BLOCK

        def to_blocks(a):
            return jnp.moveaxis(a.reshape((b, nb, Q_BLOCK) + a.shape[2:]), 1, 0)

        o_lat = lax.map(lambda xs: _attend(xs[0], xs[1], keys_c, keys_r, xs[2], kpos),
                        (to_blocks(q_abs), to_blocks(q_pe), pos.reshape(nb, Q_BLOCK)))
        o_lat = jnp.moveaxis(o_lat, 0, 1).reshape(b, t, N_HEADS, KV_LORA)
    else:
        o_lat = _attend(q_abs, q_pe, keys_c, keys_r, pos, kpos)
    o = jnp.einsum('bthc,chd->bthd', o_lat, w_uv).reshape(b, t, MLA_WIDTH)
    branch_a = (o * jax.nn.silu(g_mla)) @ w_o_mla

    u = c_c * c_x
    u_ext = jnp.concatenate([past_conv, u], axis=1)
    conv = conv_w[0] * u_ext[:, 0:t]
    for k in range(1, CONV_K):
        conv = conv + conv_w[k] * u_ext[:, k:k + t]
    branch_b = (c_b * conv * jax.nn.silu(g_conv)) @ w_o_conv

    merged = jax.nn.sigmoid(m_mla) * branch_a + jax.nn.sigmoid(m_conv) * branch_b
    y = x + _rms_norm(merged @ w_out, post_g)
    return y, c_kv, k_pe, u_ext[:, -(CONV_K - 1):]


def setup_inputs(seed: int = 0) -> dict:
    key = jax.random.key(seed)
    ks = jax.random.split(key, 17)
    f32 = jnp.float32
    nrm = lambda k, shape, s=1.0: jax.random.normal(k, shape, f32) * s
    gain = lambda k, n: 1.0 + 0.01 * jax.random.normal(k, (DEPTH, n), f32)
    return {
        "x_prompt": nrm(ks[0], (BATCH, SEQ, D_MODEL)),
        "x_sample": nrm(ks[1], (DEC_BATCH, DEC_SEQ, D_MODEL)),
        "cache_kv_latent": nrm(ks[2], (DEPTH, DEC_BATCH, PAST_LEN, KV_LORA)),
        "cache_k_rope": nrm(ks[3], (DEPTH, DEC_BATCH, PAST_LEN, QK_ROPE)),
        "state_conv": nrm(ks[4], (DEPTH, DEC_BATCH, CONV_K - 1, CONV_WIDTH)),
        "pre_norm": gain(ks[5], D_MODEL),
        "w_in": nrm(ks[6], (DEPTH, D_MODEL, IN_COLS), D_MODEL ** -0.5),
        "q_norm": gain(ks[7], Q_LORA),
        "w_uq": nrm(ks[8], (DEPTH, Q_LORA, N_HEADS * (QK_NOPE + QK_ROPE)), Q_LORA ** -0.5),
        "kv_norm": gain(ks[9], KV_LORA),
        "w_uk": nrm(ks[10], (DEPTH, KV_LORA, N_HEADS, QK_NOPE), KV_LORA ** -0.5),
        "w_uv": nrm(ks[11], (DEPTH, KV_LORA, N_HEADS, V_HEAD), KV_LORA ** -0.5),
        "w_o_mla": nrm(ks[12], (DEPTH, MLA_WIDTH, D_MODEL), MLA_WIDTH ** -0.5),
        "conv_w": nrm(ks[13], (DEPTH, CONV_K, CONV_WIDTH), CONV_K ** -0.5),
        "w_o_conv": nrm(ks[14], (DEPTH, CONV_WIDTH, D_MODEL), CONV_WIDTH ** -0.5),
        "w_out": nrm(ks[15], (DEPTH, D_MODEL, D_MODEL), D_MODEL ** -0.5),
        "post_norm": gain(ks[16], D_MODEL),
    }


def reference(x_prompt, x_sample, cache_kv_latent, cache_k_rope, state_conv,
              pre_norm, w_in, q_norm, w_uq, kv_norm, w_uk, w_uv, w_o_mla, conv_w, w_o_conv, w_out, post_norm):
    yp, ys = x_prompt, x_sample
    pos_p = jnp.arange(x_prompt.shape[1], dtype=jnp.int32)
    pos_s = cache_kv_latent.shape[2] + jnp.arange(x_sample.shape[1], dtype=jnp.int32)
    ckv_p, kpe_p, cv_p, ckv_s, kpe_s, cv_s = [], [], [], [], [], []
    for l in range(DEPTH):
        w = (pre_norm[l], w_in[l], q_norm[l], w_uq[l], kv_norm[l], w_uk[l], w_uv[l],
             w_o_mla[l], conv_w[l], w_o_conv[l], w_out[l], post_norm[l])
        pad = jnp.zeros((yp.shape[0], CONV_K - 1, CONV_WIDTH), yp.dtype)
        yp, a, r, c = _layer(yp, pos_p, None, None, pad, True, *w)
        ckv_p.append(a); kpe_p.append(r); cv_p.append(c)
        ys, a, r, c = _layer(ys, pos_s, cache_kv_latent[l], cache_k_rope[l], state_conv[l], False, *w)
        ckv_s.append(a); kpe_s.append(r); cv_s.append(c)
    return (yp, ys, jnp.stack(ckv_p), jnp.stack(kpe_p), jnp.stack(cv_p),
            jnp.stack(ckv_s), jnp.stack(kpe_s), jnp.stack(cv_s))
```

```python
import numpy as np
from contextlib import ExitStack
import concourse.bass as bass
import concourse.mybir as mybir
from concourse.bass_utils import run_bass_kernel_spmd

F32 = mybir.dt.float32
BF = mybir.dt.bfloat16
AF = mybir.ActivationFunctionType
ALU = mybir.AluOpType

D_MODEL = 1024
SEQ = 8192
N_HEADS = 8
QK_NOPE = 64
QK_ROPE = 32
KV_LORA = 128
Q_LORA = 256
CONV_W = 512
PAST = 1024
DEC_SEQ = 16
EPS = 1e-6
SM_SCALE = float((QK_NOPE + QK_ROPE) ** -0.5)
SB0 = 0
NCT = 38
TP = 256
TGP = 260
ENGS = ["sync", "tensor", "scalar", "vector", "gpsimd"]


class Op:
    __slots__ = ("eng", "fn", "deps", "signal", "val", "key", "is_dma", "group")

    def __init__(self, eng, fn, deps):
        self.eng, self.fn, self.deps = eng, fn, deps
        self.signal = False
        self.val = 0
        self.key = None
        self.is_dma = False
        self.group = False


class Res:
    __slots__ = ("writers", "readers", "excl")

    def __init__(self, excl=False):
        self.writers = []
        self.readers = []
        self.excl = excl


class Prog:
    def __init__(self):
        self.streams = {e: [] for e in ENGS}
        self.dma_cnt = {}
        self.final = []

    def _flat(self, deps):
        out = []
        for d in deps:
            if d is None:
                continue
            if isinstance(d, (list, tuple)):
                out.extend(self._flat(d))
            else:
                out.append(d)
        return out

    def op(self, eng, fn, deps=()):
        o = Op(eng, fn, self._flat(deps))
        for d in o.deps:
            d.signal = True
        self.streams[eng].append(o)
        return o

    def dma(self, eng, fn, key, deps=(), group=False):
        o = self.op(eng, fn, deps)
        o.is_dma = True
        o.key = key
        o.group = group
        n = self.dma_cnt.get(key, 0) + 1
        self.dma_cnt[key] = n
        o.val = 16 * n
        o.signal = True
        return o

    def I(self, eng, fn, reads=(), writes=(), deps=(), wadd=False, dma_key=None, group=False):
        d = list(deps)
        for r in reads:
            d += r.writers
            if r.excl:
                d += [x for x in r.readers if x.eng != eng]
        for w in writes:
            d += w.readers
            if not wadd:
                d += w.writers
        if dma_key is None:
            o = self.op(eng, fn, d)
        else:
            o = self.dma(eng, fn, dma_key, d, group)
        keep = (lambda lst: [x for x in lst if x.is_dma or x.eng != eng]) if dma_key is None else (lambda lst: list(lst))
        for r in reads:
            r.readers = keep(r.readers) + [o]
        for w in writes:
            if wadd:
                w.writers = keep(w.writers) + [o]
            else:
                w.writers = [o]
                w.readers = []
        return o

    def emit(self, nc, es):
        for eng in ENGS:
            cnt = 0
            for o in self.streams[eng]:
                if o.is_dma:
                    if o.group:
                        o.val = 16 * self.dma_cnt[o.key]
                elif o.signal:
                    cnt += 1
                    o.val = cnt
        sems = {}
        for eng in ENGS[1:]:
            sems[eng] = es.enter_context(nc.semaphore("s_" + eng))
        for key in self.dma_cnt:
            sems["d_" + key] = es.enter_context(nc.semaphore("d_" + key))
        block = es.enter_context(nc.Block())
        final = self.final

        def make(eng):
            ops = self.streams[eng]

            def body(e):
                waited = {}

                def wait_for(d):
                    name = ("d_" + d.key) if d.is_dma else d.eng
                    if waited.get(name, 0) >= d.val:
                        return
                    waited[name] = d.val
                    e.wait_ge(sems[name], d.val)

                for o in ops:
                    for d in o.deps:
                        wait_for(d)
                    ins = o.fn(e)
                    if o.is_dma:
                        ins.then_inc(sems["d_" + o.key], 16)
                    elif o.signal:
                        ins.then_inc(sems[eng], 1)
                if eng == "sync":
                    for d in final:
                        wait_for(d)
            return body

        block.sync(make("sync"))
        block.tensor(make("tensor"))
        block.scalar(make("scalar"))
        block.vector(make("vector"))
        block.gpsimd(make("gpsimd"))


def build_program(wseq_in=None):
    record = wseq_in is None
    nc = bass.Bass("TRN2", target_bir_lowering=False)
    P = Prog()
    I = P.I

    def din(name, shape, dt=F32):
        return nc.dram_tensor(name, list(shape), dt, kind="ExternalInput").ap()

    def dout(name, shape):
        return nc.dram_tensor(name, list(shape), F32, kind="ExternalOutput").ap()

    d_xT_seq = din("xT_seq", [D_MODEL, SEQ])
    d_xT_own = din("xT_own", [D_MODEL, 8 * TGP])
    d_x_own = din("x_own", [2048, D_MODEL])
    d_xT_smp = din("xT_smp", [D_MODEL, DEC_SEQ])
    d_x_smp = din("x_smp", [DEC_SEQ, D_MODEL])
    d_cache_kvT = din("cache_kvT", [128, PAST])
    d_cache_kv = din("cache_kv", [PAST, 128])
    d_cache_kpeT = din("cache_kpeT", [32, PAST])
    d_stateT = din("stateT", [128, 4, 2])
    d_mbias = din("mbias", [128, 128])
    d_w_in_t = din("w_in_t", [NCT, 128, 1024])
    d_w_kv = din("w_kv", [128, 8, 160])
    d_w_uqT = din("w_uqT_nope", [64, 8, 256])
    d_w_ukT = din("w_ukT", [64, 8, 128])
    d_w_uq_rope = din("w_uq_rope", [128, 2, 256])
    d_w_uv = din("w_uv", [128, 8, 64])
    d_w_o_mla = din("w_o_mla", [512, 1024])
    d_w_o_conv = din("w_o_conv", [512, 1024])
    d_w_out = din("w_out", [1024, 1024])
    d_pre_g = din("pre_g", [128, 8])
    d_q_g = din("q_g", [128, 2])
    d_kv_g = din("kv_g", [1, 128])
    d_post_g = din("post_g", [1, 1024])
    d_conv_w = din("conv_w_t", [128, 4, 3])
    d_cos_k = din("cos_k", [128, 64, 16])
    d_sin_k = din("sin_k", [128, 64, 16])
    d_cos_q = din("cos_q", [128, 16, 16])
    d_sin_q = din("sin_q", [128, 16, 16])
    d_cos_s = din("cos_s", [16, 16])
    d_sin_s = din("sin_s", [16, 16])
    d_ident = din("ident", [128, 128])

    d_scr = nc.dram_tensor("w_scr", [NCT, 128, 1024], BF, kind="Internal").ap()

    o_y = dout("o_y", [2048, D_MODEL])
    o_ys = dout("o_ys", [DEC_SEQ, D_MODEL])
    o_ckv = dout("o_ckv", [SEQ, 128])
    o_kpe = dout("o_kpe", [SEQ, 32])
    o_convp = dout("o_convp", [128, 4, 2])
    o_ckv_s = dout("o_ckv_s", [DEC_SEQ, 128])
    o_kpe_s = dout("o_kpe_s", [DEC_SEQ, 32])
    o_convs = dout("o_convs", [128, 4, 2])

    with ExitStack() as es:
        def sb(name, shape, dt):
            return es.enter_context(nc.sbuf_tensor(name, list(shape), dt))

        ckvT_all = sb("ckvT_all", [128, SEQ], BF)
        kpeT_all = sb("kpeT_all", [128, 2048], BF)
        V_all = sb("V_all", [128, 64, 130], BF)
        w_out_sb = sb("w_out_sb", [128, 8, 1024], BF)
        w_o_mla_sb = sb("w_o_mla_sb", [128, 4, 1024], BF)
        w_o_conv_sb = sb("w_o_conv_sb", [128, 4, 1024], BF)
        W_abs_sb = sb("W_abs_sb", [128, 2, 8, 128], BF)
        w_uq_rope_sb = sb("w_uq_rope_sb", [128, 2, 256], BF)
        w_uv_pad = sb("w_uv_pad", [128, 8, 128], BF)
        w_kv_sb = sb("w_kv_sb", [128, 8, 160], BF)
        scrA = sb("scrA", [128, 2048], F32)
        scrB = sb("scrB", [128, 1024], F32)
        w_uqT_sb = scrA[0:64, :].rearrange("p (h m) -> p h m", h=8)
        w_ukT_sb = scrB[0:64, :].rearrange("p (h c) -> p h c", h=8)
        post_g_bc = sb("post_g_bc", [128, 1024], F32)
        kv_g_bc = sb("kv_g_bc", [128, 128], F32)
        ident_bf = sb("ident_bf", [128, 128], BF)
        ones_bf = sb("ones_bf", [128, 128], BF)
        eps_t = sb("eps_t", [128, 1], F32)
        pre_g_sb = sb("pre_g_sb", [128, 8], F32)
        q_g_sb = sb("q_g_sb", [128, 2], F32)
        conv_w_sb = sb("conv_w_sb", [128, 4, 3], F32)
        ck_ring = [sb(f"ck{i}", [128, 2, 16], F32) for i in range(2)]
        sk_ring = [sb(f"sk{i}", [128, 2, 16], F32) for i in range(2)]
        cos_q = sb("cos_q_sb", [128, 16, 16], F32)
        sin_q = sb("sin_q_sb", [128, 16, 16], F32)
        cos_s = sb("cos_s_sb", [16, 16], F32)
        sin_s = sb("sin_s_sb", [16, 16], F32)
        stateT = sb("stateT_sb", [128, 4, 2], F32)
        convst = sb("convst", [128, 4, 2], F32)

        xT_ring = [sb(f"xT{i}", [128, 8, TGP], F32) for i in range(2)]
        xsq_b = [sb(f"xsq{i}", [128, 8, TGP], BF) for i in range(2)]
        lnt_b = [sb(f"lnt{i}", [128, TGP], F32) for i in range(2)]
        rstd_b = [sb(f"rstd{i}", [128, TGP], F32) for i in range(2)]
        hT_p = [sb(f"hTp{i}", [128, 8, TP], BF) for i in range(2)]
        hT_gs = [sb(f"hTg{i}", [128, 8, TGP], BF) for i in range(2)]
        WR = 6
        w_ring = [sb(f"wr{i}", [128, 1024], BF) for i in range(WR)]
        stg_ckv = [sb(f"stgc{i}", [128, 2, 128], F32) for i in range(2)]
        stg_kpe = [sb(f"stgk{i}", [128, 2, 32], F32) for i in range(2)]
        kpe_bf = [sb(f"kpebf{i}", [128, 128], BF) for i in range(2)]
        junkk_b = [sb(f"junkk{i}", [128, 128], BF) for i in range(2)]
        smallk_b = [sb(f"smallk{i}", [128, 8], F32) for i in range(2)]
        kropeA_b = [sb(f"kropeA{i}", [128, 32], F32) for i in range(2)]
        kropeB_b = [sb(f"kropeB{i}", [128, 32], F32) for i in range(2)]
        small = sb("small", [128, 16], F32)
        ropeA = scrB[:, 0:256]
        ropeB = scrB[:, 256:512]
        qlat = scrB[:, 512:1024].rearrange("p (k t) -> p k t", k=2)
        qsq = sb("qsq", [128, 2, 256], BF)
        lnq = sb("lnq", [128, 256], F32)
        rstdq = sb("rstdq", [128, 256], F32)
        qnT = sb("qnT", [128, 2, 256], BF)
        q_absT = sb("q_absT", [128, 8, 256], BF)
        siluG = sb("siluG", [128, 4, 256], BF)
        convres = sb("convres", [128, 4, 256], BF)
        ogT = sb("ogT", [128, 4, 256], BF)
        mergedT = sb("mergedT", [128, 8, 256], BF)
        cc_sb = [sb(f"ccsb{i}", [128, TGP], F32) for i in range(1)] * 2
        uext = [sb(f"uext{i}", [128, 2, 130], F32) for i in range(2)]
        cvt = [sb(f"cvt{i}", [128, 256], F32) for i in range(1)] * 2
        sgc = [sb(f"sgc{i}", [128, 256], F32) for i in range(2)]
        sga = [sb(f"sga{i}", [128, 256], F32) for i in range(1)] * 2
        sgb = [sb(f"sgb{i}", [128, 256], F32) for i in range(1)] * 2
        t1 = [sb(f"t1_{i}", [128, 256], F32) for i in range(1)] * 2
        t2 = [sb(f"t2_{i}", [128, 256], F32) for i in range(1)] * 2
        orep = sb("orep", [128, 8, 4, 32], BF)
        q_peT = [sb(f"qpeT{i}", [128, 8, 128], BF) for i in range(2)]
        mbias_sb = sb("mbias_sb", [128, 128], F32)
        qpm = [sb(f"qpm{i}", [128, 8, 128], BF) for i in range(2)]
        maskr = sb("maskr", [128, 4], F32)
        PT = [sb(f"PT{i}", [128, 8, 128], BF) for i in range(3)]
        rden = sb("rden", [128, 8], F32)
        olat = sb("olat", [128, 8, 128], BF)
        olatT = sb("olatT", [128, 8, 128], BF)
        xtok = scrA[:, 0:1024]
        yt = scrA[:, 1024:2048]

        g01 = es.enter_context(nc.psum_tensor("g01", [128, 2, 512], F32))
        g23 = es.enter_context(nc.psum_tensor("g23", [128, 2, 512], F32))
        o456 = es.enter_context(nc.psum_tensor("o456", [128, 3, 512], F32))
        m7 = es.enter_context(nc.psum_tensor("m7", [128, 512], F32))
        gbank_t = [g01, g01, g23, g23]
        G_res = [Res(True) for _ in range(4)]
        O_res = [Res(True) for _ in range(3)]
        M_res = Res(True)
        gctr = {"s": 0, "p": 0}

        def g_single():
            b = gctr["s"] % 3
            gctr["s"] += 1
            assert not (G_res[b].writers and not G_res[b].readers), "PSUM bank handed out while still live"
            return gbank_t[b][:, b % 2, :], G_res[b]

        def g_pair():
            p = gctr["p"] % 2
            gctr["p"] += 1
            return (g01, (G_res[0], G_res[1])) if p == 0 else (g23, (G_res[2], G_res[3]))

        def k_bank(sel=3):
            if sel == 7:
                return m7[:, :], M_res
            return g23[:, sel - 2, :], G_res[sel]

        R = {}

        def res(name):
            if name not in R:
                R[name] = Res()
            return R[name]

        def MM(out, lhsT, rhs, start, stop, reads=(), writes=(), wadd=False, tp=None, sgcheck=False, deps=()):
            def fn(e):
                kw = {}
                if tp is not None:
                    kw["tile_position"] = tp
                if sgcheck:
                    kw["skip_group_check"] = True
                return e.matmul(out, lhsT=lhsT, rhs=rhs, start=start, stop=stop, **kw)
            return I("tensor", fn, reads, writes, deps, wadd)

        def TR(out, in_, n, reads=(), writes=(), wadd=False, deps=()):
            return I("tensor", lambda e: e.transpose(out, in_, ident_bf[:n, :n]), reads, writes, deps, wadd)

        def ACT(out, in_, func, reads=(), writes=(), wadd=False, scale=None, bias=None, accum=None, deps=()):
            def fn(e):
                kw = {}
                if scale is not None:
                    kw["scale"] = scale
                if bias is not None:
                    kw["bias"] = bias
                if accum is not None:
                    kw["accum_out"] = accum
                return e.activation(out=out, in_=in_, func=func, **kw)
            return I("scalar", fn, reads, writes, deps, wadd)

        def TT(eng, out, in0, in1, op, reads=(), writes=(), wadd=False, deps=()):
            return I(eng, lambda e: e.tensor_tensor(out=out, in0=in0, in1=in1, op=op), reads, writes, deps, wadd)

        def STT(out, in0, scalar, in1, op0, op1, reads=(), writes=(), wadd=False, deps=()):
            return I("vector", lambda e: e.scalar_tensor_tensor(out=out, in0=in0, scalar=scalar, in1=in1, op0=op0, op1=op1),
                     reads, writes, deps, wadd)

        def TS(eng, out, in0, s1, op0, reads=(), writes=(), wadd=False, deps=()):
            return I(eng, lambda e: e.tensor_scalar(out=out, in0=in0, scalar1=s1, scalar2=None, op0=op0),
                     reads, writes, deps, wadd)

        def CP(eng, out, in_, reads=(), writes=(), wadd=False, deps=()):
            if eng == "scalar":
                return ACT(out, in_, AF.Copy, reads, writes, wadd, deps=deps)
            return I(eng, lambda e: e.tensor_copy(out=out, in_=in_), reads, writes, deps, wadd)

        def DMA(eng, out, in_, key, reads=(), writes=(), deps=(), group=False, wadd=False):
            return I(eng, lambda e: e.dma_start(out=out, in_=in_), reads, writes, deps, wadd, dma_key=key, group=group)

        def MEMSET(out, val, writes=(), wadd=False, deps=()):
            return I("gpsimd", lambda e: e.memset(out, val), (), writes, deps, wadd)

        c_res = res("consts")
        MEMSET(eps_t[:], EPS, [c_res])
        MEMSET(ones_bf[:], 1.0, [c_res], wadd=True)
        MEMSET(V_all[:, :, 128:130], 1.0, [c_res], wadd=True)
        ms_uv = MEMSET(w_uv_pad[:], 0.0, [c_res], wadd=True)
        ms_kpe = MEMSET(kpeT_all[:], 0.0, [c_res], wadd=True)
        for i in range(2):
            MEMSET(qpm[i][:], 0.0, [c_res], wadd=True)
        ms_mr = MEMSET(maskr[:], 0.0, [c_res], wadd=True)
        for r_ in range(4):
            MEMSET(maskr[32 * r_:32 * r_ + 32, r_:r_ + 1], 1.0, [c_res], wadd=True, deps=[ms_mr])
        for i in range(2):
            MEMSET(kpe_bf[i][:], 0.0, [c_res], wadd=True)

        gq = "gpsimd"
        iw = dict(key="initw", group=True, writes=[c_res], wadd=True)
        DMA(gq, ident_bf[:], d_ident, **iw)
        DMA(gq, w_kv_sb[:], d_w_kv, **iw)
        DMA(gq, w_uq_rope_sb[:], d_w_uq_rope, **iw)
        uvp = w_uv_pad[:].rearrange("p (j two) c -> p j two c", two=2)
        uvd = d_w_uv.rearrange("p (j two) c -> p j two c", two=2)
        DMA(gq, uvp[:, :, 0, 0:64], uvd[:, :, 0, :], deps=[ms_uv], **iw)
        DMA(gq, uvp[:, :, 1, 64:128], uvd[:, :, 1, :], deps=[ms_uv], **iw)
        wprep = []
        def emit_wprep():
            for i in range(NCT // 2):
                wprep.append(P.dma(gq, (lambda i: (lambda e: e.dma_start(
                    out=d_scr[2 * i:2 * i + 2].rearrange("a p n -> (a p) n"),
                    in_=d_w_in_t[2 * i:2 * i + 2].rearrange("a p n -> (a p) n"))))(i), f"wp{i}", ()))

        cw2_res = res("consts2")

        def emit_initw2():
            iw2 = dict(key="initw2", group=True, writes=[cw2_res], wadd=True)
            DMA(gq, w_o_conv_sb[:], d_w_o_conv.rearrange("(k p) n -> p k n", p=128), **iw2)
            DMA(gq, w_o_mla_sb[:], d_w_o_mla.rearrange("(k p) n -> p k n", p=128), **iw2)
            DMA(gq, w_out_sb[:], d_w_out.rearrange("(k p) n -> p k n", p=128), **iw2)


        cw3_res = res("consts3")

        def emit_cache(deps):
            iw3 = dict(key="initw3", group=True, writes=[cw3_res], wadd=True, deps=deps)
            DMA(gq, ckvT_all[:, 0:PAST], d_cache_kvT, **iw3)
            DMA(gq, kpeT_all[0:32, 0:PAST], d_cache_kpeT, **iw3)
            DMA(gq, V_all[:, 0:8, 0:128], d_cache_kv.rearrange("(b p) c -> p b c", p=128), **iw3)

        ic = dict(key="initc", group=True, writes=[c_res], wadd=True)
        DMA("sync", pre_g_sb[:], d_pre_g, **ic)
        DMA("sync", q_g_sb[:], d_q_g, **ic)
        DMA("sync", conv_w_sb[:], d_conv_w, **ic)
        DMA("sync", kv_g_bc[:], d_kv_g[0:1, :].broadcast_to([128, 128]), **ic)
        DMA("sync", post_g_bc[:], d_post_g[0:1, :].broadcast_to([128, 1024]), **ic)
        DMA("sync", mbias_sb[:], d_mbias, **ic)
        DMA("sync", cos_s[:], d_cos_s, **ic)
        DMA("sync", sin_s[:], d_sin_s, **ic)
        DMA("sync", stateT[:], d_stateT, **ic)
        DMA("sync", w_uqT_sb, d_w_uqT, **ic)
        DMA("sync", w_ukT_sb, d_w_ukT, **ic)
        DMA("sync", cos_q[:], d_cos_q, **ic)
        DMA("sync", sin_q[:], d_sin_q, **ic)

        wabs_mm = []
        for kc in range(2):
            for hq in range(2):
                bank, br = g_single()
                for hh in range(4):
                    h = hq * 4 + hh
                    wabs_mm.append(MM(bank[:, hh * 128:(hh + 1) * 128], w_uqT_sb[0:64, h, kc * 128:(kc + 1) * 128],
                                      w_ukT_sb[0:64, h, :], True, True, reads=[c_res], writes=[br], wadd=(hh > 0)))
                CP("vector", W_abs_sb[:, kc, hq * 4:hq * 4 + 4, :],
                   bank.rearrange("p (h c) -> p h c", h=4), reads=[br], writes=[c_res], wadd=True)

        wq = {"issued": 0, "used": 0, "total": 0}
        w_res = [Res() for _ in range(WR)]

        def w_issue():
            if record:
                return
            i = wq["issued"]
            if i >= wq["total"]:
                return
            ct = wseq[i]
            s = i % WR
            DMA("sync", w_ring[s][:], d_scr[ct], key=f"w{s}", writes=[w_res[s]], deps=[wprep[ct // 2]])
            wq["issued"] += 1

        def w_take(ct):
            i = wq["used"]
            if record:
                wseq.append(ct)
            else:
                assert wseq[i] == ct, (i, wseq[i], ct)
            wq["used"] += 1
            s = i % WR
            return w_ring[s], w_res[s]

        wseq = [] if record else list(wseq_in)
        wq["total"] = len(wseq)

        xctr = {"i": 0}
        xT_res = [Res(), Res()]

        def norm_tile(src, T, hT, hT_r, kbsel=3, spaced=False):
            s = xctr["i"] % 2
            xctr["i"] += 1
            xt, xr = xT_ring[s], xT_res[s]
            xsq, lnt, rstd = xsq_b[s], lnt_b[s], rstd_b[s]
            xsq_r, lnt_r, rstd_r = res(f"xsq{s}"), res(f"lnt{s}"), res(f"rstd{s}")
            DMA("sync", xt[:, :, 0:T], src, key=f"xT{s}", writes=[xr])
            TT("vector", xsq[:, :, 0:T], xt[:, :, 0:T], xt[:, :, 0:T], ALU.mult, reads=[xr], writes=[xsq_r])
            if spaced:
                for _ in range(12 if spaced is True else int(spaced)):
                    yield
            bank, br = k_bank(kbsel)
            for k in range(8):
                MM(bank[:, 0:T], ones_bf[:, :], xsq[:, k, 0:T], k == 0, k == 7,
                   reads=[xsq_r, c_res], writes=[br], wadd=(k > 0))
            ACT(lnt[:, 0:T], bank[:, 0:T], AF.Ln, reads=[br], writes=[lnt_r], scale=1.0 / D_MODEL, bias=eps_t[:, 0:1])
            ACT(rstd[:, 0:T], lnt[:, 0:T], AF.Exp, reads=[lnt_r], writes=[rstd_r], scale=-0.5)
            for k in range(8):
                STT(hT[:, k, 0:T], xt[:, k, 0:T], pre_g_sb[:, k:k + 1], rstd[:, 0:T], ALU.mult, ALU.mult,
                    reads=[xr, rstd_r, c_res], writes=[hT_r], wadd=(k > 0))
            yield
            if spaced is True:
                for _ in range(12):
                    yield

        kv_ready = {}
        kctr = {"i": 0}
        kpcol_res = [Res() for _ in range(16)]
        small_r = res("small")
        ropeA_r, ropeB_r = res("ropeA"), res("ropeB")
        for nm in ("ropeA", "ropeB", "qlat", "xtok", "yt"):
            res(nm).readers = list(wabs_mm[-1:])

        def key_tile(hT, hT_r, blocks, cos_of, sin_of, out_ckv, out_kpe, extra_deps=(), kbsel=3, spaced=False, tab_res=()):
            s = kctr["i"] % 2
            kctr["i"] += 1
            stc, stk, kb = stg_ckv[s], stg_kpe[s], kpe_bf[s]
            stc_r, stk_r, kb_r = res(f"stgc{s}"), res(f"stgk{s}"), res(f"kpebf{s}")
            junkk, smallk, kropeA, kropeB = junkk_b[s], smallk_b[s], kropeA_b[s], kropeB_b[s]
            smallk_r, junkk_r = res(f"smallk{s}"), res(f"junkk{s}")
            kropeA_r, kropeB_r = res(f"kropeA{s}"), res(f"kropeB{s}")
            bank, br = k_bank(kbsel)
            for bi, (off, n, blk) in enumerate(blocks):
                for k in range(8):
                    MM(bank[:n, bi * 160:(bi + 1) * 160], hT[:, k, off:off + n], w_kv_sb[:, k, :], k == 0, k == 7,
                       reads=[hT_r, c_res], writes=[br], wadd=(bi > 0 or k > 0))
            yield
            tb = bank[:, 320:448].bitcast(BF)
            tr = br
            for bi, (off, n, blk) in enumerate(blocks):
                r = blk // 16
                kvr = Res()
                kv_ready[blk] = kvr
                kvp = bank[:n, bi * 160:bi * 160 + 128]
                rp = bank[:n, bi * 160 + 128:bi * 160 + 160].rearrange("p (a b) -> p a b", a=2)
                ACT(junkk[:n, 0:128], kvp, AF.Square, reads=[br], writes=[junkk_r, smallk_r], accum=smallk[:n, 0:1])
                ACT(smallk[:n, 1:2], smallk[:n, 0:1], AF.Ln, reads=[smallk_r], writes=[smallk_r],
                    scale=1.0 / KV_LORA, bias=eps_t[:n, 0:1])
                ACT(smallk[:n, 2:3], smallk[:n, 1:2], AF.Exp, reads=[smallk_r], writes=[smallk_r], scale=-0.5)
                cb = cos_of(blk, n).unsqueeze(1).broadcast_to([n, 2, 16])
                sbb = sin_of(blk, n).unsqueeze(1).broadcast_to([n, 2, 16])
                A3 = kropeA[:n, 0:32].rearrange("p (a b) -> p a b", a=2)
                B3 = kropeB[:n, 0:32].rearrange("p (a b) -> p a b", a=2)
                TT("vector", A3, rp, cb, ALU.mult, reads=[br, c_res] + list(tab_res), writes=[kropeA_r])
                TT("vector", B3, rp, sbb, ALU.mult, reads=[br, c_res] + list(tab_res), writes=[kropeB_r])
                TT("vector", stk[:n, bi, 0:16], kropeA[:n, 0:16], kropeB[:n, 16:32], ALU.subtract,
                   reads=[kropeA_r, kropeB_r], writes=[stk_r], wadd=(bi > 0))
                TT("vector", stk[:n, bi, 16:32], kropeA[:n, 16:32], kropeB[:n, 0:16], ALU.add,
                   reads=[kropeA_r, kropeB_r], writes=[stk_r], wadd=True)
                CP("gpsimd", kb[:n, 32 * r:32 * r + 32], stk[:n, bi, :], reads=[stk_r], writes=[kb_r])
                STT(stc[:n, bi, :], kvp, smallk[:n, 2:3], kv_g_bc[:n, :], ALU.mult, ALU.mult,
                    reads=[br, smallk_r, c_res], writes=[stc_r], wadd=(bi > 0))
                CP("gpsimd", V_all[:n, blk, 0:128], stc[:n, bi, :], reads=[stc_r], writes=[kvr], deps=extra_deps)
                yield
                if spaced:
                    for _ in range(9):
                        yield
                TR(tb[:, 0:n], V_all[:n, blk, 0:128], n, reads=[kvr, c_res], writes=[tr], wadd=True)
                TR(tb[:, 128:128 + n], kb[:n, :], n, reads=[kb_r, c_res], writes=[tr], wadd=True)
                yield
                CP("vector", ckvT_all[:, blk * 128:blk * 128 + n], tb[:, 0:n],
                   reads=[tr], writes=[kvr], wadd=True, deps=extra_deps)
                c0 = (blk % 16) * 128
                CP("vector", kpeT_all[32 * r:32 * r + 32, c0:c0 + n], tb[32 * r:32 * r + 32, 128:128 + n],
                   reads=[tr], writes=[kvr, kpcol_res[blk % 16]], wadd=True, deps=extra_deps)
            nb = len(blocks)
            n0 = blocks[0][1]
            outs = []
            outs.append(DMA("sync", out_ckv, stc[:n0, 0:nb, :] if nb > 1 else stc[:n0, 0, :], key=f"stc{s}", reads=[stc_r]))
            outs.append(DMA("sync", out_kpe, stk[:n0, 0:nb, :] if nb > 1 else stk[:n0, 0, :], key=f"stk{s}", reads=[stk_r]))
            P.final.extend(outs)
            yield

        BG = []
        bgc = {"left": 0}

        def run(gen):
            for _ in gen:
                pass

        def bg_add(gen, nchunks):
            BG.append(gen)
            bgc["left"] += nchunks

        def bg_step(n):
            while n > 0 and BG:
                k_ = bgc.get("rr", 0) % min(2, len(BG))
                bgc["rr"] = bgc.get("rr", 0) + 1
                try:
                    next(BG[k_])
                    n -= 1
                    bgc["left"] = max(bgc["left"] - 1, 0)
                except StopIteration:
                    BG.pop(k_)

        def bg_drain():
            while BG:
                bg_step(1000)
            bgc["left"] = 0

        qlat_r, qsq_r, lnq_r, rstdq_r, qnT_r = res("qlat"), res("qsq"), res("lnq"), res("rstdq"), res("qnT")
        qabs_r, siluG_r, convres_r, ogT_r, merged_r = res("qabs"), res("siluG"), res("convres"), res("ogT"), res("merged")
        hTg_rs = [res("hTg0"), res("hTg1")]
        orep_r, rden_r, olat_r, olatT_r, xtok_r, yt_r = res("orep"), res("rden"), res("olat"), res("olatT"), res("xtok"), res("yt")
        convst_r = res("convst")
        ring2 = {}

        def nxt(name):
            i = ring2.get(name, 0)
            ring2[name] = i + 1
            return i % 2

        ptctr = {"i": 0}
        PT_res = [Res() for _ in range(3)]
        qpeT_res = [Res(), Res()]
        qpm_res = [Res(), Res()]
        qpmctr = {"i": 0}
        qpm_state = [None, None]
        attn_last = {}

        def inproj_ct(ct, N, hT_g, hTg_r):
            wt, wr = w_take(ct)
            bank, br = g_single()
            for k in range(8):
                MM(bank[:, 0:N], wt[:, k * 128:(k + 1) * 128], hT_g[:, k, 0:N], k == 0, k == 7,
                   reads=[wr, hTg_r], writes=[br], wadd=(k > 0))
            w_issue()
            if BG and bgc.get("inproj", False):
                bg_step(1)
            return bank, br

        def part1_gen(T, n_halo, qblocks, conv_state_src, conv_out, name, hT_g, hTg_r, xsrc, spaced_norm=False):
            yield from norm_tile(xsrc, T + n_halo, hT_g, hTg_r, spaced=spaced_norm)
            TA = T + n_halo
            nqb = len(qblocks)
            nq = qblocks[0]["nq"]
            for kc in range(2):
                bank, br = inproj_ct(kc, T, hT_g, hTg_r)
                CP("vector", qlat[:, kc, 0:T], bank[:, 0:T], reads=[br], writes=[qlat_r], wadd=(kc > 0))
                ACT(qsq[:, kc, 0:T], bank[:, 0:T], AF.Square, reads=[br], writes=[qsq_r], wadd=(kc > 0))
                yield
            bank, br = g_single()
            for kc in range(2):
                MM(bank[:, 0:T], ones_bf[:, :], qsq[:, kc, 0:T], kc == 0, kc == 1, reads=[qsq_r, c_res], writes=[br], wadd=(kc > 0))
            ACT(lnq[:, 0:T], bank[:, 0:T], AF.Ln, reads=[br], writes=[lnq_r], scale=1.0 / Q_LORA, bias=eps_t[:, 0:1])
            ACT(rstdq[:, 0:T], lnq[:, 0:T], AF.Exp, reads=[lnq_r], writes=[rstdq_r], scale=-0.5)
            for kc in range(2):
                STT(qnT[:, kc, 0:T], qlat[:, kc, 0:T], q_g_sb[:, kc:kc + 1], rstdq[:, 0:T], ALU.mult, ALU.mult,
                    reads=[qlat_r, rstdq_r, c_res], writes=[qnT_r], wadd=(kc > 0))
            yield
            for h in range(8):
                bank, br = g_single()
                for kc in range(2):
                    MM(bank[:, 0:T], W_abs_sb[:, kc, h, :], qnT[:, kc, 0:T], kc == 0, kc == 1,
                       reads=[qnT_r, c_res], writes=[br], wadd=(kc > 0))
                CP("scalar" if h % 2 == 0 else "vector", q_absT[:, h, 0:T], bank[:, 0:T], reads=[br], writes=[qabs_r], wadd=(h > 0))
                if h == 3:
                    yield
            KSUB = 99
            for qi, qb in enumerate(qblocks):
                qoff = qb["off"]
                for kc in range(2):
                    MM(m7[:nq, 0:256], qnT[:, kc, qoff:qoff + nq], w_uq_rope_sb[:, kc, :], kc == 0, kc == 1,
                       reads=[qnT_r, c_res], writes=[M_res], wadd=(kc > 0))
                q4 = lambda ap: ap.rearrange("p (h a b) -> p h a b", h=8, a=2)
                cb = qb["cos"].unsqueeze(1).unsqueeze(1).broadcast_to([nq, 8, 2, 16])
                sbb = qb["sin"].unsqueeze(1).unsqueeze(1).broadcast_to([nq, 8, 2, 16])
                TT("vector", q4(ropeA[:nq, :]), q4(m7[:nq, 0:256]), cb, ALU.mult, reads=[M_res, c_res], writes=[ropeA_r])
                TT("vector", q4(ropeB[:nq, :]), q4(m7[:nq, 0:256]), sbb, ALU.mult, reads=[M_res, c_res], writes=[ropeB_r])
                TT("vector", orep[:nq, :, 0, 0:16], q4(ropeA[:nq, :])[:, :, 0, :], q4(ropeB[:nq, :])[:, :, 1, :], ALU.subtract,
                   reads=[ropeA_r, ropeB_r], writes=[orep_r])
                TT("vector", orep[:nq, :, 0, 16:32], q4(ropeA[:nq, :])[:, :, 1, :], q4(ropeB[:nq, :])[:, :, 0, :], ALU.add,
                   reads=[ropeA_r, ropeB_r], writes=[orep_r], wadd=True)
                CP("vector", orep[:nq, :, 1:4, :], orep[:nq, :, 0:1, :].broadcast_to([nq, 8, 3, 32]),
                   reads=[orep_r], writes=[orep_r])
                tb_f, tr = g_single()
                tb = tb_f.bitcast(BF)
                for h in range(8):
                    TR(tb[:, h * 128:h * 128 + nq], orep[:nq, h, :, :].rearrange("p a b -> p (a b)"), nq,
                       reads=[orep_r, c_res], writes=[tr], wadd=(h > 0))
                CP("scalar", q_peT[qi][:, :, 0:nq], tb.rearrange("p (h q) -> p h q", h=8)[:, :, 0:nq],
                   reads=[tr], writes=[qpeT_res[qi]])
                yield
            yield
            for i in range(4):
                bank, br = inproj_ct(2 + i, T, hT_g, hTg_r)
                ACT(siluG[:, i, 0:T], bank[:, 0:T], AF.Silu, reads=[br], writes=[siluG_r], wadd=(i > 0))
                if i % 2 == 1:
                    yield
            yield
            for jc in range(4):
                if jc > 0:
                    yield
                s = nxt("conv")
                cc, cc_r = cc_sb[s], res("cc0")
                ue, ue_r = uext[s], res(f"ue{s}")
                cv, cv_r = cvt[s], res("cv0")
                sg, sg_r = sgc[s], res(f"sgc{s}")
                v3 = lambda ap: ap.rearrange("p (q t) -> p q t", q=nqb)
                b_cc, r_cc = inproj_ct(6 + 4 * jc, TA, hT_g, hTg_r)
                CP("scalar", cc[:, 0:TA], b_cc[:, 0:TA], reads=[r_cc], writes=[cc_r])
                b_cx, r_cx = inproj_ct(7 + 4 * jc, TA, hT_g, hTg_r)
                TT("vector", ue[:, 0:nqb, 2:2 + nq], v3(b_cx[:, 0:T]), v3(cc[:, 0:T]), ALU.mult,
                   reads=[r_cx, cc_r], writes=[ue_r])
                if n_halo:
                    TT("vector", ue[:, 0:nqb, 0:2], v3(b_cx[:, T:TA]), v3(cc[:, T:TA]), ALU.mult,
                       reads=[r_cx, cc_r], writes=[ue_r], wadd=True)
                else:
                    CP("gpsimd", ue[:, 0, 0:2], conv_state_src[:, jc, :], reads=[c_res], writes=[ue_r], wadd=True)
                b_gc, r_gc = inproj_ct(8 + 4 * jc, T, hT_g, hTg_r)
                ACT(sg[:, 0:T], b_gc[:, 0:T], AF.Silu, reads=[r_gc], writes=[sg_r])
                cv3 = cv[:, 0:T].rearrange("p (q t) -> p q t", q=nqb)
                TS("vector", cv3, ue[:, 0:nqb, 0:nq], conv_w_sb[:, jc, 0:1], ALU.mult, reads=[ue_r, c_res], writes=[cv_r])
                STT(cv3, ue[:, 0:nqb, 1:1 + nq], conv_w_sb[:, jc, 1:2], cv3, ALU.mult, ALU.add, reads=[ue_r, cv_r], writes=[cv_r])
                STT(cv3, ue[:, 0:nqb, 2:2 + nq], conv_w_sb[:, jc, 2:3], cv3, ALU.mult, ALU.add, reads=[ue_r, cv_r], writes=[cv_r])
                b_cb, r_cb = inproj_ct(9 + 4 * jc, T, hT_g, hTg_r)
                TT("vector", cv[:, 0:T], cv[:, 0:T], b_cb[:, 0:T], ALU.mult, reads=[cv_r, r_cb], writes=[cv_r])
                TT("vector", convres[:, jc, 0:T], cv[:, 0:T], sg[:, 0:T], ALU.mult, reads=[cv_r, sg_r], writes=[convres_r], wadd=(jc > 0))
                if conv_out is not None:
                    CP("gpsimd", convst[:, jc, :], ue[:, nqb - 1, nq:nq + 2], reads=[ue_r], writes=[convst_r], wadd=(jc > 0))
            if conv_out is not None:
                P.final.append(DMA("sync", conv_out, convst[:], key="stconv", reads=[convst_r]))


        def attention(qblocks, name):
            nq = qblocks[0]["nq"]
            slots_left = {"n": sum(len(q_["slots"]) + 1 for q_ in qblocks)}

            def post_A(qi):
                for ob, hs in enumerate([(0, 3), (3, 6), (6, 8)]):
                    nh = hs[1] - hs[0]
                    ov = o456[:nq, ob, 0:nh * 129].rearrange("p (h c) -> p h c", c=129)
                    I("vector", (lambda ov=ov, hs=hs: (lambda e: e.reciprocal(rden[:nq, hs[0]:hs[1]].unsqueeze(2), ov[:, :, 128:129])))(),
                      reads=[O_res[ob]], writes=[rden_r], wadd=(ob > 0))
                    TT("vector", olat[:nq, hs[0]:hs[1], :], ov[:, :, 0:128],
                       rden[:nq, hs[0]:hs[1]].unsqueeze(2).broadcast_to([nq, nh, 128]), ALU.mult,
                       reads=[O_res[ob], rden_r], writes=[olat_r], wadd=(ob > 0))

            def post_B(qi):
                qoff = qblocks[qi]["off"]
                tb_f, tr = g_single()
                tb = tb_f.bitcast(BF)
                for h in range(8):
                    TR(tb[:, h * 128:h * 128 + nq], olat[:nq, h, :], nq, reads=[olat_r, c_res], writes=[tr], wadd=(h > 0))
                CP("scalar", olatT[:, :, 0:nq], tb.rearrange("p (h q) -> p h q", h=8)[:, :, 0:nq], reads=[tr], writes=[olatT_r])
                ob_, or_ = g_single()
                for j in range(4):
                    MM(ob_[:, j * 128:j * 128 + nq], w_uv_pad[:, 2 * j, :], olatT[:, 2 * j, 0:nq], True, False,
                       reads=[olatT_r, c_res], writes=[or_], wadd=(j > 0))
                    MM(ob_[:, j * 128:j * 128 + nq], w_uv_pad[:, 2 * j + 1, :], olatT[:, 2 * j + 1, 0:nq], False, True,
                       reads=[olatT_r, c_res], writes=[or_], wadd=True)
                TT("vector", ogT[:, :, qoff:qoff + nq], ob_.rearrange("p (j q) -> p j q", j=4)[:, :, 0:nq],
                   siluG[:, :, qoff:qoff + nq], ALU.mult, reads=[or_, siluG_r], writes=[ogT_r], wadd=(qi > 0))

            pending_post = []
            for qi, qb in enumerate(qblocks):
                qoff = qb["off"]
                qp, qp_r = q_peT[qi], qpeT_res[qi]
                slots = qb["slots"]
                ns = len(slots)
                last_pv = None
                sl_state = {}

                def get_qvar(r):
                    key = (name, qi, r)
                    for i in range(2):
                        if qpm_state[i] is not None and qpm_state[i][0] == key:
                            return qpm[i], qpm_res[i]
                    i = qpmctr["i"] % 2
                    qpmctr["i"] += 1
                    if qpm_state[i] is not None and qpm_state[i][1] != r:
                        ro = qpm_state[i][1]
                        I("vector", (lambda i=i, ro=ro: (lambda e: e.memset(qpm[i][32 * ro:32 * ro + 32, :, :], 0.0)))(),
                          (), [qpm_res[i]])
                        CP("vector", qpm[i][32 * r:32 * r + 32, :, 0:nq], qp[32 * r:32 * r + 32, :, 0:nq],
                           reads=[qp_r], writes=[qpm_res[i]], wadd=True)
                    else:
                        CP("vector", qpm[i][32 * r:32 * r + 32, :, 0:nq], qp[32 * r:32 * r + 32, :, 0:nq],
                           reads=[qp_r, c_res], writes=[qpm_res[i]])
                    qpm_state[i] = (key, r)
                    return qpm[i], qpm_res[i]

                get_qvar(slots[0][0] // 16)

                def emit_S(si):
                    blk, nk, midx = slots[si]
                    r = blk // 16
                    kcol = blk * 128
                    kpcol = (blk % 16) * 128
                    kvr = kv_ready[blk]
                    pt_i = ptctr["i"] % 3
                    ptctr["i"] += 1
                    pt, pt_r = PT[pt_i], PT_res[pt_i]
                    qm, qm_r = get_qvar(r)
                    first_w = True
                    for hh in range(2):
                        sbank, brr = g_single()
                        S = sbank[:nk, 0:4 * nq].rearrange("p (h q) -> p h q", h=4)
                        MM(S, ckvT_all[:, kcol:kcol + nk], q_absT[:, 4 * hh:4 * hh + 4, qoff:qoff + nq], True, False,
                           reads=[kvr, qabs_r], writes=[brr])
                        MM(S, kpeT_all[:, kpcol:kpcol + nk], qm[:, 4 * hh:4 * hh + 4, 0:nq],
                           False, True, reads=[kvr, qm_r, c_res, kpcol_res[blk % 16]], writes=[brr], wadd=True)
                        if midx is None:
                            ACT(pt[:nk, 4 * hh:4 * hh + 4, 0:nq], S, AF.Exp, reads=[brr], writes=[pt_r], wadd=not first_w, scale=SM_SCALE)
                            first_w = False
                        else:
                            for hq in range(2):
                                col = qb["kidx"] * 8 + midx * 2 + hq
                                ACT(pt[:nk, 4 * hh:4 * hh + 4, 64 * hq:64 * hq + 64], S[:, :, 64 * hq:64 * hq + 64], AF.Exp,
                                    reads=[brr, c_res], writes=[pt_r], wadd=not first_w, scale=SM_SCALE,
                                    bias=mbias_sb[:nk, col:col + 1])
                                first_w = False
                    sl_state[si] = (pt, pt_r, kvr)

                def emit_PV(si):
                    blk, nk, midx = slots[si]
                    pt, pt_r, kvr = sl_state.pop(si)
                    lp = None
                    for h in range(8):
                        ob, oc = h // 3, h % 3
                        first = (si == 0 and oc == 0)
                        lp = MM(o456[:nq, ob, oc * 129:oc * 129 + 129], pt[:nk, h, 0:nq], V_all[:nk, blk, 0:129],
                                first, si == ns - 1, reads=[pt_r, kvr], writes=[O_res[ob]],
                                wadd=not first, sgcheck=True)
                    return lp

                for si in range(ns + 1):
                    if si + 2 < ns:
                        get_qvar(slots[si + 2][0] // 16)
                    if si < ns:
                        emit_S(si)
                    if si >= 1:
                        last_pv = emit_PV(si - 1)
                    if si == 2 and pending_post:
                        post_B(pending_post.pop(0))
                    if BG:
                        sleft = max(slots_left["n"], 1)
                        bg_step(-(-bgc["left"] // sleft))
                    slots_left["n"] -= 1
                attn_last[name] = last_pv
                while pending_post:
                    post_B(pending_post.pop(0))
                post_A(qi)
                pending_post.append(qi)
            while pending_post:
                post_B(pending_post.pop(0))


        def part3_gen(T, qblocks, hT_g, hTg_r):
            nq = qblocks[0]["nq"]
            for oc in range(8):
                if oc > 0:
                    yield
                s = nxt("merge")
                b_ma, r_ma = inproj_ct(22 + 2 * oc, T, hT_g, hTg_r)
                ACT(sga[s][:, 0:T], b_ma[:, 0:T], AF.Sigmoid, reads=[r_ma], writes=[res("sga0")])
                b_mb, r_mb = inproj_ct(23 + 2 * oc, T, hT_g, hTg_r)
                ACT(sgb[s][:, 0:T], b_mb[:, 0:T], AF.Sigmoid, reads=[r_mb], writes=[res("sgb0")])
                b_a, r_a = g_single()
                for kc in range(4):
                    MM(b_a[:, 0:T], w_o_mla_sb[:, kc, oc * 128:(oc + 1) * 128], ogT[:, kc, 0:T], kc == 0, kc == 3,
                       reads=[ogT_r, cw2_res], writes=[r_a], wadd=(kc > 0))
                TT("vector", t1[s][:, 0:T], b_a[:, 0:T], sga[s][:, 0:T], ALU.mult, reads=[r_a, res("sga0")], writes=[res("t1_0")])
                b_b, r_b = g_single()
                for kc in range(4):
                    MM(b_b[:, 0:T], w_o_conv_sb[:, kc, oc * 128:(oc + 1) * 128], convres[:, kc, 0:T], kc == 0, kc == 3,
                       reads=[convres_r, cw2_res], writes=[r_b], wadd=(kc > 0))
                TT("vector", t2[s][:, 0:T], b_b[:, 0:T], sgb[s][:, 0:T], ALU.mult, reads=[r_b, res("sgb0")], writes=[res("t2_0")])
                TT("gpsimd", mergedT[:, oc, 0:T], t1[s][:, 0:T], t2[s][:, 0:T], ALU.add,
                   reads=[res("t1_0"), res("t2_0")], writes=[merged_r], wadd=(oc > 0))
            for qb in qblocks:
                yield
                qoff = qb["off"]
                DMA("sync", xtok[:nq, :], qb["x_rows"], key="xtok", writes=[xtok_r])
                pr_t, (pr0, pr1) = g_pair()
                for half in range(2):
                    brr = pr0 if half == 0 else pr1
                    for k in range(8):
                        MM(pr_t[:nq, half, :], mergedT[:, k, qoff:qoff + nq], w_out_sb[:, k, half * 512:(half + 1) * 512],
                           k == 0, k == 7, reads=[merged_r, cw2_res], writes=[brr], wadd=(k > 0))
                z = pr_t[:nq, :, :].rearrange("p a b -> p (a b)")
                ACT(PT[0][:nq, :, :].rearrange("p a b -> p (a b)"), z, AF.Square, reads=[pr0, pr1],
                    writes=[PT_res[0], small_r], accum=small[:nq, 4:5])
                ACT(small[:nq, 5:6], small[:nq, 4:5], AF.Ln, reads=[small_r], writes=[small_r], scale=1.0 / D_MODEL, bias=eps_t[:nq, 0:1])
                ACT(small[:nq, 6:7], small[:nq, 5:6], AF.Exp, reads=[small_r], writes=[small_r], scale=-0.5)
                STT(yt[:nq, :], z, small[:nq, 6:7], post_g_bc[:nq, :], ALU.mult, ALU.mult,
                    reads=[pr0, pr1, small_r, c_res], writes=[yt_r])
                TT("gpsimd", yt[:nq, :], yt[:nq, :], xtok[:nq, :], ALU.add, reads=[yt_r, xtok_r], writes=[yt_r])
                P.final.append(DMA("sync", qb["y_rows"], yt[:nq, :], key="sty", reads=[yt_r]))


        STAGE = 99
        xT_seq_v = d_xT_seq.rearrange("(k p) t -> p k t", p=128)
        xT_own_v = d_xT_own.rearrange("(k p) t -> p k t", p=128)
        smp_done = []

        def prefix_tile(tl, kbsel=None, spaced=False):
            s = tl % 2
            if kbsel is None:
                kbsel = 3 if s == 0 else 7
            ck, sk = ck_ring[s], sk_ring[s]
            ck_r, sk_r = res(f"ck{s}"), res(f"sk{s}")
            DMA("sync", ck[:], d_cos_k[:, 2 * tl:2 * tl + 2, :], key=f"ck{s}", writes=[ck_r])
            DMA("sync", sk[:], d_sin_k[:, 2 * tl:2 * tl + 2, :], key=f"sk{s}", writes=[sk_r])
            yield from norm_tile(xT_seq_v[:, :, tl * TP:(tl + 1) * TP], TP, hT_p[s], res(f"hTp{s}"), kbsel=kbsel, spaced=spaced)
            yield from key_tile(hT_p[s], res(f"hTp{s}"), [(0, 128, 2 * tl), (128, 128, 2 * tl + 1)],
                                lambda blk, n: ck[:n, blk - 2 * tl, :], lambda blk, n: sk[:n, blk - 2 * tl, :],
                                o_ckv[tl * TP:(tl + 1) * TP, :].rearrange("(b p) c -> p b c", p=128),
                                o_kpe[tl * TP:(tl + 1) * TP, :].rearrange("(b p) c -> p b c", p=128),
                                kbsel=kbsel, spaced=spaced, tab_res=[ck_r, sk_r])
        CH_TILE = 8

        def run_rr(gens):
            live = list(gens)
            while live:
                for g_ in list(live):
                    try:
                        next(g_)
                    except StopIteration:
                        live.remove(g_)

        NUP = 4
        run_rr([prefix_tile(0), prefix_tile(1)])
        run_rr([prefix_tile(2), prefix_tile(3)])
        emit_wprep()
        emit_initw2()
        CH_TILE = 52
        for _ in range(WR):
            w_issue()
        def pair_qbs(g):
            qbs = []
            for a in range(2):
                k = 2 * g + a
                nslot = 8 * g + 4 + 4 * a
                slots = [(b, 128, (b - (nslot - 4)) if b >= nslot - 4 else None) for b in range(nslot)]
                qbs.append(dict(off=128 * a, nq=128, slots=slots, cos=cos_q[:, k, :], sin=sin_q[:, k, :],
                                x_rows=d_x_own[k * 128:(k + 1) * 128, :], y_rows=o_y[k * 128:(k + 1) * 128, :],
                                kidx=k))
            return qbs

        def pair_part1(g, spaced_norm=False):
            return part1_gen(256, 4, pair_qbs(g), None, o_convp if g == 7 else None, f"p{g}",
                             hT_gs[g % 2], hTg_rs[g % 2], xT_own_v[:, :, g * TGP:(g + 1) * TGP], spaced_norm=spaced_norm)

        run(pair_part1(0))
        for g in range(8):
            for tl in range(4 * g + NUP, min(4 * g + NUP + 4, 32)):
                bg_add(prefix_tile(tl, spaced=True), CH_TILE)
            attention(pair_qbs(g), f"p{g}")
            bg_drain()
            p3 = part3_gen(256, pair_qbs(g), hT_gs[g % 2], hTg_rs[g % 2])
            if g < 7:
                run_rr([pair_part1(g + 1, spaced_norm=3), p3])
            else:
                run(p3)
        p_done = [attn_last["p7"]]
        emit_cache(p_done)
        for b in range(SB0, SB0 + 8):
            kv_ready[b] = cw3_res
        smp_slots = [(b, 128, None) for b in range(SB0, SB0 + 8)] + [(SB0 + 8, DEC_SEQ, None)]
        smp_qbs = [dict(off=0, nq=DEC_SEQ, slots=smp_slots, cos=cos_s[:, :], sin=sin_s[:, :],
                        x_rows=d_x_smp, y_rows=o_ys, kidx=0)]
        g1 = part1_gen(DEC_SEQ, 0, smp_qbs, stateT, o_convs, "smp", hT_gs[0], hTg_rs[0],
                       d_xT_smp.rearrange("(k p) t -> p k t", p=128))
        next(g1)
        run(key_tile(hT_gs[0], hTg_rs[0], [(0, DEC_SEQ, SB0 + 8)],
                     lambda blk, n: cos_s[:n, :], lambda blk, n: sin_s[:n, :], o_ckv_s, o_kpe_s, extra_deps=p_done))
        run(g1)
        attention(smp_qbs, "smp")
        run(part3_gen(DEC_SEQ, smp_qbs, hT_gs[0], hTg_rs[0]))
        if record:
            return wseq
        assert STAGE < 99 or wq["used"] == wq["total"], (wq["used"], wq["total"])
        P.emit(nc, es)
    return nc


def _own_blocks(j):
    out = []
    for g in range(8):
        out += [8 * g + j, 8 * g + 7 - j]
    return out


def _rope_tables(pos):
    half = QK_ROPE // 2
    inv = (np.float32(10000.0) ** (-np.arange(half, dtype=np.float32) / np.float32(half))).astype(np.float32)
    ang = pos.astype(np.float32)[:, None] * inv[None, :]
    return np.cos(ang).astype(np.float32), np.sin(ang).astype(np.float32)


_NC_CACHE = {}


def kernel(x_prompt, x_sample, cache_kv_latent, cache_k_rope, state_conv, pre_norm, w_in, q_norm, w_uq, kv_norm,
           w_uk, w_uv, w_o_mla, conv_w, w_o_conv, w_out, post_norm):
    f = lambda a: np.ascontiguousarray(np.asarray(a, dtype=np.float32))
    x_prompt, x_sample = f(x_prompt), f(x_sample)
    W = f(w_in)[0]
    q0, kv0, kr0, gm0, cb0, cc0, cx0, gc0, mm0, mc0 = 0, 256, 384, 416, 928, 1440, 1952, 2464, 2976, 4000
    starts = [q0, q0 + 128] + [gm0 + 128 * i for i in range(4)]
    for jc in range(4):
        starts += [cc0 + 128 * jc, cx0 + 128 * jc, gc0 + 128 * jc, cb0 + 128 * jc]
    for oc in range(8):
        starts += [mm0 + 128 * oc, mc0 + 128 * oc]
    assert len(starts) == NCT
    cols = np.concatenate([np.arange(s, s + 128) for s in starts])
    w_in_t = f(W[:, cols].reshape(8, 128, NCT, 128).transpose(2, 1, 0, 3).reshape(NCT, 128, 1024))
    w_kv = f(W[:, kv0:kv0 + 160].reshape(8, 128, 160).transpose(1, 0, 2))
    wuq = f(w_uq)[0].reshape(Q_LORA, N_HEADS, QK_NOPE + QK_ROPE)
    w_uqT_nope = f(wuq[:, :, :QK_NOPE].transpose(2, 1, 0))
    w_ukT = f(f(w_uk)[0].transpose(2, 1, 0))
    w_uq_rope = f(wuq[:, :, QK_NOPE:].reshape(2, 128, N_HEADS * QK_ROPE).transpose(1, 0, 2))
    common = {
        "w_in_t": w_in_t, "w_kv": w_kv, "w_uqT_nope": w_uqT_nope, "w_ukT": w_ukT, "w_uq_rope": w_uq_rope,
        "w_uv": f(f(w_uv)[0]), "w_o_mla": f(f(w_o_mla)[0]), "w_o_conv": f(f(w_o_conv)[0]), "w_out": f(f(w_out)[0]),
        "pre_g": f(f(pre_norm)[0].reshape(8, 128).T), "q_g": f(f(q_norm)[0].reshape(2, 128).T),
        "kv_g": f(f(kv_norm)[0].reshape(1, 128)), "post_g": f(f(post_norm)[0].reshape(1, 1024)),
        "conv_w_t": f(f(conv_w)[0].reshape(3, 4, 128).transpose(2, 1, 0)),
        "ident": np.eye(128, dtype=np.float32),
    }
    ck, sk = _rope_tables(np.arange(SEQ))
    common["cos_k"] = f(ck.reshape(64, 128, 16).transpose(1, 0, 2))
    common["sin_k"] = f(sk.reshape(64, 128, 16).transpose(1, 0, 2))
    cs, ss = _rope_tables(PAST + np.arange(DEC_SEQ))
    common["cos_s"], common["sin_s"] = f(cs), f(ss)
    ckv_c, kpe_c, st_c = f(cache_kv_latent)[0], f(cache_k_rope)[0], f(state_conv)[0]

    in_maps = []
    own = []
    for c in range(8):
        b, j = c // 4, c % 4
        blks = _own_blocks(j)
        own.append(blks)
        xb = x_prompt[b]
        xT = f(xb.T)
        parts = []
        for g in range(8):
            A, B = blks[2 * g], blks[2 * g + 1]
            tok = list(range(128 * A, 128 * A + 128)) + list(range(128 * B, 128 * B + 128))
            main = xT[:, tok]
            halo = np.zeros((D_MODEL, 4), np.float32)
            for a, blk in enumerate((A, B)):
                if blk > 0:
                    halo[:, 2 * a:2 * a + 2] = xT[:, 128 * blk - 2:128 * blk]
            parts += [main, halo]
        xT_own = f(np.concatenate(parts, axis=1))
        x_own = f(np.concatenate([xb[128 * k:128 * k + 128] for k in blks], axis=0))
        pos_q = np.concatenate([np.arange(128 * k, 128 * k + 128) for k in blks])
        cq, sq = _rope_tables(pos_q)
        mbias = np.zeros((128, 128), np.float32)
        pidx = np.arange(128)
        for k, blk in enumerate(blks):
            g, a = k // 2, k % 2
            nslot = 8 * g + 4 + 4 * a
            for m in range(4):
                kb = nslot - 4 + m
                for hq in range(2):
                    vis = (2 * kb + pidx // 64) <= (2 * blk + hq)
                    mbias[:, k * 8 + m * 2 + hq] = np.where(vis, 0.0, -30000.0)
        d = dict(common)
        d.update({
            "xT_seq": xT, "xT_own": xT_own, "x_own": x_own,
            "xT_smp": f(x_sample[c].T), "x_smp": f(x_sample[c]),
            "cache_kvT": f(ckv_c[c].T), "cache_kv": f(ckv_c[c]), "cache_kpeT": f(kpe_c[c].T),
            "stateT": f(st_c[c].reshape(2, 4, 128).transpose(2, 1, 0)),
            "mbias": f(mbias),
            "cos_q": f(cq.reshape(16, 128, 16).transpose(1, 0, 2)), "sin_q": f(sq.reshape(16, 128, 16).transpose(1, 0, 2)),
        })
        in_maps.append(d)

    if "nc" not in _NC_CACHE:
        _NC_CACHE["nc"] = build_program(build_program(None))
    nc = _NC_CACHE["nc"]
    res = run_bass_kernel_spmd(nc, in_maps, core_ids=list(range(8)))
    R = res.results

    y_prompt = np.zeros((2, SEQ, D_MODEL), np.float32)
    y_sample = np.zeros((8, DEC_SEQ, D_MODEL), np.float32)
    ckv_p = np.zeros((1, 2, SEQ, 128), np.float32)
    kpe_p = np.zeros((1, 2, SEQ, 32), np.float32)
    cv_p = np.zeros((1, 2, 2, CONV_W), np.float32)
    ckv_s = np.zeros((1, 8, DEC_SEQ, 128), np.float32)
    kpe_s = np.zeros((1, 8, DEC_SEQ, 32), np.float32)
    cv_s = np.zeros((1, 8, 2, CONV_W), np.float32)
    for c in range(8):
        b, j = c // 4, c % 4
        r = R[c]
        oy = np.asarray(r["o_y"])
        for k, blk in enumerate(own[c]):
            y_prompt[b, 128 * blk:128 * blk + 128] = oy[128 * k:128 * k + 128]
        y_sample[c] = np.asarray(r["o_ys"])
        ckv_s[0, c] = np.asarray(r["o_ckv_s"])
        kpe_s[0, c] = np.asarray(r["o_kpe_s"])
        cv_s[0, c] = np.asarray(r["o_convs"]).transpose(2, 1, 0).reshape(2, CONV_W)
        if j == 0:
            ckv_p[0, b] = np.asarray(r["o_ckv"])
            kpe_p[0, b] = np.asarray(r["o_kpe"])
            cv_p[0, b] = np.asarray(r["o_convp"]).transpose(2, 1, 0).reshape(2, CONV_W)
    return (y_prompt, y_sample, ckv_p, kpe_p, cv_p, ckv_s, kpe_s, cv_s)
```

```python
import numpy as np
from contextlib import ExitStack
import concourse.bass as bass
import concourse.mybir as mybir
from concourse.bass_utils import run_bass_kernel_spmd

F32 = mybir.dt.float32
BF = mybir.dt.bfloat16
AF = mybir.ActivationFunctionType
ALU = mybir.AluOpType

D_MODEL = 1024
SEQ = 8192
N_HEADS = 8
QK_NOPE = 64
QK_ROPE = 32
KV_LORA = 128
Q_LORA = 256
CONV_W = 512
PAST = 1024
DEC_SEQ = 16
EPS = 1e-6
SM_SCALE = float((QK_NOPE + QK_ROPE) ** -0.5)
SB0 = 0
NCT = 38
TP = 256
TGP = 260
ENGS = ["sync", "tensor", "scalar", "vector", "gpsimd"]


class Op:
    __slots__ = ("eng", "fn", "deps", "signal", "val", "key", "is_dma", "group")

    def __init__(self, eng, fn, deps):
        self.eng, self.fn, self.deps = eng, fn, deps
        self.signal = False
        self.val = 0
        self.key = None
        self.is_dma = False
        self.group = False


class Res:
    __slots__ = ("writers", "readers", "excl")

    def __init__(self, excl=False):
        self.writers = []
        self.readers = []
        self.excl = excl


class Prog:
    def __init__(self):
        self.streams = {e: [] for e in ENGS}
        self.dma_cnt = {}
        self.final = []

    def _flat(self, deps):
        out = []
        for d in deps:
            if d is None:
                continue
            if isinstance(d, (list, tuple)):
                out.extend(self._flat(d))
            else:
                out.append(d)
        return out

    def op(self, eng, fn, deps=()):
        o = Op(eng, fn, self._flat(deps))
        for d in o.deps:
            d.signal = True
        self.streams[eng].append(o)
        return o

    def dma(self, eng, fn, key, deps=(), group=False):
        o = self.op(eng, fn, deps)
        o.is_dma = True
        o.key = key
        o.group = group
        n = self.dma_cnt.get(key, 0) + 1
        self.dma_cnt[key] = n
        o.val = 16 * n
        o.signal = True
        return o

    def I(self, eng, fn, reads=(), writes=(), deps=(), wadd=False, dma_key=None, group=False):
        d = list(deps)
        for r in reads:
            d += r.writers
            if r.excl:
                d += [x for x in r.readers if x.eng != eng]
        for w in writes:
            d += w.readers
            if not wadd:
                d += w.writers
        if dma_key is None:
            o = self.op(eng, fn, d)
        else:
            o = self.dma(eng, fn, dma_key, d, group)
        keep = (lambda lst: [x for x in lst if x.is_dma or x.eng != eng]) if dma_key is None else (lambda lst: list(lst))
        for r in reads:
            r.readers = keep(r.readers) + [o]
        for w in writes:
            if wadd:
                w.writers = keep(w.writers) + [o]
            else:
                w.writers = [o]
                w.readers = []
        return o

    def emit(self, nc, es):
        for eng in ENGS:
            cnt = 0
            for o in self.streams[eng]:
                if o.is_dma:
                    if o.group:
                        o.val = 16 * self.dma_cnt[o.key]
                elif o.signal:
                    cnt += 1
                    o.val = cnt
        sems = {}
        for eng in ENGS[1:]:
            sems[eng] = es.enter_context(nc.semaphore("s_" + eng))
        for key in self.dma_cnt:
            sems["d_" + key] = es.enter_context(nc.semaphore("d_" + key))
        block = es.enter_context(nc.Block())
        final = self.final

        def make(eng):
            ops = self.streams[eng]

            def body(e):
                waited = {}

                def wait_for(d):
                    name = ("d_" + d.key) if d.is_dma else d.eng
                    if waited.get(name, 0) >= d.val:
                        return
                    waited[name] = d.val
                    e.wait_ge(sems[name], d.val)

                for o in ops:
                    for d in o.deps:
                        wait_for(d)
                    ins = o.fn(e)
                    if o.is_dma:
                        ins.then_inc(sems["d_" + o.key], 16)
                    elif o.signal:
                        ins.then_inc(sems[eng], 1)
                if eng == "sync":
                    for d in final:
                        wait_for(d)
            return body

        block.sync(make("sync"))
        block.tensor(make("tensor"))
        block.scalar(make("scalar"))
        block.vector(make("vector"))
        block.gpsimd(make("gpsimd"))


def build_program(wseq_in=None):
    record = wseq_in is None
    nc = bass.Bass("TRN2", target_bir_lowering=False)
    P = Prog()
    I = P.I

    def din(name, shape, dt=F32):
        return nc.dram_tensor(name, list(shape), dt, kind="ExternalInput").ap()

    def dout(name, shape):
        return nc.dram_tensor(name, list(shape), F32, kind="ExternalOutput").ap()

    d_xT_seq = din("xT_seq", [D_MODEL, SEQ])
    d_xT_own = din("xT_own", [D_MODEL, 8 * TGP])
    d_x_own = din("x_own", [2048, D_MODEL])
    d_xT_smp = din("xT_smp", [D_MODEL, DEC_SEQ])
    d_x_smp = din("x_smp", [DEC_SEQ, D_MODEL])
    d_cache_kvT = din("cache_kvT", [128, PAST])
    d_cache_kv = din("cache_kv", [PAST, 128])
    d_cache_kpeT = din("cache_kpeT", [32, PAST])
    d_stateT = din("stateT", [128, 4, 2])
    d_mbias = din("mbias", [128, 128])
    d_w_in_t = din("w_in_t", [NCT, 128, 1024])
    d_w_kv = din("w_kv", [128, 8, 160])
    d_w_uqT = din("w_uqT_nope", [64, 8, 256])
    d_w_ukT = din("w_ukT", [64, 8, 128])
    d_w_uq_rope = din("w_uq_rope", [128, 2, 256])
    d_w_uv = din("w_uv", [128, 8, 64])
    d_w_o_mla = din("w_o_mla", [512, 1024])
    d_w_o_conv = din("w_o_conv", [512, 1024])
    d_w_out = din("w_out", [1024, 1024])
    d_pre_g = din("pre_g", [128, 8])
    d_q_g = din("q_g", [128, 2])
    d_kv_g = din("kv_g", [1, 128])
    d_post_g = din("post_g", [1, 1024])
    d_conv_w = din("conv_w_t", [128, 4, 3])
    d_cos_k = din("cos_k", [128, 64, 16])
    d_sin_k = din("sin_k", [128, 64, 16])
    d_cos_q = din("cos_q", [128, 16, 16])
    d_sin_q = din("sin_q", [128, 16, 16])
    d_cos_s = din("cos_s", [16, 16])
    d_sin_s = din("sin_s", [16, 16])
    d_ident = din("ident", [128, 128])

    d_scr = nc.dram_tensor("w_scr", [NCT, 128, 1024], BF, kind="Internal").ap()

    o_y = dout("o_y", [2048, D_MODEL])
    o_ys = dout("o_ys", [DEC_SEQ, D_MODEL])
    o_ckv = dout("o_ckv", [SEQ, 128])
    o_kpe = dout("o_kpe", [SEQ, 32])
    o_convp = dout("o_convp", [128, 4, 2])
    o_ckv_s = dout("o_ckv_s", [DEC_SEQ, 128])
    o_kpe_s = dout("o_kpe_s", [DEC_SEQ, 32])
    o_convs = dout("o_convs", [128, 4, 2])

    with ExitStack() as es:
        def sb(name, shape, dt):
            return es.enter_context(nc.sbuf_tensor(name, list(shape), dt))

        ckvT_all = sb("ckvT_all", [128, SEQ], BF)
        kpeT_all = sb("kpeT_all", [128, 2048], BF)
        V_all = sb("V_all", [128, 64, 130], BF)
        w_out_sb = sb("w_out_sb", [128, 8, 1024], BF)
        w_o_mla_sb = sb("w_o_mla_sb", [128, 4, 1024], BF)
        w_o_conv_sb = sb("w_o_conv_sb", [128, 4, 1024], BF)
        W_abs_sb = sb("W_abs_sb", [128, 2, 8, 128], BF)
        w_uq_rope_sb = sb("w_uq_rope_sb", [128, 2, 256], BF)
        w_uv_pad = sb("w_uv_pad", [128, 8, 128], BF)
        w_kv_sb = sb("w_kv_sb", [128, 8, 160], BF)
        scrA = sb("scrA", [128, 2048], F32)
        scrB = sb("scrB", [128, 1024], F32)
        w_uqT_sb = scrA[0:64, :].rearrange("p (h m) -> p h m", h=8)
        w_ukT_sb = scrB[0:64, :].rearrange("p (h c) -> p h c", h=8)
        post_g_bc = sb("post_g_bc", [128, 1024], F32)
        kv_g_bc = sb("kv_g_bc", [128, 128], F32)
        ident_bf = sb("ident_bf", [128, 128], BF)
        ones_bf = sb("ones_bf", [128, 128], BF)
        eps_t = sb("eps_t", [128, 1], F32)
        pre_g_sb = sb("pre_g_sb", [128, 8], F32)
        q_g_sb = sb("q_g_sb", [128, 2], F32)
        conv_w_sb = sb("conv_w_sb", [128, 4, 3], F32)
        ck_ring = [sb(f"ck{i}", [128, 2, 16], F32) for i in range(2)]
        sk_ring = [sb(f"sk{i}", [128, 2, 16], F32) for i in range(2)]
        cos_q = sb("cos_q_sb", [128, 16, 16], F32)
        sin_q = sb("sin_q_sb", [128, 16, 16], F32)
        cos_s = sb("cos_s_sb", [16, 16], F32)
        sin_s = sb("sin_s_sb", [16, 16], F32)
        stateT = sb("stateT_sb", [128, 4, 2], F32)
        convst = sb("convst", [128, 4, 2], F32)

        xT_ring = [sb(f"xT{i}", [128, 8, TGP], F32) for i in range(2)]
        xsq_b = [sb(f"xsq{i}", [128, 8, TGP], BF) for i in range(2)]
        lnt_b = [sb(f"lnt{i}", [128, TGP], F32) for i in range(2)]
        rstd_b = [sb(f"rstd{i}", [128, TGP], F32) for i in range(2)]
        hT_p = [sb(f"hTp{i}", [128, 8, TP], BF) for i in range(2)]
        hT_gs = [sb(f"hTg{i}", [128, 8, TGP], BF) for i in range(2)]
        WR = 6
        w_ring = [sb(f"wr{i}", [128, 1024], BF) for i in range(WR)]
        stg_ckv = [sb(f"stgc{i}", [128, 2, 128], F32) for i in range(2)]
        stg_kpe = [sb(f"stgk{i}", [128, 2, 32], F32) for i in range(2)]
        kpe_bf = [sb(f"kpebf{i}", [128, 128], BF) for i in range(2)]
        junkk_b = [sb(f"junkk{i}", [128, 128], BF) for i in range(2)]
        smallk_b = [sb(f"smallk{i}", [128, 8], F32) for i in range(2)]
        kropeA_b = [sb(f"kropeA{i}", [128, 32], F32) for i in range(2)]
        kropeB_b = [sb(f"kropeB{i}", [128, 32], F32) for i in range(2)]
        small = sb("small", [128, 16], F32)
        ropeA = scrB[:, 0:256]
        ropeB = scrB[:, 256:512]
        qlat = scrB[:, 512:1024].rearrange("p (k t) -> p k t", k=2)
        qsq = sb("qsq", [128, 2, 256], BF)
        lnq = sb("lnq", [128, 256], F32)
        rstdq = sb("rstdq", [128, 256], F32)
        qnT = sb("qnT", [128, 2, 256], BF)
        q_absT = sb("q_absT", [128, 8, 256], BF)
        siluG = sb("siluG", [128, 4, 256], BF)
        convres = sb("convres", [128, 4, 256], BF)
        ogT = sb("ogT", [128, 4, 256], BF)
        mergedT = sb("mergedT", [128, 8, 256], BF)
        cc_sb = [sb(f"ccsb{i}", [128, TGP], F32) for i in range(1)] * 2
        uext = [sb(f"uext{i}", [128, 2, 130], F32) for i in range(2)]
        cvt = [sb(f"cvt{i}", [128, 256], F32) for i in range(1)] * 2
        sgc = [sb(f"sgc{i}", [128, 256], F32) for i in range(2)]
        sga = [sb(f"sga{i}", [128, 256], F32) for i in range(1)] * 2
        sgb = [sb(f"sgb{i}", [128, 256], F32) for i in range(1)] * 2
        t1 = [sb(f"t1_{i}", [128, 256], F32) for i in range(1)] * 2
        t2 = [sb(f"t2_{i}", [128, 256], F32) for i in range(1)] * 2
        orep = sb("orep", [128, 8, 4, 32], BF)
        q_peT = [sb(f"qpeT{i}", [128, 8, 128], BF) for i in range(2)]
        mbias_sb = sb("mbias_sb", [128, 128], F32)
        qpm = [sb(f"qpm{i}", [128, 8, 128], BF) for i in range(2)]
        maskr = sb("maskr", [128, 4], F32)
        PT = [sb(f"PT{i}", [128, 8, 128], BF) for i in range(3)]
        rden = sb("rden", [128, 8], F32)
        olat = sb("olat", [128, 8, 128], BF)
        olatT = sb("olatT", [128, 8, 128], BF)
        xtok = scrA[:, 0:1024]
        yt = scrA[:, 1024:2048]

        g01 = es.enter_context(nc.psum_tensor("g01", [128, 2, 512], F32))
        g23 = es.enter_context(nc.psum_tensor("g23", [128, 2, 512], F32))
        o456 = es.enter_context(nc.psum_tensor("o456", [128, 3, 512], F32))
        m7 = es.enter_context(nc.psum_tensor("m7", [128, 512], F32))
        gbank_t = [g01, g01, g23, g23]
        G_res = [Res(True) for _ in range(4)]
        O_res = [Res(True) for _ in range(3)]
        M_res = Res(True)
        gctr = {"s": 0, "p": 0}

        def g_single():
            b = gctr["s"] % 3
            gctr["s"] += 1
            assert not (G_res[b].writers and not G_res[b].readers), "PSUM bank handed out while still live"
            return gbank_t[b][:, b % 2, :], G_res[b]

        def g_pair():
            p = gctr["p"] % 2
            gctr["p"] += 1
            return (g01, (G_res[0], G_res[1])) if p == 0 else (g23, (G_res[2], G_res[3]))

        def k_bank(sel=3):
            if sel == 7:
                return m7[:, :], M_res
            return g23[:, sel - 2, :], G_res[sel]

        R = {}

        def res(name):
            if name not in R:
                R[name] = Res()
            return R[name]

        def MM(out, lhsT, rhs, start, stop, reads=(), writes=(), wadd=False, tp=None, sgcheck=False, deps=()):
            def fn(e):
                kw = {}
                if tp is not None:
                    kw["tile_position"] = tp
                if sgcheck:
                    kw["skip_group_check"] = True
                return e.matmul(out, lhsT=lhsT, rhs=rhs, start=start, stop=stop, **kw)
            return I("tensor", fn, reads, writes, deps, wadd)

        def TR(out, in_, n, reads=(), writes=(), wadd=False, deps=()):
            return I("tensor", lambda e: e.transpose(out, in_, ident_bf[:n, :n]), reads, writes, deps, wadd)

        def ACT(out, in_, func, reads=(), writes=(), wadd=False, scale=None, bias=None, accum=None, deps=()):
            def fn(e):
                kw = {}
                if scale is not None:
                    kw["scale"] = scale
                if bias is not None:
                    kw["bias"] = bias
                if accum is not None:
                    kw["accum_out"] = accum
                return e.activation(out=out, in_=in_, func=func, **kw)
            return I("scalar", fn, reads, writes, deps, wadd)

        def TT(eng, out, in0, in1, op, reads=(), writes=(), wadd=False, deps=()):
            return I(eng, lambda e: e.tensor_tensor(out=out, in0=in0, in1=in1, op=op), reads, writes, deps, wadd)

        def STT(out, in0, scalar, in1, op0, op1, reads=(), writes=(), wadd=False, deps=()):
            return I("vector", lambda e: e.scalar_tensor_tensor(out=out, in0=in0, scalar=scalar, in1=in1, op0=op0, op1=op1),
                     reads, writes, deps, wadd)

        def TS(eng, out, in0, s1, op0, reads=(), writes=(), wadd=False, deps=()):
            return I(eng, lambda e: e.tensor_scalar(out=out, in0=in0, scalar1=s1, scalar2=None, op0=op0),
                     reads, writes, deps, wadd)

        def CP(eng, out, in_, reads=(), writes=(), wadd=False, deps=()):
            if eng == "scalar":
                return ACT(out, in_, AF.Copy, reads, writes, wadd, deps=deps)
            return I(eng, lambda e: e.tensor_copy(out=out, in_=in_), reads, writes, deps, wadd)

        def DMA(eng, out, in_, key, reads=(), writes=(), deps=(), group=False, wadd=False):
            return I(eng, lambda e: e.dma_start(out=out, in_=in_), reads, writes, deps, wadd, dma_key=key, group=group)

        def MEMSET(out, val, writes=(), wadd=False, deps=()):
            return I("gpsimd", lambda e: e.memset(out, val), (), writes, deps, wadd)

        c_res = res("consts")
        MEMSET(eps_t[:], EPS, [c_res])
        MEMSET(ones_bf[:], 1.0, [c_res], wadd=True)
        MEMSET(V_all[:, :, 128:130], 1.0, [c_res], wadd=True)
        ms_uv = MEMSET(w_uv_pad[:], 0.0, [c_res], wadd=True)
        ms_kpe = MEMSET(kpeT_all[:], 0.0, [c_res], wadd=True)
        for i in range(2):
            MEMSET(qpm[i][:], 0.0, [c_res], wadd=True)
        ms_mr = MEMSET(maskr[:], 0.0, [c_res], wadd=True)
        for r_ in range(4):
            MEMSET(maskr[32 * r_:32 * r_ + 32, r_:r_ + 1], 1.0, [c_res], wadd=True, deps=[ms_mr])
        for i in range(2):
            MEMSET(kpe_bf[i][:], 0.0, [c_res], wadd=True)

        gq = "gpsimd"
        iw = dict(key="initw", group=True, writes=[c_res], wadd=True)
        DMA(gq, ident_bf[:], d_ident, **iw)
        DMA(gq, w_kv_sb[:], d_w_kv, **iw)
        DMA(gq, w_uq_rope_sb[:], d_w_uq_rope, **iw)
        uvp = w_uv_pad[:].rearrange("p (j two) c -> p j two c", two=2)
        uvd = d_w_uv.rearrange("p (j two) c -> p j two c", two=2)
        DMA(gq, uvp[:, :, 0, 0:64], uvd[:, :, 0, :], deps=[ms_uv], **iw)
        DMA(gq, uvp[:, :, 1, 64:128], uvd[:, :, 1, :], deps=[ms_uv], **iw)
        wprep = []
        def emit_wprep():
            for i in range(NCT // 2):
                wprep.append(P.dma(gq, (lambda i: (lambda e: e.dma_start(
                    out=d_scr[2 * i:2 * i + 2].rearrange("a p n -> (a p) n"),
                    in_=d_w_in_t[2 * i:2 * i + 2].rearrange("a p n -> (a p) n"))))(i), f"wp{i}", ()))

        cw2_res = res("consts2")

        def emit_initw2(deps=()):
            iw2 = dict(key="initw2", group=True, writes=[cw2_res], wadd=True, deps=list(deps))
            DMA(gq, w_o_conv_sb[:], d_w_o_conv.rearrange("(k p) n -> p k n", p=128), **iw2)
            DMA(gq, w_o_mla_sb[:], d_w_o_mla.rearrange("(k p) n -> p k n", p=128), **iw2)
            DMA(gq, w_out_sb[:], d_w_out.rearrange("(k p) n -> p k n", p=128), **iw2)


        cw3_res = res("consts3")

        def emit_cache(deps):
            iw3 = dict(key="initw3", group=True, writes=[cw3_res], wadd=True, deps=deps)
            DMA(gq, ckvT_all[:, 0:PAST], d_cache_kvT, **iw3)
            DMA(gq, kpeT_all[0:32, 0:PAST], d_cache_kpeT, **iw3)
            DMA(gq, V_all[:, 0:8, 0:128], d_cache_kv.rearrange("(b p) c -> p b c", p=128), **iw3)

        ic = dict(key="initc", group=True, writes=[c_res], wadd=True)
        DMA("sync", pre_g_sb[:], d_pre_g, **ic)
        DMA("sync", q_g_sb[:], d_q_g, **ic)
        DMA("sync", conv_w_sb[:], d_conv_w, **ic)
        DMA("sync", kv_g_bc[:], d_kv_g[0:1, :].broadcast_to([128, 128]), **ic)
        DMA("sync", post_g_bc[:], d_post_g[0:1, :].broadcast_to([128, 1024]), **ic)
        DMA("sync", mbias_sb[:], d_mbias, **ic)
        DMA("sync", cos_s[:], d_cos_s, **ic)
        DMA("sync", sin_s[:], d_sin_s, **ic)
        DMA("sync", stateT[:], d_stateT, **ic)
        DMA("sync", w_uqT_sb, d_w_uqT, **ic)
        DMA("sync", w_ukT_sb, d_w_ukT, **ic)
        DMA("sync", cos_q[:], d_cos_q, **ic)
        DMA("sync", sin_q[:], d_sin_q, **ic)

        wabs_mm = []
        for kc in range(2):
            for hq in range(2):
                bank, br = g_single()
                for hh in range(4):
                    h = hq * 4 + hh
                    wabs_mm.append(MM(bank[:, hh * 128:(hh + 1) * 128], w_uqT_sb[0:64, h, kc * 128:(kc + 1) * 128],
                                      w_ukT_sb[0:64, h, :], True, True, reads=[c_res], writes=[br], wadd=(hh > 0)))
                CP("vector", W_abs_sb[:, kc, hq * 4:hq * 4 + 4, :],
                   bank.rearrange("p (h c) -> p h c", h=4), reads=[br], writes=[c_res], wadd=True)

        wq = {"issued": 0, "used": 0, "total": 0}
        w_res = [Res() for _ in range(WR)]

        def w_issue():
            if record:
                return
            i = wq["issued"]
            if i >= wq["total"]:
                return
            ct = wseq[i]
            s = i % WR
            DMA("sync", w_ring[s][:], d_scr[ct], key=f"w{s}", writes=[w_res[s]], deps=[wprep[ct // 2]])
            wq["issued"] += 1

        def w_take(ct):
            i = wq["used"]
            if record:
                wseq.append(ct)
            else:
                assert wseq[i] == ct, (i, wseq[i], ct)
            wq["used"] += 1
            s = i % WR
            return w_ring[s], w_res[s]

        wseq = [] if record else list(wseq_in)
        wq["total"] = len(wseq)

        xctr = {"i": 0}
        last_x = {"op": None}
        xT_res = [Res(), Res()]

        def norm_tile(src, T, hT, hT_r, kbsel=3, spaced=False):
            s = xctr["i"] % 2
            xctr["i"] += 1
            xt, xr = xT_ring[s], xT_res[s]
            xsq, lnt, rstd = xsq_b[s], lnt_b[s], rstd_b[s]
            xsq_r, lnt_r, rstd_r = res(f"xsq{s}"), res(f"lnt{s}"), res(f"rstd{s}")
            last_x["op"] = DMA("sync", xt[:, :, 0:T], src, key=f"xT{s}", writes=[xr])
            TT("vector", xsq[:, :, 0:T], xt[:, :, 0:T], xt[:, :, 0:T], ALU.mult, reads=[xr], writes=[xsq_r])
            if spaced:
                for _ in range(12 if spaced is True else int(spaced)):
                    yield
            bank, br = k_bank(kbsel)
            for k in range(8):
                MM(bank[:, 0:T], ones_bf[:, :], xsq[:, k, 0:T], k == 0, k == 7,
                   reads=[xsq_r, c_res], writes=[br], wadd=(k > 0))
            ACT(lnt[:, 0:T], bank[:, 0:T], AF.Ln, reads=[br], writes=[lnt_r], scale=1.0 / D_MODEL, bias=eps_t[:, 0:1])
            ACT(rstd[:, 0:T], lnt[:, 0:T], AF.Exp, reads=[lnt_r], writes=[rstd_r], scale=-0.5)
            for k in range(8):
                STT(hT[:, k, 0:T], xt[:, k, 0:T], pre_g_sb[:, k:k + 1], rstd[:, 0:T], ALU.mult, ALU.mult,
                    reads=[xr, rstd_r, c_res], writes=[hT_r], wadd=(k > 0))
            yield
            if spaced is True:
                for _ in range(12):
                    yield

        kv_ready = {}
        kctr = {"i": 0}
        kpcol_res = [Res() for _ in range(16)]
        small_r = res("small")
        ropeA_r, ropeB_r = res("ropeA"), res("ropeB")
        for nm in ("ropeA", "ropeB", "qlat", "xtok", "yt"):
            res(nm).readers = list(wabs_mm[-1:])

        def key_tile(hT, hT_r, blocks, cos_of, sin_of, out_ckv, out_kpe, extra_deps=(), kbsel=3, spaced=False, tab_res=()):
            s = kctr["i"] % 2
            kctr["i"] += 1
            stc, stk, kb = stg_ckv[s], stg_kpe[s], kpe_bf[s]
            stc_r, stk_r, kb_r = res(f"stgc{s}"), res(f"stgk{s}"), res(f"kpebf{s}")
            junkk, smallk, kropeA, kropeB = junkk_b[s], smallk_b[s], kropeA_b[s], kropeB_b[s]
            smallk_r, junkk_r = res(f"smallk{s}"), res(f"junkk{s}")
            kropeA_r, kropeB_r = res(f"kropeA{s}"), res(f"kropeB{s}")
            bank, br = k_bank(kbsel)
            for bi, (off, n, blk) in enumerate(blocks):
                for k in range(8):
                    MM(bank[:n, bi * 160:(bi + 1) * 160], hT[:, k, off:off + n], w_kv_sb[:, k, :], k == 0, k == 7,
                       reads=[hT_r, c_res], writes=[br], wadd=(bi > 0 or k > 0))
            yield
            tb = bank[:, 320:448].bitcast(BF)
            tr = br
            for bi, (off, n, blk) in enumerate(blocks):
                r = blk // 16
                kvr = Res()
                kv_ready[blk] = kvr
                kvp = bank[:n, bi * 160:bi * 160 + 128]
                rp = bank[:n, bi * 160 + 128:bi * 160 + 160].rearrange("p (a b) -> p a b", a=2)
                ACT(junkk[:n, 0:128], kvp, AF.Square, reads=[br], writes=[junkk_r, smallk_r], accum=smallk[:n, 0:1])
                ACT(smallk[:n, 1:2], smallk[:n, 0:1], AF.Ln, reads=[smallk_r], writes=[smallk_r],
                    scale=1.0 / KV_LORA, bias=eps_t[:n, 0:1])
                ACT(smallk[:n, 2:3], smallk[:n, 1:2], AF.Exp, reads=[smallk_r], writes=[smallk_r], scale=-0.5)
                cb = cos_of(blk, n).unsqueeze(1).broadcast_to([n, 2, 16])
                sbb = sin_of(blk, n).unsqueeze(1).broadcast_to([n, 2, 16])
                A3 = kropeA[:n, 0:32].rearrange("p (a b) -> p a b", a=2)
                B3 = kropeB[:n, 0:32].rearrange("p (a b) -> p a b", a=2)
                TT("vector", A3, rp, cb, ALU.mult, reads=[br, c_res] + list(tab_res), writes=[kropeA_r])
                TT("vector", B3, rp, sbb, ALU.mult, reads=[br, c_res] + list(tab_res), writes=[kropeB_r])
                TT("vector", stk[:n, bi, 0:16], kropeA[:n, 0:16], kropeB[:n, 16:32], ALU.subtract,
                   reads=[kropeA_r, kropeB_r], writes=[stk_r], wadd=(bi > 0))
                TT("vector", stk[:n, bi, 16:32], kropeA[:n, 16:32], kropeB[:n, 0:16], ALU.add,
                   reads=[kropeA_r, kropeB_r], writes=[stk_r], wadd=True)
                CP("gpsimd", kb[:n, 32 * r:32 * r + 32], stk[:n, bi, :], reads=[stk_r], writes=[kb_r])
                STT(stc[:n, bi, :], kvp, smallk[:n, 2:3], kv_g_bc[:n, :], ALU.mult, ALU.mult,
                    reads=[br, smallk_r, c_res], writes=[stc_r], wadd=(bi > 0))
                CP("gpsimd", V_all[:n, blk, 0:128], stc[:n, bi, :], reads=[stc_r], writes=[kvr], deps=extra_deps)
                yield
                if spaced:
                    for _ in range(9):
                        yield
                TR(tb[:, 0:n], V_all[:n, blk, 0:128], n, reads=[kvr, c_res], writes=[tr], wadd=True)
                TR(tb[:, 128:128 + n], kb[:n, :], n, reads=[kb_r, c_res], writes=[tr], wadd=True)
                yield
                CP("vector", ckvT_all[:, blk * 128:blk * 128 + n], tb[:, 0:n],
                   reads=[tr], writes=[kvr], wadd=True, deps=extra_deps)
                c0 = (blk % 16) * 128
                CP("vector", kpeT_all[32 * r:32 * r + 32, c0:c0 + n], tb[32 * r:32 * r + 32, 128:128 + n],
                   reads=[tr], writes=[kvr, kpcol_res[blk % 16]], wadd=True, deps=extra_deps)
            nb = len(blocks)
            n0 = blocks[0][1]
            outs = []
            outs.append(DMA("sync", out_ckv, stc[:n0, 0:nb, :] if nb > 1 else stc[:n0, 0, :], key=f"stc{s}", reads=[stc_r]))
            outs.append(DMA("sync", out_kpe, stk[:n0, 0:nb, :] if nb > 1 else stk[:n0, 0, :], key=f"stk{s}", reads=[stk_r]))
            P.final.extend(outs)
            yield

        BG = []
        bgc = {"left": 0}

        def run(gen):
            for _ in gen:
                pass

        def bg_add(gen, nchunks):
            BG.append(gen)
            bgc["left"] += nchunks

        def bg_step(n):
            while n > 0 and BG:
                k_ = bgc.get("rr", 0) % min(2, len(BG))
                bgc["rr"] = bgc.get("rr", 0) + 1
                try:
                    next(BG[k_])
                    n -= 1
                    bgc["left"] = max(bgc["left"] - 1, 0)
                except StopIteration:
                    BG.pop(k_)

        def bg_drain():
            while BG:
                bg_step(1000)
            bgc["left"] = 0

        qlat_r, qsq_r, lnq_r, rstdq_r, qnT_r = res("qlat"), res("qsq"), res("lnq"), res("rstdq"), res("qnT")
        qabs_r, siluG_r, convres_r, ogT_r, merged_r = res("qabs"), res("siluG"), res("convres"), res("ogT"), res("merged")
        hTg_rs = [res("hTg0"), res("hTg1")]
        orep_r, rden_r, olat_r, olatT_r, xtok_r, yt_r = res("orep"), res("rden"), res("olat"), res("olatT"), res("xtok"), res("yt")
        convst_r = res("convst")
        ring2 = {}

        def nxt(name):
            i = ring2.get(name, 0)
            ring2[name] = i + 1
            return i % 2

        ptctr = {"i": 0}
        PT_res = [Res() for _ in range(3)]
        qpeT_res = [Res(), Res()]
        qpm_res = [Res(), Res()]
        qpmctr = {"i": 0}
        qpm_state = [None, None]
        attn_last = {}

        def inproj_ct(ct, N, hT_g, hTg_r):
            wt, wr = w_take(ct)
            bank, br = g_single()
            for k in range(8):
                MM(bank[:, 0:N], wt[:, k * 128:(k + 1) * 128], hT_g[:, k, 0:N], k == 0, k == 7,
                   reads=[wr, hTg_r], writes=[br], wadd=(k > 0))
            w_issue()
            if BG and bgc.get("inproj", False):
                bg_step(1)
            return bank, br

        def part1_gen(T, n_halo, qblocks, conv_state_src, conv_out, name, hT_g, hTg_r, xsrc, spaced_norm=False):
            yield from norm_tile(xsrc, T + n_halo, hT_g, hTg_r, spaced=spaced_norm)
            TA = T + n_halo
            nqb = len(qblocks)
            nq = qblocks[0]["nq"]
            for kc in range(2):
                bank, br = inproj_ct(kc, T, hT_g, hTg_r)
                CP("vector", qlat[:, kc, 0:T], bank[:, 0:T], reads=[br], writes=[qlat_r], wadd=(kc > 0))
                ACT(qsq[:, kc, 0:T], bank[:, 0:T], AF.Square, reads=[br], writes=[qsq_r], wadd=(kc > 0))
                yield
            bank, br = g_single()
            for kc in range(2):
                MM(bank[:, 0:T], ones_bf[:, :], qsq[:, kc, 0:T], kc == 0, kc == 1, reads=[qsq_r, c_res], writes=[br], wadd=(kc > 0))
            ACT(lnq[:, 0:T], bank[:, 0:T], AF.Ln, reads=[br], writes=[lnq_r], scale=1.0 / Q_LORA, bias=eps_t[:, 0:1])
            ACT(rstdq[:, 0:T], lnq[:, 0:T], AF.Exp, reads=[lnq_r], writes=[rstdq_r], scale=-0.5)
            for kc in range(2):
                STT(qnT[:, kc, 0:T], qlat[:, kc, 0:T], q_g_sb[:, kc:kc + 1], rstdq[:, 0:T], ALU.mult, ALU.mult,
                    reads=[qlat_r, rstdq_r, c_res], writes=[qnT_r], wadd=(kc > 0))
            yield
            for h in range(8):
                bank, br = g_single()
                for kc in range(2):
                    MM(bank[:, 0:T], W_abs_sb[:, kc, h, :], qnT[:, kc, 0:T], kc == 0, kc == 1,
                       reads=[qnT_r, c_res], writes=[br], wadd=(kc > 0))
                CP("scalar" if h % 2 == 0 else "vector", q_absT[:, h, 0:T], bank[:, 0:T], reads=[br], writes=[qabs_r], wadd=(h > 0))
                if h == 3:
                    yield
            KSUB = 99
            for qi, qb in enumerate(qblocks):
                qoff = qb["off"]
                for kc in range(2):
                    MM(m7[:nq, 0:256], qnT[:, kc, qoff:qoff + nq], w_uq_rope_sb[:, kc, :], kc == 0, kc == 1,
                       reads=[qnT_r, c_res], writes=[M_res], wadd=(kc > 0))
                q4 = lambda ap: ap.rearrange("p (h a b) -> p h a b", h=8, a=2)
                cb = qb["cos"].unsqueeze(1).unsqueeze(1).broadcast_to([nq, 8, 2, 16])
                sbb = qb["sin"].unsqueeze(1).unsqueeze(1).broadcast_to([nq, 8, 2, 16])
                TT("vector", q4(ropeA[:nq, :]), q4(m7[:nq, 0:256]), cb, ALU.mult, reads=[M_res, c_res], writes=[ropeA_r])
                TT("vector", q4(ropeB[:nq, :]), q4(m7[:nq, 0:256]), sbb, ALU.mult, reads=[M_res, c_res], writes=[ropeB_r])
                TT("vector", orep[:nq, :, 0, 0:16], q4(ropeA[:nq, :])[:, :, 0, :], q4(ropeB[:nq, :])[:, :, 1, :], ALU.subtract,
                   reads=[ropeA_r, ropeB_r], writes=[orep_r])
                TT("vector", orep[:nq, :, 0, 16:32], q4(ropeA[:nq, :])[:, :, 1, :], q4(ropeB[:nq, :])[:, :, 0, :], ALU.add,
                   reads=[ropeA_r, ropeB_r], writes=[orep_r], wadd=True)
                CP("vector", orep[:nq, :, 1:4, :], orep[:nq, :, 0:1, :].broadcast_to([nq, 8, 3, 32]),
                   reads=[orep_r], writes=[orep_r])
                tb_f, tr = g_single()
                tb = tb_f.bitcast(BF)
                for h in range(8):
                    TR(tb[:, h * 128:h * 128 + nq], orep[:nq, h, :, :].rearrange("p a b -> p (a b)"), nq,
                       reads=[orep_r, c_res], writes=[tr], wadd=(h > 0))
                CP("scalar", q_peT[qi][:, :, 0:nq], tb.rearrange("p (h q) -> p h q", h=8)[:, :, 0:nq],
                   reads=[tr], writes=[qpeT_res[qi]])
                yield
            yield
            for i in range(4):
                bank, br = inproj_ct(2 + i, T, hT_g, hTg_r)
                ACT(siluG[:, i, 0:T], bank[:, 0:T], AF.Silu, reads=[br], writes=[siluG_r], wadd=(i > 0))
                if i % 2 == 1:
                    yield
            yield
            for jc in range(4):
                if jc > 0:
                    yield
                s = nxt("conv")
                cc, cc_r = cc_sb[s], res("cc0")
                ue, ue_r = uext[s], res(f"ue{s}")
                cv, cv_r = cvt[s], res("cv0")
                sg, sg_r = sgc[s], res(f"sgc{s}")
                v3 = lambda ap: ap.rearrange("p (q t) -> p q t", q=nqb)
                b_cc, r_cc = inproj_ct(6 + 4 * jc, TA, hT_g, hTg_r)
                CP("scalar", cc[:, 0:TA], b_cc[:, 0:TA], reads=[r_cc], writes=[cc_r])
                b_cx, r_cx = inproj_ct(7 + 4 * jc, TA, hT_g, hTg_r)
                TT("vector", ue[:, 0:nqb, 2:2 + nq], v3(b_cx[:, 0:T]), v3(cc[:, 0:T]), ALU.mult,
                   reads=[r_cx, cc_r], writes=[ue_r])
                if n_halo:
                    TT("vector", ue[:, 0:nqb, 0:2], v3(b_cx[:, T:TA]), v3(cc[:, T:TA]), ALU.mult,
                       reads=[r_cx, cc_r], writes=[ue_r], wadd=True)
                else:
                    CP("gpsimd", ue[:, 0, 0:2], conv_state_src[:, jc, :], reads=[c_res], writes=[ue_r], wadd=True)
                b_gc, r_gc = inproj_ct(8 + 4 * jc, T, hT_g, hTg_r)
                ACT(sg[:, 0:T], b_gc[:, 0:T], AF.Silu, reads=[r_gc], writes=[sg_r])
                cv3 = cv[:, 0:T].rearrange("p (q t) -> p q t", q=nqb)
                TS("vector", cv3, ue[:, 0:nqb, 0:nq], conv_w_sb[:, jc, 0:1], ALU.mult, reads=[ue_r, c_res], writes=[cv_r])
                STT(cv3, ue[:, 0:nqb, 1:1 + nq], conv_w_sb[:, jc, 1:2], cv3, ALU.mult, ALU.add, reads=[ue_r, cv_r], writes=[cv_r])
                STT(cv3, ue[:, 0:nqb, 2:2 + nq], conv_w_sb[:, jc, 2:3], cv3, ALU.mult, ALU.add, reads=[ue_r, cv_r], writes=[cv_r])
                b_cb, r_cb = inproj_ct(9 + 4 * jc, T, hT_g, hTg_r)
                TT("vector", cv[:, 0:T], cv[:, 0:T], b_cb[:, 0:T], ALU.mult, reads=[cv_r, r_cb], writes=[cv_r])
                TT("vector", convres[:, jc, 0:T], cv[:, 0:T], sg[:, 0:T], ALU.mult, reads=[cv_r, sg_r], writes=[convres_r], wadd=(jc > 0))
                if conv_out is not None:
                    CP("gpsimd", convst[:, jc, :], ue[:, nqb - 1, nq:nq + 2], reads=[ue_r], writes=[convst_r], wadd=(jc > 0))
            if conv_out is not None:
                P.final.append(DMA("sync", conv_out, convst[:], key="stconv", reads=[convst_r]))


        def attention(qblocks, name):
            nq = qblocks[0]["nq"]
            slots_left = {"n": sum(len(q_["slots"]) + 1 for q_ in qblocks)}

            def post_A(qi):
                for ob, hs in enumerate([(0, 3), (3, 6), (6, 8)]):
                    nh = hs[1] - hs[0]
                    ov = o456[:nq, ob, 0:nh * 129].rearrange("p (h c) -> p h c", c=129)
                    I("vector", (lambda ov=ov, hs=hs: (lambda e: e.reciprocal(rden[:nq, hs[0]:hs[1]].unsqueeze(2), ov[:, :, 128:129])))(),
                      reads=[O_res[ob]], writes=[rden_r], wadd=(ob > 0))
                    TT("vector", olat[:nq, hs[0]:hs[1], :], ov[:, :, 0:128],
                       rden[:nq, hs[0]:hs[1]].unsqueeze(2).broadcast_to([nq, nh, 128]), ALU.mult,
                       reads=[O_res[ob], rden_r], writes=[olat_r], wadd=(ob > 0))

            def post_B(qi):
                qoff = qblocks[qi]["off"]
                tb_f, tr = g_single()
                tb = tb_f.bitcast(BF)
                for h in range(8):
                    TR(tb[:, h * 128:h * 128 + nq], olat[:nq, h, :], nq, reads=[olat_r, c_res], writes=[tr], wadd=(h > 0))
                CP("scalar", olatT[:, :, 0:nq], tb.rearrange("p (h q) -> p h q", h=8)[:, :, 0:nq], reads=[tr], writes=[olatT_r])
                ob_, or_ = g_single()
                for j in range(4):
                    MM(ob_[:, j * 128:j * 128 + nq], w_uv_pad[:, 2 * j, :], olatT[:, 2 * j, 0:nq], True, False,
                       reads=[olatT_r, c_res], writes=[or_], wadd=(j > 0))
                    MM(ob_[:, j * 128:j * 128 + nq], w_uv_pad[:, 2 * j + 1, :], olatT[:, 2 * j + 1, 0:nq], False, True,
                       reads=[olatT_r, c_res], writes=[or_], wadd=True)
                TT("vector", ogT[:, :, qoff:qoff + nq], ob_.rearrange("p (j q) -> p j q", j=4)[:, :, 0:nq],
                   siluG[:, :, qoff:qoff + nq], ALU.mult, reads=[or_, siluG_r], writes=[ogT_r], wadd=(qi > 0))

            pending_post = []
            for qi, qb in enumerate(qblocks):
                qoff = qb["off"]
                qp, qp_r = q_peT[qi], qpeT_res[qi]
                slots = qb["slots"]
                ns = len(slots)
                last_pv = None
                sl_state = {}

                def get_qvar(r):
                    key = (name, qi, r)
                    for i in range(2):
                        if qpm_state[i] is not None and qpm_state[i][0] == key:
                            return qpm[i], qpm_res[i]
                    i = qpmctr["i"] % 2
                    qpmctr["i"] += 1
                    if qpm_state[i] is not None and qpm_state[i][1] != r:
                        ro = qpm_state[i][1]
                        I("vector", (lambda i=i, ro=ro: (lambda e: e.memset(qpm[i][32 * ro:32 * ro + 32, :, :], 0.0)))(),
                          (), [qpm_res[i]])
                        CP("vector", qpm[i][32 * r:32 * r + 32, :, 0:nq], qp[32 * r:32 * r + 32, :, 0:nq],
                           reads=[qp_r], writes=[qpm_res[i]], wadd=True)
                    else:
                        CP("vector", qpm[i][32 * r:32 * r + 32, :, 0:nq], qp[32 * r:32 * r + 32, :, 0:nq],
                           reads=[qp_r, c_res], writes=[qpm_res[i]])
                    qpm_state[i] = (key, r)
                    return qpm[i], qpm_res[i]

                get_qvar(slots[0][0] // 16)

                def emit_S(si):
                    blk, nk, midx = slots[si]
                    r = blk // 16
                    kcol = blk * 128
                    kpcol = (blk % 16) * 128
                    kvr = kv_ready[blk]
                    pt_i = ptctr["i"] % 3
                    ptctr["i"] += 1
                    pt, pt_r = PT[pt_i], PT_res[pt_i]
                    qm, qm_r = get_qvar(r)
                    first_w = True
                    for hh in range(2):
                        sbank, brr = g_single()
                        S = sbank[:nk, 0:4 * nq].rearrange("p (h q) -> p h q", h=4)
                        MM(S, ckvT_all[:, kcol:kcol + nk], q_absT[:, 4 * hh:4 * hh + 4, qoff:qoff + nq], True, False,
                           reads=[kvr, qabs_r], writes=[brr])
                        MM(S, kpeT_all[:, kpcol:kpcol + nk], qm[:, 4 * hh:4 * hh + 4, 0:nq],
                           False, True, reads=[kvr, qm_r, c_res, kpcol_res[blk % 16]], writes=[brr], wadd=True)
                        if midx is None:
                            ACT(pt[:nk, 4 * hh:4 * hh + 4, 0:nq], S, AF.Exp, reads=[brr], writes=[pt_r], wadd=not first_w, scale=SM_SCALE)
                            first_w = False
                        else:
                            for hq in range(2):
                                col = qb["kidx"] * 8 + midx * 2 + hq
                                ACT(pt[:nk, 4 * hh:4 * hh + 4, 64 * hq:64 * hq + 64], S[:, :, 64 * hq:64 * hq + 64], AF.Exp,
                                    reads=[brr, c_res], writes=[pt_r], wadd=not first_w, scale=SM_SCALE,
                                    bias=mbias_sb[:nk, col:col + 1])
                                first_w = False
                    sl_state[si] = (pt, pt_r, kvr)

                def emit_PV(si):
                    blk, nk, midx = slots[si]
                    pt, pt_r, kvr = sl_state.pop(si)
                    lp = None
                    for h in range(8):
                        ob, oc = h // 3, h % 3
                        first = (si == 0 and oc == 0)
                        lp = MM(o456[:nq, ob, oc * 129:oc * 129 + 129], pt[:nk, h, 0:nq], V_all[:nk, blk, 0:129],
                                first, si == ns - 1, reads=[pt_r, kvr], writes=[O_res[ob]],
                                wadd=not first, sgcheck=True)
                    return lp

                for si in range(ns + 1):
                    if si + 2 < ns:
                        get_qvar(slots[si + 2][0] // 16)
                    if si < ns:
                        emit_S(si)
                    if si >= 1:
                        last_pv = emit_PV(si - 1)
                    if si == 2 and pending_post:
                        post_B(pending_post.pop(0))
                    if BG:
                        sleft = max(slots_left["n"], 1)
                        bg_step(-(-bgc["left"] // sleft))
                    slots_left["n"] -= 1
                attn_last[name] = last_pv
                while pending_post:
                    post_B(pending_post.pop(0))
                post_A(qi)
                pending_post.append(qi)
            while pending_post:
                post_B(pending_post.pop(0))


        def part3_gen(T, qblocks, hT_g, hTg_r):
            nq = qblocks[0]["nq"]
            for oc in range(8):
                if oc > 0:
                    yield
                s = nxt("merge")
                b_ma, r_ma = inproj_ct(22 + 2 * oc, T, hT_g, hTg_r)
                ACT(sga[s][:, 0:T], b_ma[:, 0:T], AF.Sigmoid, reads=[r_ma], writes=[res("sga0")])
                b_mb, r_mb = inproj_ct(23 + 2 * oc, T, hT_g, hTg_r)
                ACT(sgb[s][:, 0:T], b_mb[:, 0:T], AF.Sigmoid, reads=[r_mb], writes=[res("sgb0")])
                b_a, r_a = g_single()
                for kc in range(4):
                    MM(b_a[:, 0:T], w_o_mla_sb[:, kc, oc * 128:(oc + 1) * 128], ogT[:, kc, 0:T], kc == 0, kc == 3,
                       reads=[ogT_r, cw2_res], writes=[r_a], wadd=(kc > 0))
                TT("vector", t1[s][:, 0:T], b_a[:, 0:T], sga[s][:, 0:T], ALU.mult, reads=[r_a, res("sga0")], writes=[res("t1_0")])
                b_b, r_b = g_single()
                for kc in range(4):
                    MM(b_b[:, 0:T], w_o_conv_sb[:, kc, oc * 128:(oc + 1) * 128], convres[:, kc, 0:T], kc == 0, kc == 3,
                       reads=[convres_r, cw2_res], writes=[r_b], wadd=(kc > 0))
                TT("vector", t2[s][:, 0:T], b_b[:, 0:T], sgb[s][:, 0:T], ALU.mult, reads=[r_b, res("sgb0")], writes=[res("t2_0")])
                TT("gpsimd", mergedT[:, oc, 0:T], t1[s][:, 0:T], t2[s][:, 0:T], ALU.add,
                   reads=[res("t1_0"), res("t2_0")], writes=[merged_r], wadd=(oc > 0))
            for qb in qblocks:
                yield
                qoff = qb["off"]
                DMA("sync", xtok[:nq, :], qb["x_rows"], key="xtok", writes=[xtok_r])
                pr_t, (pr0, pr1) = g_pair()
                for half in range(2):
                    brr = pr0 if half == 0 else pr1
                    for k in range(8):
                        MM(pr_t[:nq, half, :], mergedT[:, k, qoff:qoff + nq], w_out_sb[:, k, half * 512:(half + 1) * 512],
                           k == 0, k == 7, reads=[merged_r, cw2_res], writes=[brr], wadd=(k > 0))
                z = pr_t[:nq, :, :].rearrange("p a b -> p (a b)")
                ACT(PT[0][:nq, :, :].rearrange("p a b -> p (a b)"), z, AF.Square, reads=[pr0, pr1],
                    writes=[PT_res[0], small_r], accum=small[:nq, 4:5])
                ACT(small[:nq, 5:6], small[:nq, 4:5], AF.Ln, reads=[small_r], writes=[small_r], scale=1.0 / D_MODEL, bias=eps_t[:nq, 0:1])
                ACT(small[:nq, 6:7], small[:nq, 5:6], AF.Exp, reads=[small_r], writes=[small_r], scale=-0.5)
                STT(yt[:nq, :], z, small[:nq, 6:7], post_g_bc[:nq, :], ALU.mult, ALU.mult,
                    reads=[pr0, pr1, small_r, c_res], writes=[yt_r])
                TT("gpsimd", yt[:nq, :], yt[:nq, :], xtok[:nq, :], ALU.add, reads=[yt_r, xtok_r], writes=[yt_r])
                P.final.append(DMA("sync", qb["y_rows"], yt[:nq, :], key="sty", reads=[yt_r]))


        STAGE = 99
        xT_seq_v = d_xT_seq.rearrange("(k p) t -> p k t", p=128)
        xT_own_v = d_xT_own.rearrange("(k p) t -> p k t", p=128)
        smp_done = []

        def prefix_tile(tl, kbsel=None, spaced=False):
            s = tl % 2
            if kbsel is None:
                kbsel = 3 if s == 0 else 7
            ck, sk = ck_ring[s], sk_ring[s]
            ck_r, sk_r = res(f"ck{s}"), res(f"sk{s}")
            DMA("sync", ck[:], d_cos_k[:, 2 * tl:2 * tl + 2, :], key=f"ck{s}", writes=[ck_r])
            DMA("sync", sk[:], d_sin_k[:, 2 * tl:2 * tl + 2, :], key=f"sk{s}", writes=[sk_r])
            yield from norm_tile(xT_seq_v[:, :, tl * TP:(tl + 1) * TP], TP, hT_p[s], res(f"hTp{s}"), kbsel=kbsel, spaced=spaced)
            yield from key_tile(hT_p[s], res(f"hTp{s}"), [(0, 128, 2 * tl), (128, 128, 2 * tl + 1)],
                                lambda blk, n: ck[:n, blk - 2 * tl, :], lambda blk, n: sk[:n, blk - 2 * tl, :],
                                o_ckv[tl * TP:(tl + 1) * TP, :].rearrange("(b p) c -> p b c", p=128),
                                o_kpe[tl * TP:(tl + 1) * TP, :].rearrange("(b p) c -> p b c", p=128),
                                kbsel=kbsel, spaced=spaced, tab_res=[ck_r, sk_r])
        CH_TILE = 8

        def run_rr(gens):
            live = list(gens)
            while live:
                for g_ in list(live):
                    try:
                        next(g_)
                    except StopIteration:
                        live.remove(g_)

        def pair_qbs(g):
            qbs = []
            for a in range(2):
                k = 2 * g + a
                nslot = 8 * g + 4 + 4 * a
                slots = [(b, 128, (b - (nslot - 4)) if b >= nslot - 4 else None) for b in range(nslot)]
                qbs.append(dict(off=128 * a, nq=128, slots=slots, cos=cos_q[:, k, :], sin=sin_q[:, k, :],
                                x_rows=d_x_own[k * 128:(k + 1) * 128, :], y_rows=o_y[k * 128:(k + 1) * 128, :],
                                kidx=k))
            return qbs

        def pair_part1(g, spaced_norm=False):
            return part1_gen(256, 4, pair_qbs(g), None, o_convp if g == 7 else None, f"p{g}",
                             hT_gs[g % 2], hTg_rs[g % 2], xT_own_v[:, :, g * TGP:(g + 1) * TGP], spaced_norm=spaced_norm)

        NUP = 4
        emit_wprep()
        run_rr([prefix_tile(0), prefix_tile(1)])
        run_rr([prefix_tile(2), prefix_tile(3)])
        g1 = pair_part1(0)
        next(g1)
        emit_initw2([last_x["op"]])
        CH_TILE = 52
        for _ in range(WR):
            w_issue()
        run(g1)
        for g in range(8):
            for tl in range(4 * g + NUP, min(4 * g + NUP + 4, 32)):
                bg_add(prefix_tile(tl, spaced=True), CH_TILE)
            attention(pair_qbs(g), f"p{g}")
            bg_drain()
            p3 = part3_gen(256, pair_qbs(g), hT_gs[g % 2], hTg_rs[g % 2])
            if g < 7:
                run_rr([pair_part1(g + 1, spaced_norm=3), p3])
            else:
                run(p3)
        p_done = [attn_last["p7"]]
        emit_cache(p_done)
        for b in range(SB0, SB0 + 8):
            kv_ready[b] = cw3_res
        smp_slots = [(b, 128, None) for b in range(SB0, SB0 + 8)] + [(SB0 + 8, DEC_SEQ, None)]
        smp_qbs = [dict(off=0, nq=DEC_SEQ, slots=smp_slots, cos=cos_s[:, :], sin=sin_s[:, :],
                        x_rows=d_x_smp, y_rows=o_ys, kidx=0)]
        g1 = part1_gen(DEC_SEQ, 0, smp_qbs, stateT, o_convs, "smp", hT_gs[0], hTg_rs[0],
                       d_xT_smp.rearrange("(k p) t -> p k t", p=128))
        next(g1)
        run(key_tile(hT_gs[0], hTg_rs[0], [(0, DEC_SEQ, SB0 + 8)],
                     lambda blk, n: cos_s[:n, :], lambda blk, n: sin_s[:n, :], o_ckv_s, o_kpe_s, extra_deps=p_done))
        run(g1)
        attention(smp_qbs, "smp")
        run(part3_gen(DEC_SEQ, smp_qbs, hT_gs[0], hTg_rs[0]))
        if record:
            return wseq
        assert STAGE < 99 or wq["used"] == wq["total"], (wq["used"], wq["total"])
        P.emit(nc, es)
    return nc


def _own_blocks(j):
    out = []
    for g in range(8):
        out += [8 * g + j, 8 * g + 7 - j]
    return out


def _rope_tables(pos):
    half = QK_ROPE // 2
    inv = (np.float32(10000.0) ** (-np.arange(half, dtype=np.float32) / np.float32(half))).astype(np.float32)
    ang = pos.astype(np.float32)[:, None] * inv[None, :]
    return np.cos(ang).astype(np.float32), np.sin(ang).astype(np.float32)


_NC_CACHE = {}


def kernel(x_prompt, x_sample, cache_kv_latent, cache_k_rope, state_conv, pre_norm, w_in, q_norm, w_uq, kv_norm,
           w_uk, w_uv, w_o_mla, conv_w, w_o_conv, w_out, post_norm):
    f = lambda a: np.ascontiguousarray(np.asarray(a, dtype=np.float32))
    x_prompt, x_sample = f(x_prompt), f(x_sample)
    W = f(w_in)[0]
    q0, kv0, kr0, gm0, cb0, cc0, cx0, gc0, mm0, mc0 = 0, 256, 384, 416, 928, 1440, 1952, 2464, 2976, 4000
    starts = [q0, q0 + 128] + [gm0 + 128 * i for i in range(4)]
    for jc in range(4):
        starts += [cc0 + 128 * jc, cx0 + 128 * jc, gc0 + 128 * jc, cb0 + 128 * jc]
    for oc in range(8):
        starts += [mm0 + 128 * oc, mc0 + 128 * oc]
    assert len(starts) == NCT
    cols = np.concatenate([np.arange(s, s + 128) for s in starts])
    w_in_t = f(W[:, cols].reshape(8, 128, NCT, 128).transpose(2, 1, 0, 3).reshape(NCT, 128, 1024))
    w_kv = f(W[:, kv0:kv0 + 160].reshape(8, 128, 160).transpose(1, 0, 2))
    wuq = f(w_uq)[0].reshape(Q_LORA, N_HEADS, QK_NOPE + QK_ROPE)
    w_uqT_nope = f(wuq[:, :, :QK_NOPE].transpose(2, 1, 0))
    w_ukT = f(f(w_uk)[0].transpose(2, 1, 0))
    w_uq_rope = f(wuq[:, :, QK_NOPE:].reshape(2, 128, N_HEADS * QK_ROPE).transpose(1, 0, 2))
    common = {
        "w_in_t": w_in_t, "w_kv": w_kv, "w_uqT_nope": w_uqT_nope, "w_ukT": w_ukT, "w_uq_rope": w_uq_rope,
        "w_uv": f(f(w_uv)[0]), "w_o_mla": f(f(w_o_mla)[0]), "w_o_conv": f(f(w_o_conv)[0]), "w_out": f(f(w_out)[0]),
        "pre_g": f(f(pre_norm)[0].reshape(8, 128).T), "q_g": f(f(q_norm)[0].reshape(2, 128).T),
        "kv_g": f(f(kv_norm)[0].reshape(1, 128)), "post_g": f(f(post_norm)[0].reshape(1, 1024)),
        "conv_w_t": f(f(conv_w)[0].reshape(3, 4, 128).transpose(2, 1, 0)),
        "ident": np.eye(128, dtype=np.float32),
    }
    ck, sk = _rope_tables(np.arange(SEQ))
    common["cos_k"] = f(ck.reshape(64, 128, 16).transpose(1, 0, 2))
    common["sin_k"] = f(sk.reshape(64, 128, 16).transpose(1, 0, 2))
    cs, ss = _rope_tables(PAST + np.arange(DEC_SEQ))
    common["cos_s"], common["sin_s"] = f(cs), f(ss)
    ckv_c, kpe_c, st_c = f(cache_kv_latent)[0], f(cache_k_rope)[0], f(state_conv)[0]

    in_maps = []
    own = []
    for c in range(8):
        b, j = c // 4, c % 4
        blks = _own_blocks(j)
        own.append(blks)
        xb = x_prompt[b]
        xT = f(xb.T)
        parts = []
        for g in range(8):
            A, B = blks[2 * g], blks[2 * g + 1]
            tok = list(range(128 * A, 128 * A + 128)) + list(range(128 * B, 128 * B + 128))
            main = xT[:, tok]
            halo = np.zeros((D_MODEL, 4), np.float32)
            for a, blk in enumerate((A, B)):
                if blk > 0:
                    halo[:, 2 * a:2 * a + 2] = xT[:, 128 * blk - 2:128 * blk]
            parts += [main, halo]
        xT_own = f(np.concatenate(parts, axis=1))
        x_own = f(np.concatenate([xb[128 * k:128 * k + 128] for k in blks], axis=0))
        pos_q = np.concatenate([np.arange(128 * k, 128 * k + 128) for k in blks])
        cq, sq = _rope_tables(pos_q)
        mbias = np.zeros((128, 128), np.float32)
        pidx = np.arange(128)
        for k, blk in enumerate(blks):
            g, a = k // 2, k % 2
            nslot = 8 * g + 4 + 4 * a
            for m in range(4):
                kb = nslot - 4 + m
                for hq in range(2):
                    vis = (2 * kb + pidx // 64) <= (2 * blk + hq)
                    mbias[:, k * 8 + m * 2 + hq] = np.where(vis, 0.0, -30000.0)
        d = dict(common)
        d.update({
            "xT_seq": xT, "xT_own": xT_own, "x_own": x_own,
            "xT_smp": f(x_sample[c].T), "x_smp": f(x_sample[c]),
            "cache_kvT": f(ckv_c[c].T), "cache_kv": f(ckv_c[c]), "cache_kpeT": f(kpe_c[c].T),
            "stateT": f(st_c[c].reshape(2, 4, 128).transpose(2, 1, 0)),
            "mbias": f(mbias),
            "cos_q": f(cq.reshape(16, 128, 16).transpose(1, 0, 2)), "sin_q": f(sq.reshape(16, 128, 16).transpose(1, 0, 2)),
        })
        in_maps.append(d)

    if "nc" not in _NC_CACHE:
        _NC_CACHE["nc"] = build_program(build_program(None))
    nc = _NC_CACHE["nc"]
    res = run_bass_kernel_spmd(nc, in_maps, core_ids=list(range(8)))
    R = res.results

    y_prompt = np.zeros((2, SEQ, D_MODEL), np.float32)
    y_sample = np.zeros((8, DEC_SEQ, D_MODEL), np.float32)
    ckv_p = np.zeros((1, 2, SEQ, 128), np.float32)
    kpe_p = np.zeros((1, 2, SEQ, 32), np.float32)
    cv_p = np.zeros((1, 2, 2, CONV_W), np.float32)
    ckv_s = np.zeros((1, 8, DEC_SEQ, 128), np.float32)
    kpe_s = np.zeros((1, 8, DEC_SEQ, 32), np.float32)
    cv_s = np.zeros((1, 8, 2, CONV_W), np.float32)
    for c in range(8):
        b, j = c // 4, c % 4
        r = R[c]
        oy = np.asarray(r["o_y"])
        for k, blk in enumerate(own[c]):
            y_prompt[b, 128 * blk:128 * blk + 128] = oy[128 * k:128 * k + 128]
        y_sample[c] = np.asarray(r["o_ys"])
        ckv_s[0, c] = np.asarray(r["o_ckv_s"])
        kpe_s[0, c] = np.asarray(r["o_kpe_s"])
        cv_s[0, c] = np.asarray(r["o_convs"]).transpose(2, 1, 0).reshape(2, CONV_W)
        if j == 0:
            ckv_p[0, b] = np.asarray(r["o_ckv"])
            kpe_p[0, b] = np.asarray(r["o_kpe"])
            cv_p[0, b] = np.asarray(r["o_convp"]).transpose(2, 1, 0).reshape(2, CONV_W)
    return (y_prompt, y_sample, ckv_p, kpe_p, cv_p, ckv_s, kpe_s, cv_s)
```

```python
import numpy as np
from contextlib import ExitStack
import concourse.bass as bass
import concourse.mybir as mybir
from concourse.bass_utils import run_bass_kernel_spmd

F32 = mybir.dt.float32
BF = mybir.dt.bfloat16
AF = mybir.ActivationFunctionType
ALU = mybir.AluOpType

D_MODEL = 1024
SEQ = 8192
N_HEADS = 8
QK_NOPE = 64
QK_ROPE = 32
KV_LORA = 128
Q_LORA = 256
CONV_W = 512
PAST = 1024
DEC_SEQ = 16
EPS = 1e-6
SM_SCALE = float((QK_NOPE + QK_ROPE) ** -0.5)
SB0 = 0
NCT = 38
TP = 256
TGP = 260
ENGS = ["sync", "tensor", "scalar", "vector", "gpsimd"]


class Op:
    __slots__ = ("eng", "fn", "deps", "signal", "val", "key", "is_dma", "group")

    def __init__(self, eng, fn, deps):
        self.eng, self.fn, self.deps = eng, fn, deps
        self.signal = False
        self.val = 0
        self.key = None
        self.is_dma = False
        self.group = False


class Res:
    __slots__ = ("writers", "readers", "excl")

    def __init__(self, excl=False):
        self.writers = []
        self.readers = []
        self.excl = excl


class Prog:
    def __init__(self):
        self.streams = {e: [] for e in ENGS}
        self.dma_cnt = {}
        self.final = []

    def _flat(self, deps):
        out = []
        for d in deps:
            if d is None:
                continue
            if isinstance(d, (list, tuple)):
                out.extend(self._flat(d))
            else:
                out.append(d)
        return out

    def op(self, eng, fn, deps=()):
        o = Op(eng, fn, self._flat(deps))
        for d in o.deps:
            d.signal = True
        self.streams[eng].append(o)
        return o

    def dma(self, eng, fn, key, deps=(), group=False):
        o = self.op(eng, fn, deps)
        o.is_dma = True
        o.key = key
        o.group = group
        n = self.dma_cnt.get(key, 0) + 1
        self.dma_cnt[key] = n
        o.val = 16 * n
        o.signal = True
        return o

    def I(self, eng, fn, reads=(), writes=(), deps=(), wadd=False, dma_key=None, group=False):
        d = list(deps)
        for r in reads:
            d += r.writers
            if r.excl:
                d += [x for x in r.readers if x.eng != eng]
        for w in writes:
            d += w.readers
            if not wadd:
                d += w.writers
        if dma_key is None:
            o = self.op(eng, fn, d)
        else:
            o = self.dma(eng, fn, dma_key, d, group)
        keep = (lambda lst: [x for x in lst if x.is_dma or x.eng != eng]) if dma_key is None else (lambda lst: list(lst))
        for r in reads:
            r.readers = keep(r.readers) + [o]
        for w in writes:
            if wadd:
                w.writers = keep(w.writers) + [o]
            else:
                w.writers = [o]
                w.readers = []
        return o

    def emit(self, nc, es):
        for eng in ENGS:
            cnt = 0
            for o in self.streams[eng]:
                if o.is_dma:
                    if o.group:
                        o.val = 16 * self.dma_cnt[o.key]
                elif o.signal:
                    cnt += 1
                    o.val = cnt
        sems = {}
        for eng in ENGS[1:]:
            sems[eng] = es.enter_context(nc.semaphore("s_" + eng))
        for key in self.dma_cnt:
            sems["d_" + key] = es.enter_context(nc.semaphore("d_" + key))
        block = es.enter_context(nc.Block())
        final = self.final

        def make(eng):
            ops = self.streams[eng]

            def body(e):
                waited = {}

                def wait_for(d):
                    name = ("d_" + d.key) if d.is_dma else d.eng
                    if waited.get(name, 0) >= d.val:
                        return
                    waited[name] = d.val
                    e.wait_ge(sems[name], d.val)

                for o in ops:
                    for d in o.deps:
                        wait_for(d)
                    ins = o.fn(e)
                    if o.is_dma:
                        ins.then_inc(sems["d_" + o.key], 16)
                    elif o.signal:
                        ins.then_inc(sems[eng], 1)
                if eng == "sync":
                    for d in final:
                        wait_for(d)
            return body

        block.sync(make("sync"))
        block.tensor(make("tensor"))
        block.scalar(make("scalar"))
        block.vector(make("vector"))
        block.gpsimd(make("gpsimd"))


def build_program(wseq_in=None):
    record = wseq_in is None
    nc = bass.Bass("TRN2", target_bir_lowering=False)
    P = Prog()
    I = P.I

    def din(name, shape, dt=F32):
        return nc.dram_tensor(name, list(shape), dt, kind="ExternalInput").ap()

    def dout(name, shape):
        return nc.dram_tensor(name, list(shape), F32, kind="ExternalOutput").ap()

    d_xT_seq = din("xT_seq", [D_MODEL, SEQ])
    d_xT_own = din("xT_own", [D_MODEL, 8 * TGP])
    d_x_own = din("x_own", [2048, D_MODEL])
    d_xT_smp = din("xT_smp", [D_MODEL, DEC_SEQ])
    d_x_smp = din("x_smp", [DEC_SEQ, D_MODEL])
    d_cache_kvT = din("cache_kvT", [128, PAST])
    d_cache_kv = din("cache_kv", [PAST, 128])
    d_cache_kpeT = din("cache_kpeT", [32, PAST])
    d_stateT = din("stateT", [128, 4, 2])
    d_mbias = din("mbias", [128, 128])
    d_w_in_t = din("w_in_t", [NCT, 128, 1024])
    d_w_kv = din("w_kv", [128, 8, 160])
    d_w_uqT = din("w_uqT_nope", [64, 8, 256])
    d_w_ukT = din("w_ukT", [64, 8, 128])
    d_w_uq_rope = din("w_uq_rope", [128, 2, 256])
    d_w_uv = din("w_uv", [128, 8, 64])
    d_w_o_mla = din("w_o_mla", [512, 1024])
    d_w_o_conv = din("w_o_conv", [512, 1024])
    d_w_out = din("w_out", [1024, 1024])
    d_pre_g = din("pre_g", [128, 8])
    d_q_g = din("q_g", [128, 2])
    d_kv_g = din("kv_g", [1, 128])
    d_post_g = din("post_g", [1, 1024])
    d_conv_w = din("conv_w_t", [128, 4, 3])
    d_cos_k = din("cos_k", [128, 64, 16])
    d_sin_k = din("sin_k", [128, 64, 16])
    d_cos_q = din("cos_q", [128, 16, 16])
    d_sin_q = din("sin_q", [128, 16, 16])
    d_cos_s = din("cos_s", [16, 16])
    d_sin_s = din("sin_s", [16, 16])
    d_ident = din("ident", [128, 128])

    d_scr = nc.dram_tensor("w_scr", [NCT, 128, 1024], BF, kind="Internal").ap()

    o_y = dout("o_y", [2048, D_MODEL])
    o_ys = dout("o_ys", [DEC_SEQ, D_MODEL])
    o_ckv = dout("o_ckv", [SEQ, 128])
    o_kpe = dout("o_kpe", [SEQ, 32])
    o_convp = dout("o_convp", [128, 4, 2])
    o_ckv_s = dout("o_ckv_s", [DEC_SEQ, 128])
    o_kpe_s = dout("o_kpe_s", [DEC_SEQ, 32])
    o_convs = dout("o_convs", [128, 4, 2])

    with ExitStack() as es:
        def sb(name, shape, dt):
            return es.enter_context(nc.sbuf_tensor(name, list(shape), dt))

        ckvT_all = sb("ckvT_all", [128, SEQ], BF)
        kpeT_all = sb("kpeT_all", [128, 2048], BF)
        V_all = sb("V_all", [128, 64, 130], BF)
        w_out_sb = sb("w_out_sb", [128, 8, 1024], BF)
        w_o_mla_sb = sb("w_o_mla_sb", [128, 4, 1024], BF)
        w_o_conv_sb = sb("w_o_conv_sb", [128, 4, 1024], BF)
        W_abs_sb = sb("W_abs_sb", [128, 2, 8, 128], BF)
        w_uq_rope_sb = sb("w_uq_rope_sb", [128, 2, 256], BF)
        w_uv_pad = sb("w_uv_pad", [128, 8, 128], BF)
        w_kv_sb = sb("w_kv_sb", [128, 8, 160], BF)
        scrA = sb("scrA", [128, 2048], F32)
        scrB = sb("scrB", [128, 1024], F32)
        w_uqT_sb = scrA[0:64, :].rearrange("p (h m) -> p h m", h=8)
        w_ukT_sb = scrB[0:64, :].rearrange("p (h c) -> p h c", h=8)
        post_g_bc = sb("post_g_bc", [128, 1024], F32)
        kv_g_bc = sb("kv_g_bc", [128, 128], F32)
        ident_bf = sb("ident_bf", [128, 128], BF)
        ones_bf = sb("ones_bf", [128, 128], BF)
        eps_t = sb("eps_t", [128, 1], F32)
        pre_g_sb = sb("pre_g_sb", [128, 8], F32)
        q_g_sb = sb("q_g_sb", [128, 2], F32)
        conv_w_sb = sb("conv_w_sb", [128, 4, 3], F32)
        ck_ring = [sb(f"ck{i}", [128, 2, 16], F32) for i in range(2)]
        sk_ring = [sb(f"sk{i}", [128, 2, 16], F32) for i in range(2)]
        cos_q = sb("cos_q_sb", [128, 16, 16], F32)
        sin_q = sb("sin_q_sb", [128, 16, 16], F32)
        cos_s = sb("cos_s_sb", [16, 16], F32)
        sin_s = sb("sin_s_sb", [16, 16], F32)
        stateT = sb("stateT_sb", [128, 4, 2], F32)
        convst = sb("convst", [128, 4, 2], F32)

        xT_ring = [sb(f"xT{i}", [128, 8, TGP], F32) for i in range(2)]
        xsq_b = [sb(f"xsq{i}", [128, 8, TGP], BF) for i in range(2)]
        lnt_b = [sb(f"lnt{i}", [128, TGP], F32) for i in range(2)]
        rstd_b = [sb(f"rstd{i}", [128, TGP], F32) for i in range(2)]
        hT_p = [sb(f"hTp{i}", [128, 8, TP], BF) for i in range(2)]
        hT_gs = [sb(f"hTg{i}", [128, 8, TGP], BF) for i in range(2)]
        WR = 6
        w_ring = [sb(f"wr{i}", [128, 1024], BF) for i in range(WR)]
        stg_ckv = [sb(f"stgc{i}", [128, 2, 128], F32) for i in range(2)]
        stg_kpe = [sb(f"stgk{i}", [128, 2, 32], F32) for i in range(2)]
        kpe_bf = [sb(f"kpebf{i}", [128, 128], BF) for i in range(2)]
        junkk_b = [sb(f"junkk{i}", [128, 128], BF) for i in range(2)]
        smallk_b = [sb(f"smallk{i}", [128, 8], F32) for i in range(2)]
        kropeA_b = [sb(f"kropeA{i}", [128, 32], F32) for i in range(2)]
        kropeB_b = [sb(f"kropeB{i}", [128, 32], F32) for i in range(2)]
        small = sb("small", [128, 16], F32)
        ropeA = scrB[:, 0:256]
        ropeB = scrB[:, 256:512]
        qlat = scrB[:, 512:1024].rearrange("p (k t) -> p k t", k=2)
        qsq = sb("qsq", [128, 2, 256], BF)
        lnq = sb("lnq", [128, 256], F32)
        rstdq = sb("rstdq", [128, 256], F32)
        qnT = sb("qnT", [128, 2, 256], BF)
        q_absT = sb("q_absT", [128, 8, 256], BF)
        siluG = sb("siluG", [128, 4, 256], BF)
        convres = sb("convres", [128, 4, 256], BF)
        ogT = sb("ogT", [128, 4, 256], BF)
        mergedT = sb("mergedT", [128, 8, 256], BF)
        cc_sb = [sb(f"ccsb{i}", [128, TGP], F32) for i in range(1)] * 2
        uext = [sb(f"uext{i}", [128, 2, 130], F32) for i in range(2)]
        cvt = [sb(f"cvt{i}", [128, 256], F32) for i in range(1)] * 2
        sgc = [sb(f"sgc{i}", [128, 256], F32) for i in range(2)]
        sga = [sb(f"sga{i}", [128, 256], F32) for i in range(1)] * 2
        sgb = [sb(f"sgb{i}", [128, 256], F32) for i in range(1)] * 2
        t1 = [sb(f"t1_{i}", [128, 256], F32) for i in range(1)] * 2
        t2 = [sb(f"t2_{i}", [128, 256], F32) for i in range(1)] * 2
        orep = sb("orep", [128, 8, 4, 32], BF)
        q_peT = [sb(f"qpeT{i}", [128, 8, 128], BF) for i in range(2)]
        mbias_sb = sb("mbias_sb", [128, 128], F32)
        qpm = [sb(f"qpm{i}", [128, 8, 128], BF) for i in range(2)]
        maskr = sb("maskr", [128, 4], F32)
        PT = [sb(f"PT{i}", [128, 8, 128], BF) for i in range(3)]
        rden = sb("rden", [128, 8], F32)
        olat = sb("olat", [128, 8, 128], BF)
        olatT = sb("olatT", [128, 8, 128], BF)
        xtok = scrA[:, 0:1024]
        yt = scrA[:, 1024:2048]

        g01 = es.enter_context(nc.psum_tensor("g01", [128, 2, 512], F32))
        g23 = es.enter_context(nc.psum_tensor("g23", [128, 2, 512], F32))
        o456 = es.enter_context(nc.psum_tensor("o456", [128, 3, 512], F32))
        m7 = es.enter_context(nc.psum_tensor("m7", [128, 512], F32))
        gbank_t = [g01, g01, g23, g23]
        G_res = [Res(True) for _ in range(4)]
        O_res = [Res(True) for _ in range(3)]
        M_res = Res(True)
        gctr = {"s": 0, "p": 0}

        def g_single():
            b = gctr["s"] % 3
            gctr["s"] += 1
            assert not (G_res[b].writers and not G_res[b].readers), "PSUM bank handed out while still live"
            return gbank_t[b][:, b % 2, :], G_res[b]

        def g_pair():
            p = gctr["p"] % 2
            gctr["p"] += 1
            return (g01, (G_res[0], G_res[1])) if p == 0 else (g23, (G_res[2], G_res[3]))

        def k_bank(sel=3):
            if sel == 7:
                return m7[:, :], M_res
            return g23[:, sel - 2, :], G_res[sel]

        R = {}

        def res(name):
            if name not in R:
                R[name] = Res()
            return R[name]

        def MM(out, lhsT, rhs, start, stop, reads=(), writes=(), wadd=False, tp=None, sgcheck=False, deps=()):
            def fn(e):
                kw = {}
                if tp is not None:
                    kw["tile_position"] = tp
                if sgcheck:
                    kw["skip_group_check"] = True
                return e.matmul(out, lhsT=lhsT, rhs=rhs, start=start, stop=stop, **kw)
            return I("tensor", fn, reads, writes, deps, wadd)

        def TR(out, in_, n, reads=(), writes=(), wadd=False, deps=()):
            return I("tensor", lambda e: e.transpose(out, in_, ident_bf[:n, :n]), reads, writes, deps, wadd)

        def ACT(out, in_, func, reads=(), writes=(), wadd=False, scale=None, bias=None, accum=None, deps=()):
            def fn(e):
                kw = {}
                if scale is not None:
                    kw["scale"] = scale
                if bias is not None:
                    kw["bias"] = bias
                if accum is not None:
                    kw["accum_out"] = accum
                return e.activation(out=out, in_=in_, func=func, **kw)
            return I("scalar", fn, reads, writes, deps, wadd)

        def TT(eng, out, in0, in1, op, reads=(), writes=(), wadd=False, deps=()):
            return I(eng, lambda e: e.tensor_tensor(out=out, in0=in0, in1=in1, op=op), reads, writes, deps, wadd)

        def STT(out, in0, scalar, in1, op0, op1, reads=(), writes=(), wadd=False, deps=()):
            return I("vector", lambda e: e.scalar_tensor_tensor(out=out, in0=in0, scalar=scalar, in1=in1, op0=op0, op1=op1),
                     reads, writes, deps, wadd)

        def TS(eng, out, in0, s1, op0, reads=(), writes=(), wadd=False, deps=()):
            return I(eng, lambda e: e.tensor_scalar(out=out, in0=in0, scalar1=s1, scalar2=None, op0=op0),
                     reads, writes, deps, wadd)

        def CP(eng, out, in_, reads=(), writes=(), wadd=False, deps=()):
            if eng == "scalar":
                return ACT(out, in_, AF.Copy, reads, writes, wadd, deps=deps)
            return I(eng, lambda e: e.tensor_copy(out=out, in_=in_), reads, writes, deps, wadd)

        def DMA(eng, out, in_, key, reads=(), writes=(), deps=(), group=False, wadd=False):
            return I(eng, lambda e: e.dma_start(out=out, in_=in_), reads, writes, deps, wadd, dma_key=key, group=group)

        def MEMSET(out, val, writes=(), wadd=False, deps=()):
            return I("gpsimd", lambda e: e.memset(out, val), (), writes, deps, wadd)

        c_res = res("consts")
        MEMSET(eps_t[:], EPS, [c_res])
        MEMSET(ones_bf[:], 1.0, [c_res], wadd=True)
        MEMSET(V_all[:, :, 128:130], 1.0, [c_res], wadd=True)
        ms_uv = MEMSET(w_uv_pad[:], 0.0, [c_res], wadd=True)
        ms_kpe = MEMSET(kpeT_all[:], 0.0, [c_res], wadd=True)
        for i in range(2):
            MEMSET(qpm[i][:], 0.0, [c_res], wadd=True)
        ms_mr = MEMSET(maskr[:], 0.0, [c_res], wadd=True)
        for r_ in range(4):
            MEMSET(maskr[32 * r_:32 * r_ + 32, r_:r_ + 1], 1.0, [c_res], wadd=True, deps=[ms_mr])
        for i in range(2):
            MEMSET(kpe_bf[i][:], 0.0, [c_res], wadd=True)

        gq = "gpsimd"
        iw = dict(key="initw", group=True, writes=[c_res], wadd=True)
        DMA(gq, ident_bf[:], d_ident, **iw)
        DMA(gq, w_kv_sb[:], d_w_kv, **iw)
        DMA(gq, w_uq_rope_sb[:], d_w_uq_rope, **iw)
        uvp = w_uv_pad[:].rearrange("p (j two) c -> p j two c", two=2)
        uvd = d_w_uv.rearrange("p (j two) c -> p j two c", two=2)
        DMA(gq, uvp[:, :, 0, 0:64], uvd[:, :, 0, :], deps=[ms_uv], **iw)
        DMA(gq, uvp[:, :, 1, 64:128], uvd[:, :, 1, :], deps=[ms_uv], **iw)
        wprep = []
        def emit_wprep(deps=()):
            for i in range(NCT // 2):
                wprep.append(P.dma(gq, (lambda i: (lambda e: e.dma_start(
                    out=d_scr[2 * i:2 * i + 2].rearrange("a p n -> (a p) n"),
                    in_=d_w_in_t[2 * i:2 * i + 2].rearrange("a p n -> (a p) n"))))(i), f"wp{i}", list(deps)))

        cw2_res = res("consts2")

        def emit_initw2(deps=()):
            iw2 = dict(key="initw2", group=True, writes=[cw2_res], wadd=True, deps=list(deps))
            DMA(gq, w_o_conv_sb[:], d_w_o_conv.rearrange("(k p) n -> p k n", p=128), **iw2)
            DMA(gq, w_o_mla_sb[:], d_w_o_mla.rearrange("(k p) n -> p k n", p=128), **iw2)
            DMA(gq, w_out_sb[:], d_w_out.rearrange("(k p) n -> p k n", p=128), **iw2)


        cw3_res = res("consts3")

        def emit_cache(deps):
            iw3 = dict(key="initw3", group=True, writes=[cw3_res], wadd=True, deps=deps)
            DMA(gq, ckvT_all[:, 0:PAST], d_cache_kvT, **iw3)
            DMA(gq, kpeT_all[0:32, 0:PAST], d_cache_kpeT, **iw3)
            DMA(gq, V_all[:, 0:8, 0:128], d_cache_kv.rearrange("(b p) c -> p b c", p=128), **iw3)

        ic = dict(key="initc", group=True, writes=[c_res], wadd=True)
        DMA("sync", pre_g_sb[:], d_pre_g, **ic)
        DMA("sync", q_g_sb[:], d_q_g, **ic)
        DMA("sync", conv_w_sb[:], d_conv_w, **ic)
        DMA("sync", kv_g_bc[:], d_kv_g[0:1, :].broadcast_to([128, 128]), **ic)
        DMA("sync", post_g_bc[:], d_post_g[0:1, :].broadcast_to([128, 1024]), **ic)
        DMA("sync", mbias_sb[:], d_mbias, **ic)
        DMA("sync", cos_s[:], d_cos_s, **ic)
        DMA("sync", sin_s[:], d_sin_s, **ic)
        DMA("sync", stateT[:], d_stateT, **ic)
        DMA("sync", w_uqT_sb, d_w_uqT, **ic)
        DMA("sync", w_ukT_sb, d_w_ukT, **ic)
        DMA("sync", cos_q[:], d_cos_q, **ic)
        DMA("sync", sin_q[:], d_sin_q, **ic)

        wabs_mm = []
        for kc in range(2):
            for hq in range(2):
                bank, br = g_single()
                for hh in range(4):
                    h = hq * 4 + hh
                    wabs_mm.append(MM(bank[:, hh * 128:(hh + 1) * 128], w_uqT_sb[0:64, h, kc * 128:(kc + 1) * 128],
                                      w_ukT_sb[0:64, h, :], True, True, reads=[c_res], writes=[br], wadd=(hh > 0)))
                CP("vector", W_abs_sb[:, kc, hq * 4:hq * 4 + 4, :],
                   bank.rearrange("p (h c) -> p h c", h=4), reads=[br], writes=[c_res], wadd=True)

        wq = {"issued": 0, "used": 0, "total": 0}
        w_res = [Res() for _ in range(WR)]

        def w_issue():
            if record:
                return
            i = wq["issued"]
            if i >= wq["total"]:
                return
            ct = wseq[i]
            s = i % WR
            DMA("sync", w_ring[s][:], d_scr[ct], key=f"w{s}", writes=[w_res[s]], deps=[wprep[ct // 2]])
            wq["issued"] += 1

        def w_take(ct):
            i = wq["used"]
            if record:
                wseq.append(ct)
            else:
                assert wseq[i] == ct, (i, wseq[i], ct)
            wq["used"] += 1
            s = i % WR
            return w_ring[s], w_res[s]

        wseq = [] if record else list(wseq_in)
        wq["total"] = len(wseq)

        xctr = {"i": 0}
        last_x = {"op": None}
        xT_res = [Res(), Res()]

        def norm_tile(src, T, hT, hT_r, kbsel=3, spaced=False):
            s = xctr["i"] % 2
            xctr["i"] += 1
            xt, xr = xT_ring[s], xT_res[s]
            xsq, lnt, rstd = xsq_b[s], lnt_b[s], rstd_b[s]
            xsq_r, lnt_r, rstd_r = res(f"xsq{s}"), res(f"lnt{s}"), res(f"rstd{s}")
            last_x["op"] = DMA("sync", xt[:, :, 0:T], src, key=f"xT{s}", writes=[xr])
            TT("vector", xsq[:, :, 0:T], xt[:, :, 0:T], xt[:, :, 0:T], ALU.mult, reads=[xr], writes=[xsq_r])
            if spaced:
                for _ in range(12 if spaced is True else int(spaced)):
                    yield
            bank, br = k_bank(kbsel)
            for k in range(8):
                MM(bank[:, 0:T], ones_bf[:, :], xsq[:, k, 0:T], k == 0, k == 7,
                   reads=[xsq_r, c_res], writes=[br], wadd=(k > 0))
            ACT(lnt[:, 0:T], bank[:, 0:T], AF.Ln, reads=[br], writes=[lnt_r], scale=1.0 / D_MODEL, bias=eps_t[:, 0:1])
            ACT(rstd[:, 0:T], lnt[:, 0:T], AF.Exp, reads=[lnt_r], writes=[rstd_r], scale=-0.5)
            for k in range(8):
                STT(hT[:, k, 0:T], xt[:, k, 0:T], pre_g_sb[:, k:k + 1], rstd[:, 0:T], ALU.mult, ALU.mult,
                    reads=[xr, rstd_r, c_res], writes=[hT_r], wadd=(k > 0))
            yield
            if spaced is True:
                for _ in range(12):
                    yield

        kv_ready = {}
        kctr = {"i": 0}
        kpcol_res = [Res() for _ in range(16)]
        small_r = res("small")
        ropeA_r, ropeB_r = res("ropeA"), res("ropeB")
        for nm in ("ropeA", "ropeB", "qlat", "xtok", "yt"):
            res(nm).readers = list(wabs_mm[-1:])

        def key_tile(hT, hT_r, blocks, cos_of, sin_of, out_ckv, out_kpe, extra_deps=(), kbsel=3, spaced=False, tab_res=()):
            s = kctr["i"] % 2
            kctr["i"] += 1
            stc, stk, kb = stg_ckv[s], stg_kpe[s], kpe_bf[s]
            stc_r, stk_r, kb_r = res(f"stgc{s}"), res(f"stgk{s}"), res(f"kpebf{s}")
            junkk, smallk, kropeA, kropeB = junkk_b[s], smallk_b[s], kropeA_b[s], kropeB_b[s]
            smallk_r, junkk_r = res(f"smallk{s}"), res(f"junkk{s}")
            kropeA_r, kropeB_r = res(f"kropeA{s}"), res(f"kropeB{s}")
            bank, br = k_bank(kbsel)
            for bi, (off, n, blk) in enumerate(blocks):
                for k in range(8):
                    MM(bank[:n, bi * 160:(bi + 1) * 160], hT[:, k, off:off + n], w_kv_sb[:, k, :], k == 0, k == 7,
                       reads=[hT_r, c_res], writes=[br], wadd=(bi > 0 or k > 0))
            yield
            tb = bank[:, 320:448].bitcast(BF)
            tr = br
            for bi, (off, n, blk) in enumerate(blocks):
                r = blk // 16
                kvr = Res()
                kv_ready[blk] = kvr
                kvp = bank[:n, bi * 160:bi * 160 + 128]
                rp = bank[:n, bi * 160 + 128:bi * 160 + 160].rearrange("p (a b) -> p a b", a=2)
                ACT(junkk[:n, 0:128], kvp, AF.Square, reads=[br], writes=[junkk_r, smallk_r], accum=smallk[:n, 0:1])
                ACT(smallk[:n, 1:2], smallk[:n, 0:1], AF.Ln, reads=[smallk_r], writes=[smallk_r],
                    scale=1.0 / KV_LORA, bias=eps_t[:n, 0:1])
                ACT(smallk[:n, 2:3], smallk[:n, 1:2], AF.Exp, reads=[smallk_r], writes=[smallk_r], scale=-0.5)
                cb = cos_of(blk, n).unsqueeze(1).broadcast_to([n, 2, 16])
                sbb = sin_of(blk, n).unsqueeze(1).broadcast_to([n, 2, 16])
                A3 = kropeA[:n, 0:32].rearrange("p (a b) -> p a b", a=2)
                B3 = kropeB[:n, 0:32].rearrange("p (a b) -> p a b", a=2)
                TT("vector", A3, rp, cb, ALU.mult, reads=[br, c_res] + list(tab_res), writes=[kropeA_r])
                TT("vector", B3, rp, sbb, ALU.mult, reads=[br, c_res] + list(tab_res), writes=[kropeB_r])
                TT("vector", stk[:n, bi, 0:16], kropeA[:n, 0:16], kropeB[:n, 16:32], ALU.subtract,
                   reads=[kropeA_r, kropeB_r], writes=[stk_r], wadd=(bi > 0))
                TT("vector", stk[:n, bi, 16:32], kropeA[:n, 16:32], kropeB[:n, 0:16], ALU.add,
                   reads=[kropeA_r, kropeB_r], writes=[stk_r], wadd=True)
                CP("gpsimd", kb[:n, 32 * r:32 * r + 32], stk[:n, bi, :], reads=[stk_r], writes=[kb_r])
                STT(stc[:n, bi, :], kvp, smallk[:n, 2:3], kv_g_bc[:n, :], ALU.mult, ALU.mult,
                    reads=[br, smallk_r, c_res], writes=[stc_r], wadd=(bi > 0))
                CP("gpsimd", V_all[:n, blk, 0:128], stc[:n, bi, :], reads=[stc_r], writes=[kvr], deps=extra_deps)
                yield
                if spaced:
                    for _ in range(9):
                        yield
                TR(tb[:, 0:n], V_all[:n, blk, 0:128], n, reads=[kvr, c_res], writes=[tr], wadd=True)
                TR(tb[:, 128:128 + n], kb[:n, :], n, reads=[kb_r, c_res], writes=[tr], wadd=True)
                yield
                CP("vector", ckvT_all[:, blk * 128:blk * 128 + n], tb[:, 0:n],
                   reads=[tr], writes=[kvr], wadd=True, deps=extra_deps)
                c0 = (blk % 16) * 128
                CP("vector", kpeT_all[32 * r:32 * r + 32, c0:c0 + n], tb[32 * r:32 * r + 32, 128:128 + n],
                   reads=[tr], writes=[kvr, kpcol_res[blk % 16]], wadd=True, deps=extra_deps)
            nb = len(blocks)
            n0 = blocks[0][1]
            outs = []
            outs.append(DMA("sync", out_ckv, stc[:n0, 0:nb, :] if nb > 1 else stc[:n0, 0, :], key=f"stc{s}", reads=[stc_r]))
            outs.append(DMA("sync", out_kpe, stk[:n0, 0:nb, :] if nb > 1 else stk[:n0, 0, :], key=f"stk{s}", reads=[stk_r]))
            P.final.extend(outs)
            yield

        BG = []
        bgc = {"left": 0}

        def run(gen):
            for _ in gen:
                pass

        def bg_add(gen, nchunks):
            BG.append(gen)
            bgc["left"] += nchunks

        def bg_step(n):
            while n > 0 and BG:
                k_ = bgc.get("rr", 0) % min(2, len(BG))
                bgc["rr"] = bgc.get("rr", 0) + 1
                try:
                    next(BG[k_])
                    n -= 1
                    bgc["left"] = max(bgc["left"] - 1, 0)
                except StopIteration:
                    BG.pop(k_)

        def bg_drain():
            while BG:
                bg_step(1000)
            bgc["left"] = 0

        qlat_r, qsq_r, lnq_r, rstdq_r, qnT_r = res("qlat"), res("qsq"), res("lnq"), res("rstdq"), res("qnT")
        qabs_r, siluG_r, convres_r, ogT_r, merged_r = res("qabs"), res("siluG"), res("convres"), res("ogT"), res("merged")
        hTg_rs = [res("hTg0"), res("hTg1")]
        orep_r, rden_r, olat_r, olatT_r, xtok_r, yt_r = res("orep"), res("rden"), res("olat"), res("olatT"), res("xtok"), res("yt")
        convst_r = res("convst")
        ring2 = {}

        def nxt(name):
            i = ring2.get(name, 0)
            ring2[name] = i + 1
            return i % 2

        ptctr = {"i": 0}
        PT_res = [Res() for _ in range(3)]
        qpeT_res = [Res(), Res()]
        qpm_res = [Res(), Res()]
        qpmctr = {"i": 0}
        qpm_state = [None, None]
        attn_last = {}

        def inproj_ct(ct, N, hT_g, hTg_r):
            wt, wr = w_take(ct)
            bank, br = g_single()
            for k in range(8):
                MM(bank[:, 0:N], wt[:, k * 128:(k + 1) * 128], hT_g[:, k, 0:N], k == 0, k == 7,
                   reads=[wr, hTg_r], writes=[br], wadd=(k > 0))
            w_issue()
            if BG and bgc.get("inproj", False):
                bg_step(1)
            return bank, br

        def part1_gen(T, n_halo, qblocks, conv_state_src, conv_out, name, hT_g, hTg_r, xsrc, spaced_norm=False):
            yield from norm_tile(xsrc, T + n_halo, hT_g, hTg_r, spaced=spaced_norm)
            TA = T + n_halo
            nqb = len(qblocks)
            nq = qblocks[0]["nq"]
            for kc in range(2):
                bank, br = inproj_ct(kc, T, hT_g, hTg_r)
                CP("vector", qlat[:, kc, 0:T], bank[:, 0:T], reads=[br], writes=[qlat_r], wadd=(kc > 0))
                ACT(qsq[:, kc, 0:T], bank[:, 0:T], AF.Square, reads=[br], writes=[qsq_r], wadd=(kc > 0))
                yield
            bank, br = g_single()
            for kc in range(2):
                MM(bank[:, 0:T], ones_bf[:, :], qsq[:, kc, 0:T], kc == 0, kc == 1, reads=[qsq_r, c_res], writes=[br], wadd=(kc > 0))
            ACT(lnq[:, 0:T], bank[:, 0:T], AF.Ln, reads=[br], writes=[lnq_r], scale=1.0 / Q_LORA, bias=eps_t[:, 0:1])
            ACT(rstdq[:, 0:T], lnq[:, 0:T], AF.Exp, reads=[lnq_r], writes=[rstdq_r], scale=-0.5)
            for kc in range(2):
                STT(qnT[:, kc, 0:T], qlat[:, kc, 0:T], q_g_sb[:, kc:kc + 1], rstdq[:, 0:T], ALU.mult, ALU.mult,
                    reads=[qlat_r, rstdq_r, c_res], writes=[qnT_r], wadd=(kc > 0))
            yield
            for h in range(8):
                bank, br = g_single()
                for kc in range(2):
                    MM(bank[:, 0:T], W_abs_sb[:, kc, h, :], qnT[:, kc, 0:T], kc == 0, kc == 1,
                       reads=[qnT_r, c_res], writes=[br], wadd=(kc > 0))
                CP("scalar" if h % 2 == 0 else "vector", q_absT[:, h, 0:T], bank[:, 0:T], reads=[br], writes=[qabs_r], wadd=(h > 0))
                if h == 3:
                    yield
            KSUB = 99
            for qi, qb in enumerate(qblocks):
                qoff = qb["off"]
                for kc in range(2):
                    MM(m7[:nq, 0:256], qnT[:, kc, qoff:qoff + nq], w_uq_rope_sb[:, kc, :], kc == 0, kc == 1,
                       reads=[qnT_r, c_res], writes=[M_res], wadd=(kc > 0))
                q4 = lambda ap: ap.rearrange("p (h a b) -> p h a b", h=8, a=2)
                cb = qb["cos"].unsqueeze(1).unsqueeze(1).broadcast_to([nq, 8, 2, 16])
                sbb = qb["sin"].unsqueeze(1).unsqueeze(1).broadcast_to([nq, 8, 2, 16])
                TT("vector", q4(ropeA[:nq, :]), q4(m7[:nq, 0:256]), cb, ALU.mult, reads=[M_res, c_res], writes=[ropeA_r])
                TT("vector", q4(ropeB[:nq, :]), q4(m7[:nq, 0:256]), sbb, ALU.mult, reads=[M_res, c_res], writes=[ropeB_r])
                TT("vector", orep[:nq, :, 0, 0:16], q4(ropeA[:nq, :])[:, :, 0, :], q4(ropeB[:nq, :])[:, :, 1, :], ALU.subtract,
                   reads=[ropeA_r, ropeB_r], writes=[orep_r])
                TT("vector", orep[:nq, :, 0, 16:32], q4(ropeA[:nq, :])[:, :, 1, :], q4(ropeB[:nq, :])[:, :, 0, :], ALU.add,
                   reads=[ropeA_r, ropeB_r], writes=[orep_r], wadd=True)
                CP("vector", orep[:nq, :, 1:4, :], orep[:nq, :, 0:1, :].broadcast_to([nq, 8, 3, 32]),
                   reads=[orep_r], writes=[orep_r])
                tb_f, tr = g_single()
                tb = tb_f.bitcast(BF)
                for h in range(8):
                    TR(tb[:, h * 128:h * 128 + nq], orep[:nq, h, :, :].rearrange("p a b -> p (a b)"), nq,
                       reads=[orep_r, c_res], writes=[tr], wadd=(h > 0))
                CP("scalar", q_peT[qi][:, :, 0:nq], tb.rearrange("p (h q) -> p h q", h=8)[:, :, 0:nq],
                   reads=[tr], writes=[qpeT_res[qi]])
                yield
            yield
            for i in range(4):
                bank, br = inproj_ct(2 + i, T, hT_g, hTg_r)
                ACT(siluG[:, i, 0:T], bank[:, 0:T], AF.Silu, reads=[br], writes=[siluG_r], wadd=(i > 0))
                if i % 2 == 1:
                    yield
            yield
            for jc in range(4):
                if jc > 0:
                    yield
                s = nxt("conv")
                cc, cc_r = cc_sb[s], res("cc0")
                ue, ue_r = uext[s], res(f"ue{s}")
                cv, cv_r = cvt[s], res("cv0")
                sg, sg_r = sgc[s], res(f"sgc{s}")
                v3 = lambda ap: ap.rearrange("p (q t) -> p q t", q=nqb)
                b_cc, r_cc = inproj_ct(6 + 4 * jc, TA, hT_g, hTg_r)
                CP("scalar", cc[:, 0:TA], b_cc[:, 0:TA], reads=[r_cc], writes=[cc_r])
                b_cx, r_cx = inproj_ct(7 + 4 * jc, TA, hT_g, hTg_r)
                TT("vector", ue[:, 0:nqb, 2:2 + nq], v3(b_cx[:, 0:T]), v3(cc[:, 0:T]), ALU.mult,
                   reads=[r_cx, cc_r], writes=[ue_r])
                if n_halo:
                    TT("vector", ue[:, 0:nqb, 0:2], v3(b_cx[:, T:TA]), v3(cc[:, T:TA]), ALU.mult,
                       reads=[r_cx, cc_r], writes=[ue_r], wadd=True)
                else:
                    CP("gpsimd", ue[:, 0, 0:2], conv_state_src[:, jc, :], reads=[c_res], writes=[ue_r], wadd=True)
                b_gc, r_gc = inproj_ct(8 + 4 * jc, T, hT_g, hTg_r)
                ACT(sg[:, 0:T], b_gc[:, 0:T], AF.Silu, reads=[r_gc], writes=[sg_r])
                cv3 = cv[:, 0:T].rearrange("p (q t) -> p q t", q=nqb)
                TS("vector", cv3, ue[:, 0:nqb, 0:nq], conv_w_sb[:, jc, 0:1], ALU.mult, reads=[ue_r, c_res], writes=[cv_r])
                STT(cv3, ue[:, 0:nqb, 1:1 + nq], conv_w_sb[:, jc, 1:2], cv3, ALU.mult, ALU.add, reads=[ue_r, cv_r], writes=[cv_r])
                STT(cv3, ue[:, 0:nqb, 2:2 + nq], conv_w_sb[:, jc, 2:3], cv3, ALU.mult, ALU.add, reads=[ue_r, cv_r], writes=[cv_r])
                b_cb, r_cb = inproj_ct(9 + 4 * jc, T, hT_g, hTg_r)
                TT("vector", cv[:, 0:T], cv[:, 0:T], b_cb[:, 0:T], ALU.mult, reads=[cv_r, r_cb], writes=[cv_r])
                TT("vector", convres[:, jc, 0:T], cv[:, 0:T], sg[:, 0:T], ALU.mult, reads=[cv_r, sg_r], writes=[convres_r], wadd=(jc > 0))
                if conv_out is not None:
                    CP("gpsimd", convst[:, jc, :], ue[:, nqb - 1, nq:nq + 2], reads=[ue_r], writes=[convst_r], wadd=(jc > 0))
            if conv_out is not None:
                P.final.append(DMA("sync", conv_out, convst[:], key="stconv", reads=[convst_r]))


        def attention(qblocks, name):
            nq = qblocks[0]["nq"]
            slots_left = {"n": sum(len(q_["slots"]) + 1 for q_ in qblocks)}

            def post_A(qi):
                for ob, hs in enumerate([(0, 3), (3, 6), (6, 8)]):
                    nh = hs[1] - hs[0]
                    ov = o456[:nq, ob, 0:nh * 129].rearrange("p (h c) -> p h c", c=129)
                    I("vector", (lambda ov=ov, hs=hs: (lambda e: e.reciprocal(rden[:nq, hs[0]:hs[1]].unsqueeze(2), ov[:, :, 128:129])))(),
                      reads=[O_res[ob]], writes=[rden_r], wadd=(ob > 0))
                    TT("vector", olat[:nq, hs[0]:hs[1], :], ov[:, :, 0:128],
                       rden[:nq, hs[0]:hs[1]].unsqueeze(2).broadcast_to([nq, nh, 128]), ALU.mult,
                       reads=[O_res[ob], rden_r], writes=[olat_r], wadd=(ob > 0))

            def post_B(qi):
                qoff = qblocks[qi]["off"]
                tb_f, tr = g_single()
                tb = tb_f.bitcast(BF)
                for h in range(8):
                    TR(tb[:, h * 128:h * 128 + nq], olat[:nq, h, :], nq, reads=[olat_r, c_res], writes=[tr], wadd=(h > 0))
                CP("scalar", olatT[:, :, 0:nq], tb.rearrange("p (h q) -> p h q", h=8)[:, :, 0:nq], reads=[tr], writes=[olatT_r])
                ob_, or_ = g_single()
                for j in range(4):
                    MM(ob_[:, j * 128:j * 128 + nq], w_uv_pad[:, 2 * j, :], olatT[:, 2 * j, 0:nq], True, False,
                       reads=[olatT_r, c_res], writes=[or_], wadd=(j > 0))
                    MM(ob_[:, j * 128:j * 128 + nq], w_uv_pad[:, 2 * j + 1, :], olatT[:, 2 * j + 1, 0:nq], False, True,
                       reads=[olatT_r, c_res], writes=[or_], wadd=True)
                TT("vector", ogT[:, :, qoff:qoff + nq], ob_.rearrange("p (j q) -> p j q", j=4)[:, :, 0:nq],
                   siluG[:, :, qoff:qoff + nq], ALU.mult, reads=[or_, siluG_r], writes=[ogT_r], wadd=(qi > 0))

            pending_post = []
            for qi, qb in enumerate(qblocks):
                qoff = qb["off"]
                qp, qp_r = q_peT[qi], qpeT_res[qi]
                slots = qb["slots"]
                ns = len(slots)
                last_pv = None
                sl_state = {}

                def get_qvar(r):
                    key = (name, qi, r)
                    for i in range(2):
                        if qpm_state[i] is not None and qpm_state[i][0] == key:
                            return qpm[i], qpm_res[i]
                    i = qpmctr["i"] % 2
                    qpmctr["i"] += 1
                    if qpm_state[i] is not None and qpm_state[i][1] != r:
                        ro = qpm_state[i][1]
                        I("vector", (lambda i=i, ro=ro: (lambda e: e.memset(qpm[i][32 * ro:32 * ro + 32, :, :], 0.0)))(),
                          (), [qpm_res[i]])
                        CP("vector", qpm[i][32 * r:32 * r + 32, :, 0:nq], qp[32 * r:32 * r + 32, :, 0:nq],
                           reads=[qp_r], writes=[qpm_res[i]], wadd=True)
                    else:
                        CP("vector", qpm[i][32 * r:32 * r + 32, :, 0:nq], qp[32 * r:32 * r + 32, :, 0:nq],
                           reads=[qp_r, c_res], writes=[qpm_res[i]])
                    qpm_state[i] = (key, r)
                    return qpm[i], qpm_res[i]

                get_qvar(slots[0][0] // 16)

                def emit_S(si):
                    blk, nk, midx = slots[si]
                    r = blk // 16
                    kcol = blk * 128
                    kpcol = (blk % 16) * 128
                    kvr = kv_ready[blk]
                    pt_i = ptctr["i"] % 3
                    ptctr["i"] += 1
                    pt, pt_r = PT[pt_i], PT_res[pt_i]
                    qm, qm_r = get_qvar(r)
                    first_w = True
                    for hh in range(2):
                        sbank, brr = g_single()
                        S = sbank[:nk, 0:4 * nq].rearrange("p (h q) -> p h q", h=4)
                        MM(S, ckvT_all[:, kcol:kcol + nk], q_absT[:, 4 * hh:4 * hh + 4, qoff:qoff + nq], True, False,
                           reads=[kvr, qabs_r], writes=[brr])
                        MM(S, kpeT_all[:, kpcol:kpcol + nk], qm[:, 4 * hh:4 * hh + 4, 0:nq],
                           False, True, reads=[kvr, qm_r, c_res, kpcol_res[blk % 16]], writes=[brr], wadd=True)
                        if midx is None:
                            ACT(pt[:nk, 4 * hh:4 * hh + 4, 0:nq], S, AF.Exp, reads=[brr], writes=[pt_r], wadd=not first_w, scale=SM_SCALE)
                            first_w = False
                        else:
                            for hq in range(2):
                                col = qb["kidx"] * 8 + midx * 2 + hq
                                ACT(pt[:nk, 4 * hh:4 * hh + 4, 64 * hq:64 * hq + 64], S[:, :, 64 * hq:64 * hq + 64], AF.Exp,
                                    reads=[brr, c_res], writes=[pt_r], wadd=not first_w, scale=SM_SCALE,
                                    bias=mbias_sb[:nk, col:col + 1])
                                first_w = False
                    sl_state[si] = (pt, pt_r, kvr)

                def emit_PV(si):
                    blk, nk, midx = slots[si]
                    pt, pt_r, kvr = sl_state.pop(si)
                    lp = None
                    for h in range(8):
                        ob, oc = h // 3, h % 3
                        first = (si == 0 and oc == 0)
                        lp = MM(o456[:nq, ob, oc * 129:oc * 129 + 129], pt[:nk, h, 0:nq], V_all[:nk, blk, 0:129],
                                first, si == ns - 1, reads=[pt_r, kvr], writes=[O_res[ob]],
                                wadd=not first, sgcheck=True)
                    return lp

                for si in range(ns + 1):
                    if si + 2 < ns:
                        get_qvar(slots[si + 2][0] // 16)
                    if si < ns:
                        emit_S(si)
                    if si >= 1:
                        last_pv = emit_PV(si - 1)
                    if si == 2 and pending_post:
                        post_B(pending_post.pop(0))
                    if BG:
                        sleft = max(slots_left["n"], 1)
                        bg_step(-(-bgc["left"] // sleft))
                    slots_left["n"] -= 1
                attn_last[name] = last_pv
                while pending_post:
                    post_B(pending_post.pop(0))
                post_A(qi)
                pending_post.append(qi)
            while pending_post:
                post_B(pending_post.pop(0))


        def part3_gen(T, qblocks, hT_g, hTg_r):
            nq = qblocks[0]["nq"]
            for oc in range(8):
                if oc > 0:
                    yield
                s = nxt("merge")
                b_ma, r_ma = inproj_ct(22 + 2 * oc, T, hT_g, hTg_r)
                ACT(sga[s][:, 0:T], b_ma[:, 0:T], AF.Sigmoid, reads=[r_ma], writes=[res("sga0")])
                b_mb, r_mb = inproj_ct(23 + 2 * oc, T, hT_g, hTg_r)
                ACT(sgb[s][:, 0:T], b_mb[:, 0:T], AF.Sigmoid, reads=[r_mb], writes=[res("sgb0")])
                b_a, r_a = g_single()
                for kc in range(4):
                    MM(b_a[:, 0:T], w_o_mla_sb[:, kc, oc * 128:(oc + 1) * 128], ogT[:, kc, 0:T], kc == 0, kc == 3,
                       reads=[ogT_r, cw2_res], writes=[r_a], wadd=(kc > 0))
                TT("vector", t1[s][:, 0:T], b_a[:, 0:T], sga[s][:, 0:T], ALU.mult, reads=[r_a, res("sga0")], writes=[res("t1_0")])
                b_b, r_b = g_single()
                for kc in range(4):
                    MM(b_b[:, 0:T], w_o_conv_sb[:, kc, oc * 128:(oc + 1) * 128], convres[:, kc, 0:T], kc == 0, kc == 3,
                       reads=[convres_r, cw2_res], writes=[r_b], wadd=(kc > 0))
                TT("vector", t2[s][:, 0:T], b_b[:, 0:T], sgb[s][:, 0:T], ALU.mult, reads=[r_b, res("sgb0")], writes=[res("t2_0")])
                TT("gpsimd", mergedT[:, oc, 0:T], t1[s][:, 0:T], t2[s][:, 0:T], ALU.add,
                   reads=[res("t1_0"), res("t2_0")], writes=[merged_r], wadd=(oc > 0))
            for qb in qblocks:
                yield
                qoff = qb["off"]
                DMA("sync", xtok[:nq, :], qb["x_rows"], key="xtok", writes=[xtok_r])
                pr_t, (pr0, pr1) = g_pair()
                for half in range(2):
                    brr = pr0 if half == 0 else pr1
                    for k in range(8):
                        MM(pr_t[:nq, half, :], mergedT[:, k, qoff:qoff + nq], w_out_sb[:, k, half * 512:(half + 1) * 512],
                           k == 0, k == 7, reads=[merged_r, cw2_res], writes=[brr], wadd=(k > 0))
                z = pr_t[:nq, :, :].rearrange("p a b -> p (a b)")
                ACT(PT[0][:nq, :, :].rearrange("p a b -> p (a b)"), z, AF.Square, reads=[pr0, pr1],
                    writes=[PT_res[0], small_r], accum=small[:nq, 4:5])
                ACT(small[:nq, 5:6], small[:nq, 4:5], AF.Ln, reads=[small_r], writes=[small_r], scale=1.0 / D_MODEL, bias=eps_t[:nq, 0:1])
                ACT(small[:nq, 6:7], small[:nq, 5:6], AF.Exp, reads=[small_r], writes=[small_r], scale=-0.5)
                STT(yt[:nq, :], z, small[:nq, 6:7], post_g_bc[:nq, :], ALU.mult, ALU.mult,
                    reads=[pr0, pr1, small_r, c_res], writes=[yt_r])
                TT("gpsimd", yt[:nq, :], yt[:nq, :], xtok[:nq, :], ALU.add, reads=[yt_r, xtok_r], writes=[yt_r])
                P.final.append(DMA("sync", qb["y_rows"], yt[:nq, :], key="sty", reads=[yt_r]))


        STAGE = 99
        xT_seq_v = d_xT_seq.rearrange("(k p) t -> p k t", p=128)
        xT_own_v = d_xT_own.rearrange("(k p) t -> p k t", p=128)
        smp_done = []

        def prefix_tile(tl, kbsel=None, spaced=False):
            s = tl % 2
            if kbsel is None:
                kbsel = 3 if s == 0 else 7
            ck, sk = ck_ring[s], sk_ring[s]
            ck_r, sk_r = res(f"ck{s}"), res(f"sk{s}")
            DMA("sync", ck[:], d_cos_k[:, 2 * tl:2 * tl + 2, :], key=f"ck{s}", writes=[ck_r])
            DMA("sync", sk[:], d_sin_k[:, 2 * tl:2 * tl + 2, :], key=f"sk{s}", writes=[sk_r])
            yield from norm_tile(xT_seq_v[:, :, tl * TP:(tl + 1) * TP], TP, hT_p[s], res(f"hTp{s}"), kbsel=kbsel, spaced=spaced)
            yield from key_tile(hT_p[s], res(f"hTp{s}"), [(0, 128, 2 * tl), (128, 128, 2 * tl + 1)],
                                lambda blk, n: ck[:n, blk - 2 * tl, :], lambda blk, n: sk[:n, blk - 2 * tl, :],
                                o_ckv[tl * TP:(tl + 1) * TP, :].rearrange("(b p) c -> p b c", p=128),
                                o_kpe[tl * TP:(tl + 1) * TP, :].rearrange("(b p) c -> p b c", p=128),
                                kbsel=kbsel, spaced=spaced, tab_res=[ck_r, sk_r])
        CH_TILE = 8

        def run_rr(gens):
            live = list(gens)
            while live:
                for g_ in list(live):
                    try:
                        next(g_)
                    except StopIteration:
                        live.remove(g_)

        def pair_qbs(g):
            qbs = []
            for a in range(2):
                k = 2 * g + a
                nslot = 8 * g + 4 + 4 * a
                slots = [(b, 128, (b - (nslot - 4)) if b >= nslot - 4 else None) for b in range(nslot)]
                qbs.append(dict(off=128 * a, nq=128, slots=slots, cos=cos_q[:, k, :], sin=sin_q[:, k, :],
                                x_rows=d_x_own[k * 128:(k + 1) * 128, :], y_rows=o_y[k * 128:(k + 1) * 128, :],
                                kidx=k))
            return qbs

        def pair_part1(g, spaced_norm=False):
            return part1_gen(256, 4, pair_qbs(g), None, o_convp if g == 7 else None, f"p{g}",
                             hT_gs[g % 2], hTg_rs[g % 2], xT_own_v[:, :, g * TGP:(g + 1) * TGP], spaced_norm=spaced_norm)

        NUP = 4
        t0_, t1_ = prefix_tile(0), prefix_tile(1)
        next(t0_)
        x0_ = last_x["op"]
        next(t1_)
        x1_ = last_x["op"]
        emit_wprep([x0_, x1_])
        run_rr([t0_, t1_])
        run_rr([prefix_tile(2), prefix_tile(3)])
        g1 = pair_part1(0)
        next(g1)
        emit_initw2([last_x["op"]])
        CH_TILE = 52
        for _ in range(WR):
            w_issue()
        run(g1)
        for g in range(8):
            for tl in range(4 * g + NUP, min(4 * g + NUP + 4, 32)):
                bg_add(prefix_tile(tl, spaced=True), CH_TILE)
            attention(pair_qbs(g), f"p{g}")
            bg_drain()
            p3 = part3_gen(256, pair_qbs(g), hT_gs[g % 2], hTg_rs[g % 2])
            if g < 7:
                run_rr([pair_part1(g + 1, spaced_norm=3), p3])
            else:
                run(p3)
        p_done = [attn_last["p7"]]
        emit_cache(p_done)
        for b in range(SB0, SB0 + 8):
            kv_ready[b] = cw3_res
        smp_slots = [(b, 128, None) for b in range(SB0, SB0 + 8)] + [(SB0 + 8, DEC_SEQ, None)]
        smp_qbs = [dict(off=0, nq=DEC_SEQ, slots=smp_slots, cos=cos_s[:, :], sin=sin_s[:, :],
                        x_rows=d_x_smp, y_rows=o_ys, kidx=0)]
        g1 = part1_gen(DEC_SEQ, 0, smp_qbs, stateT, o_convs, "smp", hT_gs[0], hTg_rs[0],
                       d_xT_smp.rearrange("(k p) t -> p k t", p=128))
        next(g1)
        run(key_tile(hT_gs[0], hTg_rs[0], [(0, DEC_SEQ, SB0 + 8)],
                     lambda blk, n: cos_s[:n, :], lambda blk, n: sin_s[:n, :], o_ckv_s, o_kpe_s, extra_deps=p_done))
        run(g1)
        attention(smp_qbs, "smp")
        run(part3_gen(DEC_SEQ, smp_qbs, hT_gs[0], hTg_rs[0]))
        if record:
            return wseq
        assert STAGE < 99 or wq["used"] == wq["total"], (wq["used"], wq["total"])
        P.emit(nc, es)
    return nc


def _own_blocks(j):
    out = []
    for g in range(8):
        out += [8 * g + j, 8 * g + 7 - j]
    return out


def _rope_tables(pos):
    half = QK_ROPE // 2
    inv = (np.float32(10000.0) ** (-np.arange(half, dtype=np.float32) / np.float32(half))).astype(np.float32)
    ang = pos.astype(np.float32)[:, None] * inv[None, :]
    return np.cos(ang).astype(np.float32), np.sin(ang).astype(np.float32)


_NC_CACHE = {}


def kernel(x_prompt, x_sample, cache_kv_latent, cache_k_rope, state_conv, pre_norm, w_in, q_norm, w_uq, kv_norm,
           w_uk, w_uv, w_o_mla, conv_w, w_o_conv, w_out, post_norm):
    f = lambda a: np.ascontiguousarray(np.asarray(a, dtype=np.float32))
    x_prompt, x_sample = f(x_prompt), f(x_sample)
    W = f(w_in)[0]
    q0, kv0, kr0, gm0, cb0, cc0, cx0, gc0, mm0, mc0 = 0, 256, 384, 416, 928, 1440, 1952, 2464, 2976, 4000
    starts = [q0, q0 + 128] + [gm0 + 128 * i for i in range(4)]
    for jc in range(4):
        starts += [cc0 + 128 * jc, cx0 + 128 * jc, gc0 + 128 * jc, cb0 + 128 * jc]
    for oc in range(8):
        starts += [mm0 + 128 * oc, mc0 + 128 * oc]
    assert len(starts) == NCT
    cols = np.concatenate([np.arange(s, s + 128) for s in starts])
    w_in_t = f(W[:, cols].reshape(8, 128, NCT, 128).transpose(2, 1, 0, 3).reshape(NCT, 128, 1024))
    w_kv = f(W[:, kv0:kv0 + 160].reshape(8, 128, 160).transpose(1, 0, 2))
    wuq = f(w_uq)[0].reshape(Q_LORA, N_HEADS, QK_NOPE + QK_ROPE)
    w_uqT_nope = f(wuq[:, :, :QK_NOPE].transpose(2, 1, 0))
    w_ukT = f(f(w_uk)[0].transpose(2, 1, 0))
    w_uq_rope = f(wuq[:, :, QK_NOPE:].reshape(2, 128, N_HEADS * QK_ROPE).transpose(1, 0, 2))
    common = {
        "w_in_t": w_in_t, "w_kv": w_kv, "w_uqT_nope": w_uqT_nope, "w_ukT": w_ukT, "w_uq_rope": w_uq_rope,
        "w_uv": f(f(w_uv)[0]), "w_o_mla": f(f(w_o_mla)[0]), "w_o_conv": f(f(w_o_conv)[0]), "w_out": f(f(w_out)[0]),
        "pre_g": f(f(pre_norm)[0].reshape(8, 128).T), "q_g": f(f(q_norm)[0].reshape(2, 128).T),
        "kv_g": f(f(kv_norm)[0].reshape(1, 128)), "post_g": f(f(post_norm)[0].reshape(1, 1024)),
        "conv_w_t": f(f(conv_w)[0].reshape(3, 4, 128).transpose(2, 1, 0)),
        "ident": np.eye(128, dtype=np.float32),
    }
    ck, sk = _rope_tables(np.arange(SEQ))
    common["cos_k"] = f(ck.reshape(64, 128, 16).transpose(1, 0, 2))
    common["sin_k"] = f(sk.reshape(64, 128, 16).transpose(1, 0, 2))
    cs, ss = _rope_tables(PAST + np.arange(DEC_SEQ))
    common["cos_s"], common["sin_s"] = f(cs), f(ss)
    ckv_c, kpe_c, st_c = f(cache_kv_latent)[0], f(cache_k_rope)[0], f(state_conv)[0]

    in_maps = []
    own = []
    for c in range(8):
        b, j = c // 4, c % 4
        blks = _own_blocks(j)
        own.append(blks)
        xb = x_prompt[b]
        xT = f(xb.T)
        parts = []
        for g in range(8):
            A, B = blks[2 * g], blks[2 * g + 1]
            tok = list(range(128 * A, 128 * A + 128)) + list(range(128 * B, 128 * B + 128))
            main = xT[:, tok]
            halo = np.zeros((D_MODEL, 4), np.float32)
            for a, blk in enumerate((A, B)):
                if blk > 0:
                    halo[:, 2 * a:2 * a + 2] = xT[:, 128 * blk - 2:128 * blk]
            parts += [main, halo]
        xT_own = f(np.concatenate(parts, axis=1))
        x_own = f(np.concatenate([xb[128 * k:128 * k + 128] for k in blks], axis=0))
        pos_q = np.concatenate([np.arange(128 * k, 128 * k + 128) for k in blks])
        cq, sq = _rope_tables(pos_q)
        mbias = np.zeros((128, 128), np.float32)
        pidx = np.arange(128)
        for k, blk in enumerate(blks):
            g, a = k // 2, k % 2
            nslot = 8 * g + 4 + 4 * a
            for m in range(4):
                kb = nslot - 4 + m
                for hq in range(2):
                    vis = (2 * kb + pidx // 64) <= (2 * blk + hq)
                    mbias[:, k * 8 + m * 2 + hq] = np.where(vis, 0.0, -30000.0)
        d = dict(common)
        d.update({
            "xT_seq": xT, "xT_own": xT_own, "x_own": x_own,
            "xT_smp": f(x_sample[c].T), "x_smp": f(x_sample[c]),
            "cache_kvT": f(ckv_c[c].T), "cache_kv": f(ckv_c[c]), "cache_kpeT": f(kpe_c[c].T),
            "stateT": f(st_c[c].reshape(2, 4, 128).transpose(2, 1, 0)),
            "mbias": f(mbias),
            "cos_q": f(cq.reshape(16, 128, 16).transpose(1, 0, 2)), "sin_q": f(sq.reshape(16, 128, 16).transpose(1, 0, 2)),
        })
        in_maps.append(d)

    if "nc" not in _NC_CACHE:
        _NC_CACHE["nc"] = build_program(build_program(None))
    nc = _NC_CACHE["nc"]
    res = run_bass_kernel_spmd(nc, in_maps, core_ids=list(range(8)))
    R = res.results

    y_prompt = np.zeros((2, SEQ, D_MODEL), np.float32)
    y_sample = np.zeros((8, DEC_SEQ, D_MODEL), np.float32)
    ckv_p = np.zeros((1, 2, SEQ, 128), np.float32)
    kpe_p = np.zeros((1, 2, SEQ, 32), np.float32)
    cv_p = np.zeros((1, 2, 2, CONV_W), np.float32)
    ckv_s = np.zeros((1, 8, DEC_SEQ, 128), np.float32)
    kpe_s = np.zeros((1, 8, DEC_SEQ, 32), np.float32)
    cv_s = np.zeros((1, 8, 2, CONV_W), np.float32)
    for c in range(8):
        b, j = c // 4, c % 4
        r = R[c]
        oy = np.asarray(r["o_y"])
        for k, blk in enumerate(own[c]):
            y_prompt[b, 128 * blk:128 * blk + 128] = oy[128 * k:128 * k + 128]
        y_sample[c] = np.asarray(r["o_ys"])
        ckv_s[0, c] = np.asarray(r["o_ckv_s"])
        kpe_s[0, c] = np.asarray(r["o_kpe_s"])
        cv_s[0, c] = np.asarray(r["o_convs"]).transpose(2, 1, 0).reshape(2, CONV_W)
        if j == 0:
            ckv_p[0, b] = np.asarray(r["o_ckv"])
            kpe_p[0, b] = np.asarray(r["o_kpe"])
            cv_p[0, b] = np.asarray(r["o_convp"]).transpose(2, 1, 0).reshape(2, CONV_W)
    return (y_prompt, y_sample, ckv_p, kpe_p, cv_p, ckv_s, kpe_s, cv_s)
```

```python
import numpy as np
from contextlib import ExitStack
import concourse.bass as bass
import concourse.mybir as mybir
from concourse.bass_utils import run_bass_kernel_spmd

F32 = mybir.dt.float32
BF = mybir.dt.bfloat16
AF = mybir.ActivationFunctionType
ALU = mybir.AluOpType

D_MODEL = 1024
SEQ = 8192
N_HEADS = 8
QK_NOPE = 64
QK_ROPE = 32
KV_LORA = 128
Q_LORA = 256
CONV_W = 512
PAST = 1024
DEC_SEQ = 16
EPS = 1e-6
SM_SCALE = float((QK_NOPE + QK_ROPE) ** -0.5)
SB0 = 0
NCT = 38
TP = 256
TGP = 260
ENGS = ["sync", "tensor", "scalar", "vector", "gpsimd"]


class Op:
    __slots__ = ("eng", "fn", "deps", "signal", "val", "key", "is_dma", "group")

    def __init__(self, eng, fn, deps):
        self.eng, self.fn, self.deps = eng, fn, deps
        self.signal = False
        self.val = 0
        self.key = None
        self.is_dma = False
        self.group = False


class Res:
    __slots__ = ("writers", "readers", "excl")

    def __init__(self, excl=False):
        self.writers = []
        self.readers = []
        self.excl = excl


class Prog:
    def __init__(self):
        self.streams = {e: [] for e in ENGS}
        self.dma_cnt = {}
        self.final = []

    def _flat(self, deps):
        out = []
        for d in deps:
            if d is None:
                continue
            if isinstance(d, (list, tuple)):
                out.extend(self._flat(d))
            else:
                out.append(d)
        return out

    def op(self, eng, fn, deps=()):
        o = Op(eng, fn, self._flat(deps))
        for d in o.deps:
            d.signal = True
        self.streams[eng].append(o)
        return o

    def dma(self, eng, fn, key, deps=(), group=False):
        o = self.op(eng, fn, deps)
        o.is_dma = True
        o.key = key
        o.group = group
        n = self.dma_cnt.get(key, 0) + 1
        self.dma_cnt[key] = n
        o.val = 16 * n
        o.signal = True
        return o

    def I(self, eng, fn, reads=(), writes=(), deps=(), wadd=False, dma_key=None, group=False):
        d = list(deps)
        for r in reads:
            d += r.writers
            if r.excl:
                d += [x for x in r.readers if x.eng != eng]
        for w in writes:
            d += w.readers
            if not wadd:
                d += w.writers
        if dma_key is None:
            o = self.op(eng, fn, d)
        else:
            o = self.dma(eng, fn, dma_key, d, group)
        keep = (lambda lst: [x for x in lst if x.is_dma or x.eng != eng]) if dma_key is None else (lambda lst: list(lst))
        for r in reads:
            r.readers = keep(r.readers) + [o]
        for w in writes:
            if wadd:
                w.writers = keep(w.writers) + [o]
            else:
                w.writers = [o]
                w.readers = []
        return o

    def emit(self, nc, es):
        for eng in ENGS:
            cnt = 0
            for o in self.streams[eng]:
                if o.is_dma:
                    if o.group:
                        o.val = 16 * self.dma_cnt[o.key]
                elif o.signal:
                    cnt += 1
                    o.val = cnt
        sems = {}
        for eng in ENGS[1:]:
            sems[eng] = es.enter_context(nc.semaphore("s_" + eng))
        for key in self.dma_cnt:
            sems["d_" + key] = es.enter_context(nc.semaphore("d_" + key))
        block = es.enter_context(nc.Block())
        final = self.final

        def make(eng):
            ops = self.streams[eng]

            def body(e):
                waited = {}

                def wait_for(d):
                    name = ("d_" + d.key) if d.is_dma else d.eng
                    if waited.get(name, 0) >= d.val:
                        return
                    waited[name] = d.val
                    e.wait_ge(sems[name], d.val)

                for o in ops:
                    for d in o.deps:
                        wait_for(d)
                    ins = o.fn(e)
                    if o.is_dma:
                        ins.then_inc(sems["d_" + o.key], 16)
                    elif o.signal:
                        ins.then_inc(sems[eng], 1)
                if eng == "sync":
                    for d in final:
                        wait_for(d)
            return body

        block.sync(make("sync"))
        block.tensor(make("tensor"))
        block.scalar(make("scalar"))
        block.vector(make("vector"))
        block.gpsimd(make("gpsimd"))


def build_program(wseq_in=None):
    record = wseq_in is None
    nc = bass.Bass("TRN2", target_bir_lowering=False)
    P = Prog()
    I = P.I

    def din(name, shape, dt=F32):
        return nc.dram_tensor(name, list(shape), dt, kind="ExternalInput").ap()

    def dout(name, shape):
        return nc.dram_tensor(name, list(shape), F32, kind="ExternalOutput").ap()

    d_xT_seq = din("xT_seq", [D_MODEL, SEQ])
    d_xT_own = din("xT_own", [D_MODEL, 8 * TGP])
    d_x_own = din("x_own", [2048, D_MODEL])
    d_xT_smp = din("xT_smp", [D_MODEL, DEC_SEQ])
    d_x_smp = din("x_smp", [DEC_SEQ, D_MODEL])
    d_cache_kvT = din("cache_kvT", [128, PAST])
    d_cache_kv = din("cache_kv", [PAST, 128])
    d_cache_kpeT = din("cache_kpeT", [32, PAST])
    d_stateT = din("stateT", [128, 4, 2])
    d_mbias = din("mbias", [128, 128])
    d_w_in_t = din("w_in_t", [NCT, 128, 1024])
    d_w_kv = din("w_kv", [128, 8, 160])
    d_w_uqT = din("w_uqT_nope", [64, 8, 256])
    d_w_ukT = din("w_ukT", [64, 8, 128])
    d_w_uq_rope = din("w_uq_rope", [128, 2, 256])
    d_w_uv = din("w_uv", [128, 8, 64])
    d_w_o_mla = din("w_o_mla", [512, 1024])
    d_w_o_conv = din("w_o_conv", [512, 1024])
    d_w_out = din("w_out", [1024, 1024])
    d_pre_g = din("pre_g", [128, 8])
    d_q_g = din("q_g", [128, 2])
    d_kv_g = din("kv_g", [1, 128])
    d_post_g = din("post_g", [1, 1024])
    d_conv_w = din("conv_w_t", [128, 4, 3])
    d_cos_k = din("cos_k", [128, 64, 16])
    d_sin_k = din("sin_k", [128, 64, 16])
    d_cos_q = din("cos_q", [128, 16, 16])
    d_sin_q = din("sin_q", [128, 16, 16])
    d_cos_s = din("cos_s", [16, 16])
    d_sin_s = din("sin_s", [16, 16])
    d_ident = din("ident", [128, 128])

    d_scr = nc.dram_tensor("w_scr", [NCT, 128, 1024], BF, kind="Internal").ap()

    o_y = dout("o_y", [2048, D_MODEL])
    o_ys = dout("o_ys", [DEC_SEQ, D_MODEL])
    o_ckv = dout("o_ckv", [SEQ, 128])
    o_kpe = dout("o_kpe", [SEQ, 32])
    o_convp = dout("o_convp", [128, 4, 2])
    o_ckv_s = dout("o_ckv_s", [DEC_SEQ, 128])
    o_kpe_s = dout("o_kpe_s", [DEC_SEQ, 32])
    o_convs = dout("o_convs", [128, 4, 2])

    with ExitStack() as es:
        def sb(name, shape, dt):
            return es.enter_context(nc.sbuf_tensor(name, list(shape), dt))

        ckvT_all = sb("ckvT_all", [128, SEQ], BF)
        kpeT_all = sb("kpeT_all", [128, 2048], BF)
        V_all = sb("V_all", [128, 64, 130], BF)
        w_out_sb = sb("w_out_sb", [128, 8, 1024], BF)
        w_o_mla_sb = sb("w_o_mla_sb", [128, 4, 1024], BF)
        w_o_conv_sb = sb("w_o_conv_sb", [128, 4, 1024], BF)
        W_abs_sb = sb("W_abs_sb", [128, 2, 8, 128], BF)
        w_uq_rope_sb = sb("w_uq_rope_sb", [128, 2, 256], BF)
        w_uv_pad = sb("w_uv_pad", [128, 8, 128], BF)
        w_kv_sb = sb("w_kv_sb", [128, 8, 160], BF)
        scrA = sb("scrA", [128, 2048], F32)
        scrB = sb("scrB", [128, 1024], F32)
        w_uqT_sb = scrA[0:64, :].rearrange("p (h m) -> p h m", h=8)
        w_ukT_sb = scrB[0:64, :].rearrange("p (h c) -> p h c", h=8)
        post_g_bc = sb("post_g_bc", [128, 1024], F32)
        kv_g_bc = sb("kv_g_bc", [128, 128], F32)
        ident_bf = sb("ident_bf", [128, 128], BF)
        ones_bf = sb("ones_bf", [128, 128], BF)
        eps_t = sb("eps_t", [128, 1], F32)
        pre_g_sb = sb("pre_g_sb", [128, 8], F32)
        q_g_sb = sb("q_g_sb", [128, 2], F32)
        conv_w_sb = sb("conv_w_sb", [128, 4, 3], F32)
        ck_ring = [sb(f"ck{i}", [128, 2, 16], F32) for i in range(2)]
        sk_ring = [sb(f"sk{i}", [128, 2, 16], F32) for i in range(2)]
        cos_q = sb("cos_q_sb", [128, 16, 16], F32)
        sin_q = sb("sin_q_sb", [128, 16, 16], F32)
        cos_s = sb("cos_s_sb", [16, 16], F32)
        sin_s = sb("sin_s_sb", [16, 16], F32)
        stateT = sb("stateT_sb", [128, 4, 2], F32)
        convst = sb("convst", [128, 4, 2], F32)

        xT_ring = [sb(f"xT{i}", [128, 8, TGP], F32) for i in range(2)]
        xsq_b = [sb(f"xsq{i}", [128, 8, TGP], BF) for i in range(2)]
        lnt_b = [sb(f"lnt{i}", [128, TGP], F32) for i in range(2)]
        rstd_b = [sb(f"rstd{i}", [128, TGP], F32) for i in range(2)]
        hT_p = [sb(f"hTp{i}", [128, 8, TP], BF) for i in range(2)]
        hT_gs = [sb(f"hTg{i}", [128, 8, TGP], BF) for i in range(2)]
        WR = 6
        w_ring = [sb(f"wr{i}", [128, 1024], BF) for i in range(WR)]
        stg_ckv = [sb(f"stgc{i}", [128, 2, 128], F32) for i in range(2)]
        stg_kpe = [sb(f"stgk{i}", [128, 2, 32], F32) for i in range(2)]
        kpe_bf = [sb(f"kpebf{i}", [128, 128], BF) for i in range(2)]
        junkk_b = [sb(f"junkk{i}", [128, 128], BF) for i in range(2)]
        smallk_b = [sb(f"smallk{i}", [128, 8], F32) for i in range(2)]
        kropeA_b = [sb(f"kropeA{i}", [128, 32], F32) for i in range(2)]
        kropeB_b = [sb(f"kropeB{i}", [128, 32], F32) for i in range(2)]
        small = sb("small", [128, 16], F32)
        ropeA = scrB[:, 0:256]
        ropeB = scrB[:, 256:512]
        qlat = scrB[:, 512:1024].rearrange("p (k t) -> p k t", k=2)
        qsq = sb("qsq", [128, 2, 256], BF)
        lnq = sb("lnq", [128, 256], F32)
        rstdq = sb("rstdq", [128, 256], F32)
        qnT = sb("qnT", [128, 2, 256], BF)
        q_absT = sb("q_absT", [128, 8, 256], BF)
        siluG = sb("siluG", [128, 4, 256], BF)
        convres = sb("convres", [128, 4, 256], BF)
        ogT = sb("ogT", [128, 4, 256], BF)
        mergedT = sb("mergedT", [128, 8, 256], BF)
        cc_sb = [sb(f"ccsb{i}", [128, TGP], F32) for i in range(1)] * 2
        uext = [sb(f"uext{i}", [128, 2, 130], F32) for i in range(2)]
        cvt = [sb(f"cvt{i}", [128, 256], F32) for i in range(1)] * 2
        sgc = [sb(f"sgc{i}", [128, 256], F32) for i in range(2)]
        sga = [sb(f"sga{i}", [128, 256], F32) for i in range(1)] * 2
        sgb = [sb(f"sgb{i}", [128, 256], F32) for i in range(1)] * 2
        t1 = [sb(f"t1_{i}", [128, 256], F32) for i in range(1)] * 2
        t2 = [sb(f"t2_{i}", [128, 256], F32) for i in range(1)] * 2
        orep = sb("orep", [128, 8, 4, 32], BF)
        q_peT = [sb(f"qpeT{i}", [128, 8, 128], BF) for i in range(2)]
        mbias_sb = sb("mbias_sb", [128, 128], F32)
        qpm = [sb(f"qpm{i}", [128, 8, 128], BF) for i in range(2)]
        maskr = sb("maskr", [128, 4], F32)
        PT = [sb(f"PT{i}", [128, 8, 128], BF) for i in range(3)]
        rden = sb("rden", [128, 8], F32)
        olat = sb("olat", [128, 8, 128], BF)
        olatT = sb("olatT", [128, 8, 128], BF)
        xtok = scrA[:, 0:1024]
        yt = scrA[:, 1024:2048]

        g01 = es.enter_context(nc.psum_tensor("g01", [128, 2, 512], F32))
        g23 = es.enter_context(nc.psum_tensor("g23", [128, 2, 512], F32))
        o456 = es.enter_context(nc.psum_tensor("o456", [128, 3, 512], F32))
        m7 = es.enter_context(nc.psum_tensor("m7", [128, 512], F32))
        gbank_t = [g01, g01, g23, g23]
        G_res = [Res(True) for _ in range(4)]
        O_res = [Res(True) for _ in range(3)]
        M_res = Res(True)
        gctr = {"s": 0, "p": 0}

        def g_single():
            b = gctr["s"] % 3
            gctr["s"] += 1
            assert not (G_res[b].writers and not G_res[b].readers), "PSUM bank handed out while still live"
            return gbank_t[b][:, b % 2, :], G_res[b]

        def g_pair():
            p = gctr["p"] % 2
            gctr["p"] += 1
            return (g01, (G_res[0], G_res[1])) if p == 0 else (g23, (G_res[2], G_res[3]))

        def k_bank(sel=3):
            if sel == 7:
                return m7[:, :], M_res
            return g23[:, sel - 2, :], G_res[sel]

        R = {}

        def res(name):
            if name not in R:
                R[name] = Res()
            return R[name]

        def MM(out, lhsT, rhs, start, stop, reads=(), writes=(), wadd=False, tp=None, sgcheck=False, deps=()):
            def fn(e):
                kw = {}
                if tp is not None:
                    kw["tile_position"] = tp
                if sgcheck:
                    kw["skip_group_check"] = True
                return e.matmul(out, lhsT=lhsT, rhs=rhs, start=start, stop=stop, **kw)
            return I("tensor", fn, reads, writes, deps, wadd)

        def TR(out, in_, n, reads=(), writes=(), wadd=False, deps=()):
            return I("tensor", lambda e: e.transpose(out, in_, ident_bf[:n, :n]), reads, writes, deps, wadd)

        def ACT(out, in_, func, reads=(), writes=(), wadd=False, scale=None, bias=None, accum=None, deps=()):
            def fn(e):
                kw = {}
                if scale is not None:
                    kw["scale"] = scale
                if bias is not None:
                    kw["bias"] = bias
                if accum is not None:
                    kw["accum_out"] = accum
                return e.activation(out=out, in_=in_, func=func, **kw)
            return I("scalar", fn, reads, writes, deps, wadd)

        def TT(eng, out, in0, in1, op, reads=(), writes=(), wadd=False, deps=()):
            return I(eng, lambda e: e.tensor_tensor(out=out, in0=in0, in1=in1, op=op), reads, writes, deps, wadd)

        def STT(out, in0, scalar, in1, op0, op1, reads=(), writes=(), wadd=False, deps=()):
            return I("vector", lambda e: e.scalar_tensor_tensor(out=out, in0=in0, scalar=scalar, in1=in1, op0=op0, op1=op1),
                     reads, writes, deps, wadd)

        def TS(eng, out, in0, s1, op0, reads=(), writes=(), wadd=False, deps=()):
            return I(eng, lambda e: e.tensor_scalar(out=out, in0=in0, scalar1=s1, scalar2=None, op0=op0),
                     reads, writes, deps, wadd)

        def CP(eng, out, in_, reads=(), writes=(), wadd=False, deps=()):
            if eng == "scalar":
                return ACT(out, in_, AF.Copy, reads, writes, wadd, deps=deps)
            return I(eng, lambda e: e.tensor_copy(out=out, in_=in_), reads, writes, deps, wadd)

        def DMA(eng, out, in_, key, reads=(), writes=(), deps=(), group=False, wadd=False):
            return I(eng, lambda e: e.dma_start(out=out, in_=in_), reads, writes, deps, wadd, dma_key=key, group=group)

        def MEMSET(out, val, writes=(), wadd=False, deps=()):
            return I("gpsimd", lambda e: e.memset(out, val), (), writes, deps, wadd)

        c_res = res("consts")
        MEMSET(eps_t[:], EPS, [c_res])
        MEMSET(ones_bf[:], 1.0, [c_res], wadd=True)
        MEMSET(V_all[:, :, 128:130], 1.0, [c_res], wadd=True)
        ms_uv = MEMSET(w_uv_pad[:], 0.0, [c_res], wadd=True)
        ms_kpe = MEMSET(kpeT_all[:], 0.0, [c_res], wadd=True)
        for i in range(2):
            MEMSET(qpm[i][:], 0.0, [c_res], wadd=True)
        ms_mr = MEMSET(maskr[:], 0.0, [c_res], wadd=True)
        for r_ in range(4):
            MEMSET(maskr[32 * r_:32 * r_ + 32, r_:r_ + 1], 1.0, [c_res], wadd=True, deps=[ms_mr])
        for i in range(2):
            MEMSET(kpe_bf[i][:], 0.0, [c_res], wadd=True)

        gq = "gpsimd"
        iw = dict(key="initw", group=True, writes=[c_res], wadd=True)
        DMA(gq, ident_bf[:], d_ident, **iw)
        DMA(gq, w_kv_sb[:], d_w_kv, **iw)
        DMA(gq, w_uq_rope_sb[:], d_w_uq_rope, **iw)
        uvp = w_uv_pad[:].rearrange("p (j two) c -> p j two c", two=2)
        uvd = d_w_uv.rearrange("p (j two) c -> p j two c", two=2)
        DMA(gq, uvp[:, :, 0, 0:64], uvd[:, :, 0, :], deps=[ms_uv], **iw)
        DMA(gq, uvp[:, :, 1, 64:128], uvd[:, :, 1, :], deps=[ms_uv], **iw)
        wprep = []
        def emit_wprep(deps=(), lo=0, hi=NCT // 2):
            for i in range(lo, hi):
                wprep.append(P.dma(gq, (lambda i: (lambda e: e.dma_start(
                    out=d_scr[2 * i:2 * i + 2].rearrange("a p n -> (a p) n"),
                    in_=d_w_in_t[2 * i:2 * i + 2].rearrange("a p n -> (a p) n"))))(i), f"wp{i}", list(deps)))

        cw2_res = res("consts2")

        def emit_initw2(deps=()):
            iw2 = dict(key="initw2", group=True, writes=[cw2_res], wadd=True, deps=list(deps))
            DMA(gq, w_o_conv_sb[:], d_w_o_conv.rearrange("(k p) n -> p k n", p=128), **iw2)
            DMA(gq, w_o_mla_sb[:], d_w_o_mla.rearrange("(k p) n -> p k n", p=128), **iw2)
            DMA(gq, w_out_sb[:], d_w_out.rearrange("(k p) n -> p k n", p=128), **iw2)


        cw3_res = res("consts3")

        def emit_cache(deps):
            iw3 = dict(key="initw3", group=True, writes=[cw3_res], wadd=True, deps=deps)
            DMA(gq, ckvT_all[:, 0:PAST], d_cache_kvT, **iw3)
            DMA(gq, kpeT_all[0:32, 0:PAST], d_cache_kpeT, **iw3)
            DMA(gq, V_all[:, 0:8, 0:128], d_cache_kv.rearrange("(b p) c -> p b c", p=128), **iw3)

        ic = dict(key="initc", group=True, writes=[c_res], wadd=True)
        DMA("sync", pre_g_sb[:], d_pre_g, **ic)
        DMA("sync", q_g_sb[:], d_q_g, **ic)
        DMA("sync", conv_w_sb[:], d_conv_w, **ic)
        DMA("sync", kv_g_bc[:], d_kv_g[0:1, :].broadcast_to([128, 128]), **ic)
        DMA("sync", post_g_bc[:], d_post_g[0:1, :].broadcast_to([128, 1024]), **ic)
        DMA("sync", mbias_sb[:], d_mbias, **ic)
        DMA("sync", cos_s[:], d_cos_s, **ic)
        DMA("sync", sin_s[:], d_sin_s, **ic)
        DMA("sync", stateT[:], d_stateT, **ic)
        DMA("sync", w_uqT_sb, d_w_uqT, **ic)
        DMA("sync", w_ukT_sb, d_w_ukT, **ic)
        DMA("sync", cos_q[:], d_cos_q, **ic)
        DMA("sync", sin_q[:], d_sin_q, **ic)

        wabs_mm = []
        for kc in range(2):
            for hq in range(2):
                bank, br = g_single()
                for hh in range(4):
                    h = hq * 4 + hh
                    wabs_mm.append(MM(bank[:, hh * 128:(hh + 1) * 128], w_uqT_sb[0:64, h, kc * 128:(kc + 1) * 128],
                                      w_ukT_sb[0:64, h, :], True, True, reads=[c_res], writes=[br], wadd=(hh > 0)))
                CP("vector", W_abs_sb[:, kc, hq * 4:hq * 4 + 4, :],
                   bank.rearrange("p (h c) -> p h c", h=4), reads=[br], writes=[c_res], wadd=True)

        wq = {"issued": 0, "used": 0, "total": 0}
        w_res = [Res() for _ in range(WR)]

        def w_issue():
            if record:
                return
            i = wq["issued"]
            if i >= wq["total"]:
                return
            ct = wseq[i]
            s = i % WR
            DMA("sync", w_ring[s][:], d_scr[ct], key=f"w{s}", writes=[w_res[s]], deps=[wprep[ct // 2]])
            wq["issued"] += 1

        def w_take(ct):
            i = wq["used"]
            if record:
                wseq.append(ct)
            else:
                assert wseq[i] == ct, (i, wseq[i], ct)
            wq["used"] += 1
            s = i % WR
            return w_ring[s], w_res[s]

        wseq = [] if record else list(wseq_in)
        wq["total"] = len(wseq)

        xctr = {"i": 0}
        last_x = {"op": None}
        xT_res = [Res(), Res()]

        def norm_tile(src, T, hT, hT_r, kbsel=3, spaced=False):
            s = xctr["i"] % 2
            xctr["i"] += 1
            xt, xr = xT_ring[s], xT_res[s]
            xsq, lnt, rstd = xsq_b[s], lnt_b[s], rstd_b[s]
            xsq_r, lnt_r, rstd_r = res(f"xsq{s}"), res(f"lnt{s}"), res(f"rstd{s}")
            last_x["op"] = DMA("sync", xt[:, :, 0:T], src, key=f"xT{s}", writes=[xr])
            TT("vector", xsq[:, :, 0:T], xt[:, :, 0:T], xt[:, :, 0:T], ALU.mult, reads=[xr], writes=[xsq_r])
            if spaced:
                for _ in range(12 if spaced is True else int(spaced)):
                    yield
            bank, br = k_bank(kbsel)
            for k in range(8):
                MM(bank[:, 0:T], ones_bf[:, :], xsq[:, k, 0:T], k == 0, k == 7,
                   reads=[xsq_r, c_res], writes=[br], wadd=(k > 0))
            ACT(lnt[:, 0:T], bank[:, 0:T], AF.Ln, reads=[br], writes=[lnt_r], scale=1.0 / D_MODEL, bias=eps_t[:, 0:1])
            ACT(rstd[:, 0:T], lnt[:, 0:T], AF.Exp, reads=[lnt_r], writes=[rstd_r], scale=-0.5)
            for k in range(8):
                STT(hT[:, k, 0:T], xt[:, k, 0:T], pre_g_sb[:, k:k + 1], rstd[:, 0:T], ALU.mult, ALU.mult,
                    reads=[xr, rstd_r, c_res], writes=[hT_r], wadd=(k > 0))
            yield
            if spaced is True:
                for _ in range(12):
                    yield

        kv_ready = {}
        kctr = {"i": 0}
        kpcol_res = [Res() for _ in range(16)]
        small_r = res("small")
        ropeA_r, ropeB_r = res("ropeA"), res("ropeB")
        for nm in ("ropeA", "ropeB", "qlat", "xtok", "yt"):
            res(nm).readers = list(wabs_mm[-1:])

        def key_tile(hT, hT_r, blocks, cos_of, sin_of, out_ckv, out_kpe, extra_deps=(), kbsel=3, spaced=False, tab_res=()):
            s = kctr["i"] % 2
            kctr["i"] += 1
            stc, stk, kb = stg_ckv[s], stg_kpe[s], kpe_bf[s]
            stc_r, stk_r, kb_r = res(f"stgc{s}"), res(f"stgk{s}"), res(f"kpebf{s}")
            junkk, smallk, kropeA, kropeB = junkk_b[s], smallk_b[s], kropeA_b[s], kropeB_b[s]
            smallk_r, junkk_r = res(f"smallk{s}"), res(f"junkk{s}")
            kropeA_r, kropeB_r = res(f"kropeA{s}"), res(f"kropeB{s}")
            bank, br = k_bank(kbsel)
            for bi, (off, n, blk) in enumerate(blocks):
                for k in range(8):
                    MM(bank[:n, bi * 160:(bi + 1) * 160], hT[:, k, off:off + n], w_kv_sb[:, k, :], k == 0, k == 7,
                       reads=[hT_r, c_res], writes=[br], wadd=(bi > 0 or k > 0))
            yield
            tb = bank[:, 320:448].bitcast(BF)
            tr = br
            for bi, (off, n, blk) in enumerate(blocks):
                r = blk // 16
                kvr = Res()
                kv_ready[blk] = kvr
                kvp = bank[:n, bi * 160:bi * 160 + 128]
                rp = bank[:n, bi * 160 + 128:bi * 160 + 160].rearrange("p (a b) -> p a b", a=2)
                ACT(junkk[:n, 0:128], kvp, AF.Square, reads=[br], writes=[junkk_r, smallk_r], accum=smallk[:n, 0:1])
                ACT(smallk[:n, 1:2], smallk[:n, 0:1], AF.Ln, reads=[smallk_r], writes=[smallk_r],
                    scale=1.0 / KV_LORA, bias=eps_t[:n, 0:1])
                ACT(smallk[:n, 2:3], smallk[:n, 1:2], AF.Exp, reads=[smallk_r], writes=[smallk_r], scale=-0.5)
                cb = cos_of(blk, n).unsqueeze(1).broadcast_to([n, 2, 16])
                sbb = sin_of(blk, n).unsqueeze(1).broadcast_to([n, 2, 16])
                A3 = kropeA[:n, 0:32].rearrange("p (a b) -> p a b", a=2)
                B3 = kropeB[:n, 0:32].rearrange("p (a b) -> p a b", a=2)
                TT("vector", A3, rp, cb, ALU.mult, reads=[br, c_res] + list(tab_res), writes=[kropeA_r])
                TT("vector", B3, rp, sbb, ALU.mult, reads=[br, c_res] + list(tab_res), writes=[kropeB_r])
                TT("vector", stk[:n, bi, 0:16], kropeA[:n, 0:16], kropeB[:n, 16:32], ALU.subtract,
                   reads=[kropeA_r, kropeB_r], writes=[stk_r], wadd=(bi > 0))
                TT("vector", stk[:n, bi, 16:32], kropeA[:n, 16:32], kropeB[:n, 0:16], ALU.add,
                   reads=[kropeA_r, kropeB_r], writes=[stk_r], wadd=True)
                CP("gpsimd", kb[:n, 32 * r:32 * r + 32], stk[:n, bi, :], reads=[stk_r], writes=[kb_r])
                STT(stc[:n, bi, :], kvp, smallk[:n, 2:3], kv_g_bc[:n, :], ALU.mult, ALU.mult,
                    reads=[br, smallk_r, c_res], writes=[stc_r], wadd=(bi > 0))
                CP("gpsimd", V_all[:n, blk, 0:128], stc[:n, bi, :], reads=[stc_r], writes=[kvr], deps=extra_deps)
                yield
                if spaced:
                    for _ in range(9):
                        yield
                TR(tb[:, 0:n], V_all[:n, blk, 0:128], n, reads=[kvr, c_res], writes=[tr], wadd=True)
                TR(tb[:, 128:128 + n], kb[:n, :], n, reads=[kb_r, c_res], writes=[tr], wadd=True)
                yield
                CP("vector", ckvT_all[:, blk * 128:blk * 128 + n], tb[:, 0:n],
                   reads=[tr], writes=[kvr], wadd=True, deps=extra_deps)
                c0 = (blk % 16) * 128
                CP("vector", kpeT_all[32 * r:32 * r + 32, c0:c0 + n], tb[32 * r:32 * r + 32, 128:128 + n],
                   reads=[tr], writes=[kvr, kpcol_res[blk % 16]], wadd=True, deps=extra_deps)
            nb = len(blocks)
            n0 = blocks[0][1]
            outs = []
            outs.append(DMA("sync", out_ckv, stc[:n0, 0:nb, :] if nb > 1 else stc[:n0, 0, :], key=f"stc{s}", reads=[stc_r]))
            outs.append(DMA("sync", out_kpe, stk[:n0, 0:nb, :] if nb > 1 else stk[:n0, 0, :], key=f"stk{s}", reads=[stk_r]))
            P.final.extend(outs)
            yield

        BG = []
        bgc = {"left": 0}

        def run(gen):
            for _ in gen:
                pass

        def bg_add(gen, nchunks):
            BG.append(gen)
            bgc["left"] += nchunks

        def bg_step(n):
            while n > 0 and BG:
                k_ = bgc.get("rr", 0) % min(2, len(BG))
                bgc["rr"] = bgc.get("rr", 0) + 1
                try:
                    next(BG[k_])
                    n -= 1
                    bgc["left"] = max(bgc["left"] - 1, 0)
                except StopIteration:
                    BG.pop(k_)

        def bg_drain():
            while BG:
                bg_step(1000)
            bgc["left"] = 0

        qlat_r, qsq_r, lnq_r, rstdq_r, qnT_r = res("qlat"), res("qsq"), res("lnq"), res("rstdq"), res("qnT")
        qabs_r, siluG_r, convres_r, ogT_r, merged_r = res("qabs"), res("siluG"), res("convres"), res("ogT"), res("merged")
        hTg_rs = [res("hTg0"), res("hTg1")]
        orep_r, rden_r, olat_r, olatT_r, xtok_r, yt_r = res("orep"), res("rden"), res("olat"), res("olatT"), res("xtok"), res("yt")
        convst_r = res("convst")
        ring2 = {}

        def nxt(name):
            i = ring2.get(name, 0)
            ring2[name] = i + 1
            return i % 2

        ptctr = {"i": 0}
        PT_res = [Res() for _ in range(3)]
        qpeT_res = [Res(), Res()]
        qpm_res = [Res(), Res()]
        qpmctr = {"i": 0}
        qpm_state = [None, None]
        attn_last = {}

        def inproj_ct(ct, N, hT_g, hTg_r):
            wt, wr = w_take(ct)
            bank, br = g_single()
            for k in range(8):
                MM(bank[:, 0:N], wt[:, k * 128:(k + 1) * 128], hT_g[:, k, 0:N], k == 0, k == 7,
                   reads=[wr, hTg_r], writes=[br], wadd=(k > 0))
            w_issue()
            if BG and bgc.get("inproj", False):
                bg_step(1)
            return bank, br

        def part1_gen(T, n_halo, qblocks, conv_state_src, conv_out, name, hT_g, hTg_r, xsrc, spaced_norm=False):
            yield from norm_tile(xsrc, T + n_halo, hT_g, hTg_r, spaced=spaced_norm)
            TA = T + n_halo
            nqb = len(qblocks)
            nq = qblocks[0]["nq"]
            for kc in range(2):
                bank, br = inproj_ct(kc, T, hT_g, hTg_r)
                CP("vector", qlat[:, kc, 0:T], bank[:, 0:T], reads=[br], writes=[qlat_r], wadd=(kc > 0))
                ACT(qsq[:, kc, 0:T], bank[:, 0:T], AF.Square, reads=[br], writes=[qsq_r], wadd=(kc > 0))
                yield
            bank, br = g_single()
            for kc in range(2):
                MM(bank[:, 0:T], ones_bf[:, :], qsq[:, kc, 0:T], kc == 0, kc == 1, reads=[qsq_r, c_res], writes=[br], wadd=(kc > 0))
            ACT(lnq[:, 0:T], bank[:, 0:T], AF.Ln, reads=[br], writes=[lnq_r], scale=1.0 / Q_LORA, bias=eps_t[:, 0:1])
            ACT(rstdq[:, 0:T], lnq[:, 0:T], AF.Exp, reads=[lnq_r], writes=[rstdq_r], scale=-0.5)
            for kc in range(2):
                STT(qnT[:, kc, 0:T], qlat[:, kc, 0:T], q_g_sb[:, kc:kc + 1], rstdq[:, 0:T], ALU.mult, ALU.mult,
                    reads=[qlat_r, rstdq_r, c_res], writes=[qnT_r], wadd=(kc > 0))
            yield
            for h in range(8):
                bank, br = g_single()
                for kc in range(2):
                    MM(bank[:, 0:T], W_abs_sb[:, kc, h, :], qnT[:, kc, 0:T], kc == 0, kc == 1,
                       reads=[qnT_r, c_res], writes=[br], wadd=(kc > 0))
                CP("scalar" if h % 2 == 0 else "vector", q_absT[:, h, 0:T], bank[:, 0:T], reads=[br], writes=[qabs_r], wadd=(h > 0))
                if h == 3:
                    yield
            KSUB = 99
            for qi, qb in enumerate(qblocks):
                qoff = qb["off"]
                for kc in range(2):
                    MM(m7[:nq, 0:256], qnT[:, kc, qoff:qoff + nq], w_uq_rope_sb[:, kc, :], kc == 0, kc == 1,
                       reads=[qnT_r, c_res], writes=[M_res], wadd=(kc > 0))
                q4 = lambda ap: ap.rearrange("p (h a b) -> p h a b", h=8, a=2)
                cb = qb["cos"].unsqueeze(1).unsqueeze(1).broadcast_to([nq, 8, 2, 16])
                sbb = qb["sin"].unsqueeze(1).unsqueeze(1).broadcast_to([nq, 8, 2, 16])
                TT("vector", q4(ropeA[:nq, :]), q4(m7[:nq, 0:256]), cb, ALU.mult, reads=[M_res, c_res], writes=[ropeA_r])
                TT("vector", q4(ropeB[:nq, :]), q4(m7[:nq, 0:256]), sbb, ALU.mult, reads=[M_res, c_res], writes=[ropeB_r])
                TT("vector", orep[:nq, :, 0, 0:16], q4(ropeA[:nq, :])[:, :, 0, :], q4(ropeB[:nq, :])[:, :, 1, :], ALU.subtract,
                   reads=[ropeA_r, ropeB_r], writes=[orep_r])
                TT("vector", orep[:nq, :, 0, 16:32], q4(ropeA[:nq, :])[:, :, 1, :], q4(ropeB[:nq, :])[:, :, 0, :], ALU.add,
                   reads=[ropeA_r, ropeB_r], writes=[orep_r], wadd=True)
                CP("vector", orep[:nq, :, 1:4, :], orep[:nq, :, 0:1, :].broadcast_to([nq, 8, 3, 32]),
                   reads=[orep_r], writes=[orep_r])
                tb_f, tr = g_single()
                tb = tb_f.bitcast(BF)
                for h in range(8):
                    TR(tb[:, h * 128:h * 128 + nq], orep[:nq, h, :, :].rearrange("p a b -> p (a b)"), nq,
                       reads=[orep_r, c_res], writes=[tr], wadd=(h > 0))
                CP("scalar", q_peT[qi][:, :, 0:nq], tb.rearrange("p (h q) -> p h q", h=8)[:, :, 0:nq],
                   reads=[tr], writes=[qpeT_res[qi]])
                yield
            yield
            for i in range(4):
                bank, br = inproj_ct(2 + i, T, hT_g, hTg_r)
                ACT(siluG[:, i, 0:T], bank[:, 0:T], AF.Silu, reads=[br], writes=[siluG_r], wadd=(i > 0))
                if i % 2 == 1:
                    yield
            yield
            for jc in range(4):
                if jc > 0:
                    yield
                s = nxt("conv")
                cc, cc_r = cc_sb[s], res("cc0")
                ue, ue_r = uext[s], res(f"ue{s}")
                cv, cv_r = cvt[s], res("cv0")
                sg, sg_r = sgc[s], res(f"sgc{s}")
                v3 = lambda ap: ap.rearrange("p (q t) -> p q t", q=nqb)
                b_cc, r_cc = inproj_ct(6 + 4 * jc, TA, hT_g, hTg_r)
                CP("scalar", cc[:, 0:TA], b_cc[:, 0:TA], reads=[r_cc], writes=[cc_r])
                b_cx, r_cx = inproj_ct(7 + 4 * jc, TA, hT_g, hTg_r)
                TT("vector", ue[:, 0:nqb, 2:2 + nq], v3(b_cx[:, 0:T]), v3(cc[:, 0:T]), ALU.mult,
                   reads=[r_cx, cc_r], writes=[ue_r])
                if n_halo:
                    TT("vector", ue[:, 0:nqb, 0:2], v3(b_cx[:, T:TA]), v3(cc[:, T:TA]), ALU.mult,
                       reads=[r_cx, cc_r], writes=[ue_r], wadd=True)
                else:
                    CP("gpsimd", ue[:, 0, 0:2], conv_state_src[:, jc, :], reads=[c_res], writes=[ue_r], wadd=True)
                b_gc, r_gc = inproj_ct(8 + 4 * jc, T, hT_g, hTg_r)
                ACT(sg[:, 0:T], b_gc[:, 0:T], AF.Silu, reads=[r_gc], writes=[sg_r])
                cv3 = cv[:, 0:T].rearrange("p (q t) -> p q t", q=nqb)
                TS("vector", cv3, ue[:, 0:nqb, 0:nq], conv_w_sb[:, jc, 0:1], ALU.mult, reads=[ue_r, c_res], writes=[cv_r])
                STT(cv3, ue[:, 0:nqb, 1:1 + nq], conv_w_sb[:, jc, 1:2], cv3, ALU.mult, ALU.add, reads=[ue_r, cv_r], writes=[cv_r])
                STT(cv3, ue[:, 0:nqb, 2:2 + nq], conv_w_sb[:, jc, 2:3], cv3, ALU.mult, ALU.add, reads=[ue_r, cv_r], writes=[cv_r])
                b_cb, r_cb = inproj_ct(9 + 4 * jc, T, hT_g, hTg_r)
                TT("vector", cv[:, 0:T], cv[:, 0:T], b_cb[:, 0:T], ALU.mult, reads=[cv_r, r_cb], writes=[cv_r])
                TT("vector", convres[:, jc, 0:T], cv[:, 0:T], sg[:, 0:T], ALU.mult, reads=[cv_r, sg_r], writes=[convres_r], wadd=(jc > 0))
                if conv_out is not None:
                    CP("gpsimd", convst[:, jc, :], ue[:, nqb - 1, nq:nq + 2], reads=[ue_r], writes=[convst_r], wadd=(jc > 0))
            if conv_out is not None:
                P.final.append(DMA("sync", conv_out, convst[:], key="stconv", reads=[convst_r]))


        def attention(qblocks, name):
            nq = qblocks[0]["nq"]
            slots_left = {"n": sum(len(q_["slots"]) + 1 for q_ in qblocks)}

            def post_A(qi):
                for ob, hs in enumerate([(0, 3), (3, 6), (6, 8)]):
                    nh = hs[1] - hs[0]
                    ov = o456[:nq, ob, 0:nh * 129].rearrange("p (h c) -> p h c", c=129)
                    I("vector", (lambda ov=ov, hs=hs: (lambda e: e.reciprocal(rden[:nq, hs[0]:hs[1]].unsqueeze(2), ov[:, :, 128:129])))(),
                      reads=[O_res[ob]], writes=[rden_r], wadd=(ob > 0))
                    TT("vector", olat[:nq, hs[0]:hs[1], :], ov[:, :, 0:128],
                       rden[:nq, hs[0]:hs[1]].unsqueeze(2).broadcast_to([nq, nh, 128]), ALU.mult,
                       reads=[O_res[ob], rden_r], writes=[olat_r], wadd=(ob > 0))

            def post_B(qi):
                qoff = qblocks[qi]["off"]
                tb_f, tr = g_single()
                tb = tb_f.bitcast(BF)
                for h in range(8):
                    TR(tb[:, h * 128:h * 128 + nq], olat[:nq, h, :], nq, reads=[olat_r, c_res], writes=[tr], wadd=(h > 0))
                CP("scalar", olatT[:, :, 0:nq], tb.rearrange("p (h q) -> p h q", h=8)[:, :, 0:nq], reads=[tr], writes=[olatT_r])
                ob_, or_ = g_single()
                for j in range(4):
                    MM(ob_[:, j * 128:j * 128 + nq], w_uv_pad[:, 2 * j, :], olatT[:, 2 * j, 0:nq], True, False,
                       reads=[olatT_r, c_res], writes=[or_], wadd=(j > 0))
                    MM(ob_[:, j * 128:j * 128 + nq], w_uv_pad[:, 2 * j + 1, :], olatT[:, 2 * j + 1, 0:nq], False, True,
                       reads=[olatT_r, c_res], writes=[or_], wadd=True)
                TT("vector", ogT[:, :, qoff:qoff + nq], ob_.rearrange("p (j q) -> p j q", j=4)[:, :, 0:nq],
                   siluG[:, :, qoff:qoff + nq], ALU.mult, reads=[or_, siluG_r], writes=[ogT_r], wadd=(qi > 0))

            pending_post = []
            for qi, qb in enumerate(qblocks):
                qoff = qb["off"]
                qp, qp_r = q_peT[qi], qpeT_res[qi]
                slots = qb["slots"]
                ns = len(slots)
                last_pv = None
                sl_state = {}

                def get_qvar(r):
                    key = (name, qi, r)
                    for i in range(2):
                        if qpm_state[i] is not None and qpm_state[i][0] == key:
                            return qpm[i], qpm_res[i]
                    i = qpmctr["i"] % 2
                    qpmctr["i"] += 1
                    if qpm_state[i] is not None and qpm_state[i][1] != r:
                        ro = qpm_state[i][1]
                        I("vector", (lambda i=i, ro=ro: (lambda e: e.memset(qpm[i][32 * ro:32 * ro + 32, :, :], 0.0)))(),
                          (), [qpm_res[i]])
                        CP("vector", qpm[i][32 * r:32 * r + 32, :, 0:nq], qp[32 * r:32 * r + 32, :, 0:nq],
                           reads=[qp_r], writes=[qpm_res[i]], wadd=True)
                    else:
                        CP("vector", qpm[i][32 * r:32 * r + 32, :, 0:nq], qp[32 * r:32 * r + 32, :, 0:nq],
                           reads=[qp_r, c_res], writes=[qpm_res[i]])
                    qpm_state[i] = (key, r)
                    return qpm[i], qpm_res[i]

                get_qvar(slots[0][0] // 16)

                def emit_S(si):
                    blk, nk, midx = slots[si]
                    r = blk // 16
                    kcol = blk * 128
                    kpcol = (blk % 16) * 128
                    kvr = kv_ready[blk]
                    pt_i = ptctr["i"] % 3
                    ptctr["i"] += 1
                    pt, pt_r = PT[pt_i], PT_res[pt_i]
                    qm, qm_r = get_qvar(r)
                    first_w = True
                    for hh in range(2):
                        sbank, brr = g_single()
                        S = sbank[:nk, 0:4 * nq].rearrange("p (h q) -> p h q", h=4)
                        MM(S, ckvT_all[:, kcol:kcol + nk], q_absT[:, 4 * hh:4 * hh + 4, qoff:qoff + nq], True, False,
                           reads=[kvr, qabs_r], writes=[brr])
                        MM(S, kpeT_all[:, kpcol:kpcol + nk], qm[:, 4 * hh:4 * hh + 4, 0:nq],
                           False, True, reads=[kvr, qm_r, c_res, kpcol_res[blk % 16]], writes=[brr], wadd=True)
                        if midx is None:
                            ACT(pt[:nk, 4 * hh:4 * hh + 4, 0:nq], S, AF.Exp, reads=[brr], writes=[pt_r], wadd=not first_w, scale=SM_SCALE)
                            first_w = False
                        else:
                            for hq in range(2):
                                col = qb["kidx"] * 8 + midx * 2 + hq
                                ACT(pt[:nk, 4 * hh:4 * hh + 4, 64 * hq:64 * hq + 64], S[:, :, 64 * hq:64 * hq + 64], AF.Exp,
                                    reads=[brr, c_res], writes=[pt_r], wadd=not first_w, scale=SM_SCALE,
                                    bias=mbias_sb[:nk, col:col + 1])
                                first_w = False
                    sl_state[si] = (pt, pt_r, kvr)

                def emit_PV(si):
                    blk, nk, midx = slots[si]
                    pt, pt_r, kvr = sl_state.pop(si)
                    lp = None
                    for h in range(8):
                        ob, oc = h // 3, h % 3
                        first = (si == 0 and oc == 0)
                        lp = MM(o456[:nq, ob, oc * 129:oc * 129 + 129], pt[:nk, h, 0:nq], V_all[:nk, blk, 0:129],
                                first, si == ns - 1, reads=[pt_r, kvr], writes=[O_res[ob]],
                                wadd=not first, sgcheck=True)
                    return lp

                for si in range(ns + 1):
                    if si + 2 < ns:
                        get_qvar(slots[si + 2][0] // 16)
                    if si < ns:
                        emit_S(si)
                    if si >= 1:
                        last_pv = emit_PV(si - 1)
                    if si == 2 and pending_post:
                        post_B(pending_post.pop(0))
                    if BG:
                        sleft = max(slots_left["n"], 1)
                        bg_step(-(-bgc["left"] // sleft))
                    slots_left["n"] -= 1
                attn_last[name] = last_pv
                while pending_post:
                    post_B(pending_post.pop(0))
                post_A(qi)
                pending_post.append(qi)
            while pending_post:
                post_B(pending_post.pop(0))


        def part3_gen(T, qblocks, hT_g, hTg_r):
            nq = qblocks[0]["nq"]
            for oc in range(8):
                if oc > 0:
                    yield
                s = nxt("merge")
                b_ma, r_ma = inproj_ct(22 + 2 * oc, T, hT_g, hTg_r)
                ACT(sga[s][:, 0:T], b_ma[:, 0:T], AF.Sigmoid, reads=[r_ma], writes=[res("sga0")])
                b_mb, r_mb = inproj_ct(23 + 2 * oc, T, hT_g, hTg_r)
                ACT(sgb[s][:, 0:T], b_mb[:, 0:T], AF.Sigmoid, reads=[r_mb], writes=[res("sgb0")])
                b_a, r_a = g_single()
                for kc in range(4):
                    MM(b_a[:, 0:T], w_o_mla_sb[:, kc, oc * 128:(oc + 1) * 128], ogT[:, kc, 0:T], kc == 0, kc == 3,
                       reads=[ogT_r, cw2_res], writes=[r_a], wadd=(kc > 0))
                TT("vector", t1[s][:, 0:T], b_a[:, 0:T], sga[s][:, 0:T], ALU.mult, reads=[r_a, res("sga0")], writes=[res("t1_0")])
                b_b, r_b = g_single()
                for kc in range(4):
                    MM(b_b[:, 0:T], w_o_conv_sb[:, kc, oc * 128:(oc + 1) * 128], convres[:, kc, 0:T], kc == 0, kc == 3,
                       reads=[convres_r, cw2_res], writes=[r_b], wadd=(kc > 0))
                TT("vector", t2[s][:, 0:T], b_b[:, 0:T], sgb[s][:, 0:T], ALU.mult, reads=[r_b, res("sgb0")], writes=[res("t2_0")])
                TT("gpsimd", mergedT[:, oc, 0:T], t1[s][:, 0:T], t2[s][:, 0:T], ALU.add,
                   reads=[res("t1_0"), res("t2_0")], writes=[merged_r], wadd=(oc > 0))
            for qb in qblocks:
                yield
                qoff = qb["off"]
                DMA("sync", xtok[:nq, :], qb["x_rows"], key="xtok", writes=[xtok_r])
                pr_t, (pr0, pr1) = g_pair()
                for half in range(2):
                    brr = pr0 if half == 0 else pr1
                    for k in range(8):
                        MM(pr_t[:nq, half, :], mergedT[:, k, qoff:qoff + nq], w_out_sb[:, k, half * 512:(half + 1) * 512],
                           k == 0, k == 7, reads=[merged_r, cw2_res], writes=[brr], wadd=(k > 0))
                z = pr_t[:nq, :, :].rearrange("p a b -> p (a b)")
                ACT(PT[0][:nq, :, :].rearrange("p a b -> p (a b)"), z, AF.Square, reads=[pr0, pr1],
                    writes=[PT_res[0], small_r], accum=small[:nq, 4:5])
                ACT(small[:nq, 5:6], small[:nq, 4:5], AF.Ln, reads=[small_r], writes=[small_r], scale=1.0 / D_MODEL, bias=eps_t[:nq, 0:1])
                ACT(small[:nq, 6:7], small[:nq, 5:6], AF.Exp, reads=[small_r], writes=[small_r], scale=-0.5)
                STT(yt[:nq, :], z, small[:nq, 6:7], post_g_bc[:nq, :], ALU.mult, ALU.mult,
                    reads=[pr0, pr1, small_r, c_res], writes=[yt_r])
                TT("gpsimd", yt[:nq, :], yt[:nq, :], xtok[:nq, :], ALU.add, reads=[yt_r, xtok_r], writes=[yt_r])
                P.final.append(DMA("sync", qb["y_rows"], yt[:nq, :], key="sty", reads=[yt_r]))


        STAGE = 99
        xT_seq_v = d_xT_seq.rearrange("(k p) t -> p k t", p=128)
        xT_own_v = d_xT_own.rearrange("(k p) t -> p k t", p=128)
        smp_done = []

        def prefix_tile(tl, kbsel=None, spaced=False):
            s = tl % 2
            if kbsel is None:
                kbsel = 3 if s == 0 else 7
            ck, sk = ck_ring[s], sk_ring[s]
            ck_r, sk_r = res(f"ck{s}"), res(f"sk{s}")
            DMA("sync", ck[:], d_cos_k[:, 2 * tl:2 * tl + 2, :], key=f"ck{s}", writes=[ck_r])
            DMA("sync", sk[:], d_sin_k[:, 2 * tl:2 * tl + 2, :], key=f"sk{s}", writes=[sk_r])
            yield from norm_tile(xT_seq_v[:, :, tl * TP:(tl + 1) * TP], TP, hT_p[s], res(f"hTp{s}"), kbsel=kbsel, spaced=spaced)
            yield from key_tile(hT_p[s], res(f"hTp{s}"), [(0, 128, 2 * tl), (128, 128, 2 * tl + 1)],
                                lambda blk, n: ck[:n, blk - 2 * tl, :], lambda blk, n: sk[:n, blk - 2 * tl, :],
                                o_ckv[tl * TP:(tl + 1) * TP, :].rearrange("(b p) c -> p b c", p=128),
                                o_kpe[tl * TP:(tl + 1) * TP, :].rearrange("(b p) c -> p b c", p=128),
                                kbsel=kbsel, spaced=spaced, tab_res=[ck_r, sk_r])
        CH_TILE = 8

        def run_rr(gens):
            live = list(gens)
            while live:
                for g_ in list(live):
                    try:
                        next(g_)
                    except StopIteration:
                        live.remove(g_)

        def pair_qbs(g):
            qbs = []
            for a in range(2):
                k = 2 * g + a
                nslot = 8 * g + 4 + 4 * a
                slots = [(b, 128, (b - (nslot - 4)) if b >= nslot - 4 else None) for b in range(nslot)]
                qbs.append(dict(off=128 * a, nq=128, slots=slots, cos=cos_q[:, k, :], sin=sin_q[:, k, :],
                                x_rows=d_x_own[k * 128:(k + 1) * 128, :], y_rows=o_y[k * 128:(k + 1) * 128, :],
                                kidx=k))
            return qbs

        def pair_part1(g, spaced_norm=False):
            return part1_gen(256, 4, pair_qbs(g), None, o_convp if g == 7 else None, f"p{g}",
                             hT_gs[g % 2], hTg_rs[g % 2], xT_own_v[:, :, g * TGP:(g + 1) * TGP], spaced_norm=spaced_norm)

        NUP = 4
        t0_, t1_ = prefix_tile(0), prefix_tile(1)
        next(t0_)
        x0_ = last_x["op"]
        next(t1_)
        x1_ = last_x["op"]
        emit_wprep([x0_, x1_], 0, 9)
        run_rr([t0_, t1_])
        t2_, t3_ = prefix_tile(2), prefix_tile(3)
        next(t2_)
        x2_ = last_x["op"]
        next(t3_)
        x3_ = last_x["op"]
        emit_wprep([x2_, x3_], 9, NCT // 2)
        run_rr([t2_, t3_])
        g1 = pair_part1(0)
        next(g1)
        emit_initw2([last_x["op"]])
        CH_TILE = 52
        for _ in range(WR):
            w_issue()
        run(g1)
        for g in range(8):
            for tl in range(4 * g + NUP, min(4 * g + NUP + 4, 32)):
                bg_add(prefix_tile(tl, spaced=True), CH_TILE)
            attention(pair_qbs(g), f"p{g}")
            bg_drain()
            p3 = part3_gen(256, pair_qbs(g), hT_gs[g % 2], hTg_rs[g % 2])
            if g < 7:
                run_rr([pair_part1(g + 1, spaced_norm=3), p3])
            else:
                run(p3)
        p_done = [attn_last["p7"]]
        emit_cache(p_done)
        for b in range(SB0, SB0 + 8):
            kv_ready[b] = cw3_res
        smp_slots = [(b, 128, None) for b in range(SB0, SB0 + 8)] + [(SB0 + 8, DEC_SEQ, None)]
        smp_qbs = [dict(off=0, nq=DEC_SEQ, slots=smp_slots, cos=cos_s[:, :], sin=sin_s[:, :],
                        x_rows=d_x_smp, y_rows=o_ys, kidx=0)]
        g1 = part1_gen(DEC_SEQ, 0, smp_qbs, stateT, o_convs, "smp", hT_gs[0], hTg_rs[0],
                       d_xT_smp.rearrange("(k p) t -> p k t", p=128))
        next(g1)
        run(key_tile(hT_gs[0], hTg_rs[0], [(0, DEC_SEQ, SB0 + 8)],
                     lambda blk, n: cos_s[:n, :], lambda blk, n: sin_s[:n, :], o_ckv_s, o_kpe_s, extra_deps=p_done))
        run(g1)
        attention(smp_qbs, "smp")
        run(part3_gen(DEC_SEQ, smp_qbs, hT_gs[0], hTg_rs[0]))
        if record:
            return wseq
        assert STAGE < 99 or wq["used"] == wq["total"], (wq["used"], wq["total"])
        P.emit(nc, es)
    return nc


def _own_blocks(j):
    out = []
    for g in range(8):
        out += [8 * g + j, 8 * g + 7 - j]
    return out


def _rope_tables(pos):
    half = QK_ROPE // 2
    inv = (np.float32(10000.0) ** (-np.arange(half, dtype=np.float32) / np.float32(half))).astype(np.float32)
    ang = pos.astype(np.float32)[:, None] * inv[None, :]
    return np.cos(ang).astype(np.float32), np.sin(ang).astype(np.float32)


_NC_CACHE = {}


def kernel(x_prompt, x_sample, cache_kv_latent, cache_k_rope, state_conv, pre_norm, w_in, q_norm, w_uq, kv_norm,
           w_uk, w_uv, w_o_mla, conv_w, w_o_conv, w_out, post_norm):
    f = lambda a: np.ascontiguousarray(np.asarray(a, dtype=np.float32))
    x_prompt, x_sample = f(x_prompt), f(x_sample)
    W = f(w_in)[0]
    q0, kv0, kr0, gm0, cb0, cc0, cx0, gc0, mm0, mc0 = 0, 256, 384, 416, 928, 1440, 1952, 2464, 2976, 4000
    starts = [q0, q0 + 128] + [gm0 + 128 * i for i in range(4)]
    for jc in range(4):
        starts += [cc0 + 128 * jc, cx0 + 128 * jc, gc0 + 128 * jc, cb0 + 128 * jc]
    for oc in range(8):
        starts += [mm0 + 128 * oc, mc0 + 128 * oc]
    assert len(starts) == NCT
    cols = np.concatenate([np.arange(s, s + 128) for s in starts])
    w_in_t = f(W[:, cols].reshape(8, 128, NCT, 128).transpose(2, 1, 0, 3).reshape(NCT, 128, 1024))
    w_kv = f(W[:, kv0:kv0 + 160].reshape(8, 128, 160).transpose(1, 0, 2))
    wuq = f(w_uq)[0].reshape(Q_LORA, N_HEADS, QK_NOPE + QK_ROPE)
    w_uqT_nope = f(wuq[:, :, :QK_NOPE].transpose(2, 1, 0))
    w_ukT = f(f(w_uk)[0].transpose(2, 1, 0))
    w_uq_rope = f(wuq[:, :, QK_NOPE:].reshape(2, 128, N_HEADS * QK_ROPE).transpose(1, 0, 2))
    common = {
        "w_in_t": w_in_t, "w_kv": w_kv, "w_uqT_nope": w_uqT_nope, "w_ukT": w_ukT, "w_uq_rope": w_uq_rope,
        "w_uv": f(f(w_uv)[0]), "w_o_mla": f(f(w_o_mla)[0]), "w_o_conv": f(f(w_o_conv)[0]), "w_out": f(f(w_out)[0]),
        "pre_g": f(f(pre_norm)[0].reshape(8, 128).T), "q_g": f(f(q_norm)[0].reshape(2, 128).T),
        "kv_g": f(f(kv_norm)[0].reshape(1, 128)), "post_g": f(f(post_norm)[0].reshape(1, 1024)),
        "conv_w_t": f(f(conv_w)[0].reshape(3, 4, 128).transpose(2, 1, 0)),
        "ident": np.eye(128, dtype=np.float32),
    }
    ck, sk = _rope_tables(np.arange(SEQ))
    common["cos_k"] = f(ck.reshape(64, 128, 16).transpose(1, 0, 2))
    common["sin_k"] = f(sk.reshape(64, 128, 16).transpose(1, 0, 2))
    cs, ss = _rope_tables(PAST + np.arange(DEC_SEQ))
    common["cos_s"], common["sin_s"] = f(cs), f(ss)
    ckv_c, kpe_c, st_c = f(cache_kv_latent)[0], f(cache_k_rope)[0], f(state_conv)[0]

    in_maps = []
    own = []
    for c in range(8):
        b, j = c // 4, c % 4
        blks = _own_blocks(j)
        own.append(blks)
        xb = x_prompt[b]
        xT = f(xb.T)
        parts = []
        for g in range(8):
            A, B = blks[2 * g], blks[2 * g + 1]
            tok = list(range(128 * A, 128 * A + 128)) + list(range(128 * B, 128 * B + 128))
            main = xT[:, tok]
            halo = np.zeros((D_MODEL, 4), np.float32)
            for a, blk in enumerate((A, B)):
                if blk > 0:
                    halo[:, 2 * a:2 * a + 2] = xT[:, 128 * blk - 2:128 * blk]
            parts += [main, halo]
        xT_own = f(np.concatenate(parts, axis=1))
        x_own = f(np.concatenate([xb[128 * k:128 * k + 128] for k in blks], axis=0))
        pos_q = np.concatenate([np.arange(128 * k, 128 * k + 128) for k in blks])
        cq, sq = _rope_tables(pos_q)
        mbias = np.zeros((128, 128), np.float32)
        pidx = np.arange(128)
        for k, blk in enumerate(blks):
            g, a = k // 2, k % 2
            nslot = 8 * g + 4 + 4 * a
            for m in range(4):
                kb = nslot - 4 + m
                for hq in range(2):
                    vis = (2 * kb + pidx // 64) <= (2 * blk + hq)
                    mbias[:, k * 8 + m * 2 + hq] = np.where(vis, 0.0, -30000.0)
        d = dict(common)
        d.update({
            "xT_seq": xT, "xT_own": xT_own, "x_own": x_own,
            "xT_smp": f(x_sample[c].T), "x_smp": f(x_sample[c]),
            "cache_kvT": f(ckv_c[c].T), "cache_kv": f(ckv_c[c]), "cache_kpeT": f(kpe_c[c].T),
            "stateT": f(st_c[c].reshape(2, 4, 128).transpose(2, 1, 0)),
            "mbias": f(mbias),
            "cos_q": f(cq.reshape(16, 128, 16).transpose(1, 0, 2)), "sin_q": f(sq.reshape(16, 128, 16).transpose(1, 0, 2)),
        })
        in_maps.append(d)

    if "nc" not in _NC_CACHE:
        _NC_CACHE["nc"] = build_program(build_program(None))
    nc = _NC_CACHE["nc"]
    res = run_bass_kernel_spmd(nc, in_maps, core_ids=list(range(8)))
    R = res.results

    y_prompt = np.zeros((2, SEQ, D_MODEL), np.float32)
    y_sample = np.zeros((8, DEC_SEQ, D_MODEL), np.float32)
    ckv_p = np.zeros((1, 2, SEQ, 128), np.float32)
    kpe_p = np.zeros((1, 2, SEQ, 32), np.float32)
    cv_p = np.zeros((1, 2, 2, CONV_W), np.float32)
    ckv_s = np.zeros((1, 8, DEC_SEQ, 128), np.float32)
    kpe_s = np.zeros((1, 8, DEC_SEQ, 32), np.float32)
    cv_s = np.zeros((1, 8, 2, CONV_W), np.float32)
    for c in range(8):
        b, j = c // 4, c % 4
        r = R[c]
        oy = np.asarray(r["o_y"])
        for k, blk in enumerate(own[c]):
            y_prompt[b, 128 * blk:128 * blk + 128] = oy[128 * k:128 * k + 128]
        y_sample[c] = np.asarray(r["o_ys"])
        ckv_s[0, c] = np.asarray(r["o_ckv_s"])
        kpe_s[0, c] = np.asarray(r["o_kpe_s"])
        cv_s[0, c] = np.asarray(r["o_convs"]).transpose(2, 1, 0).reshape(2, CONV_W)
        if j == 0:
            ckv_p[0, b] = np.asarray(r["o_ckv"])
            kpe_p[0, b] = np.asarray(r["o_kpe"])
            cv_p[0, b] = np.asarray(r["o_convp"]).transpose(2, 1, 0).reshape(2, CONV_W)
    return (y_prompt, y_sample, ckv_p, kpe_p, cv_p, ckv_s, kpe_s, cv_s)
```

```python
import numpy as np
from contextlib import ExitStack
import concourse.bass as bass
import concourse.mybir as mybir
from concourse.bass_utils import run_bass_kernel_spmd

F32 = mybir.dt.float32
BF = mybir.dt.bfloat16
AF = mybir.ActivationFunctionType
ALU = mybir.AluOpType

D_MODEL = 1024
SEQ = 8192
N_HEADS = 8
QK_NOPE = 64
QK_ROPE = 32
KV_LORA = 128
Q_LORA = 256
CONV_W = 512
PAST = 1024
DEC_SEQ = 16
EPS = 1e-6
SM_SCALE = float((QK_NOPE + QK_ROPE) ** -0.5)
SB0 = 0
NCT = 38
TP = 256
TGP = 260
ENGS = ["sync", "tensor", "scalar", "vector", "gpsimd"]


class Op:
    __slots__ = ("eng", "fn", "deps", "signal", "val", "key", "is_dma", "group")

    def __init__(self, eng, fn, deps):
        self.eng, self.fn, self.deps = eng, fn, deps
        self.signal = False
        self.val = 0
        self.key = None
        self.is_dma = False
        self.group = False


class Res:
    __slots__ = ("writers", "readers", "excl")

    def __init__(self, excl=False):
        self.writers = []
        self.readers = []
        self.excl = excl


class Prog:
    def __init__(self):
        self.streams = {e: [] for e in ENGS}
        self.dma_cnt = {}
        self.final = []

    def _flat(self, deps):
        out = []
        for d in deps:
            if d is None:
                continue
            if isinstance(d, (list, tuple)):
                out.extend(self._flat(d))
            else:
                out.append(d)
        return out

    def op(self, eng, fn, deps=()):
        o = Op(eng, fn, self._flat(deps))
        for d in o.deps:
            d.signal = True
        self.streams[eng].append(o)
        return o

    def dma(self, eng, fn, key, deps=(), group=False):
        o = self.op(eng, fn, deps)
        o.is_dma = True
        o.key = key
        o.group = group
        n = self.dma_cnt.get(key, 0) + 1
        self.dma_cnt[key] = n
        o.val = 16 * n
        o.signal = True
        return o

    def I(self, eng, fn, reads=(), writes=(), deps=(), wadd=False, dma_key=None, group=False):
        d = list(deps)
        for r in reads:
            d += r.writers
            if r.excl:
                d += [x for x in r.readers if x.eng != eng]
        for w in writes:
            d += w.readers
            if not wadd:
                d += w.writers
        if dma_key is None:
            o = self.op(eng, fn, d)
        else:
            o = self.dma(eng, fn, dma_key, d, group)
        keep = (lambda lst: [x for x in lst if x.is_dma or x.eng != eng]) if dma_key is None else (lambda lst: list(lst))
        for r in reads:
            r.readers = keep(r.readers) + [o]
        for w in writes:
            if wadd:
                w.writers = keep(w.writers) + [o]
            else:
                w.writers = [o]
                w.readers = []
        return o

    def emit(self, nc, es):
        for eng in ENGS:
            cnt = 0
            for o in self.streams[eng]:
                if o.is_dma:
                    if o.group:
                        o.val = 16 * self.dma_cnt[o.key]
                elif o.signal:
                    cnt += 1
                    o.val = cnt
        sems = {}
        for eng in ENGS[1:]:
            sems[eng] = es.enter_context(nc.semaphore("s_" + eng))
        for key in self.dma_cnt:
            sems["d_" + key] = es.enter_context(nc.semaphore("d_" + key))
        block = es.enter_context(nc.Block())
        final = self.final

        def make(eng):
            ops = self.streams[eng]

            def body(e):
                waited = {}

                def wait_for(d):
                    name = ("d_" + d.key) if d.is_dma else d.eng
                    if waited.get(name, 0) >= d.val:
                        return
                    waited[name] = d.val
                    e.wait_ge(sems[name], d.val)

                for o in ops:
                    for d in o.deps:
                        wait_for(d)
                    ins = o.fn(e)
                    if o.is_dma:
                        ins.then_inc(sems["d_" + o.key], 16)
                    elif o.signal:
                        ins.then_inc(sems[eng], 1)
                if eng == "sync":
                    for d in final:
                        wait_for(d)
            return body

        block.sync(make("sync"))
        block.tensor(make("tensor"))
        block.scalar(make("scalar"))
        block.vector(make("vector"))
        block.gpsimd(make("gpsimd"))


def build_program(wseq_in=None):
    record = wseq_in is None
    nc = bass.Bass("TRN2", target_bir_lowering=False)
    P = Prog()
    I = P.I

    def din(name, shape, dt=F32):
        return nc.dram_tensor(name, list(shape), dt, kind="ExternalInput").ap()

    def dout(name, shape):
        return nc.dram_tensor(name, list(shape), F32, kind="ExternalOutput").ap()

    d_xT_seq = din("xT_seq", [D_MODEL, SEQ])
    d_xT_own = din("xT_own", [D_MODEL, 8 * TGP])
    d_x_own = din("x_own", [2048, D_MODEL])
    d_xT_smp = din("xT_smp", [D_MODEL, DEC_SEQ])
    d_x_smp = din("x_smp", [DEC_SEQ, D_MODEL])
    d_cache_kvT = din("cache_kvT", [128, PAST])
    d_cache_kv = din("cache_kv", [PAST, 128])
    d_cache_kpeT = din("cache_kpeT", [32, PAST])
    d_stateT = din("stateT", [128, 4, 2])
    d_mbias = din("mbias", [128, 128])
    d_w_in_t = din("w_in_t", [NCT, 128, 1024])
    d_w_kv = din("w_kv", [128, 8, 160])
    d_w_uqT = din("w_uqT_nope", [64, 8, 256])
    d_w_ukT = din("w_ukT", [64, 8, 128])
    d_w_uq_rope = din("w_uq_rope", [128, 2, 256])
    d_w_uv = din("w_uv", [128, 8, 64])
    d_w_o_mla = din("w_o_mla", [512, 1024])
    d_w_o_conv = din("w_o_conv", [512, 1024])
    d_w_out = din("w_out", [1024, 1024])
    d_pre_g = din("pre_g", [128, 8])
    d_q_g = din("q_g", [128, 2])
    d_kv_g = din("kv_g", [1, 128])
    d_post_g = din("post_g", [1, 1024])
    d_conv_w = din("conv_w_t", [128, 4, 3])
    d_cos_k = din("cos_k", [128, 64, 16])
    d_sin_k = din("sin_k", [128, 64, 16])
    d_cos_q = din("cos_q", [128, 16, 16])
    d_sin_q = din("sin_q", [128, 16, 16])
    d_cos_s = din("cos_s", [16, 16])
    d_sin_s = din("sin_s", [16, 16])
    d_ident = din("ident", [128, 128])

    d_scr = nc.dram_tensor("w_scr", [NCT, 128, 1024], BF, kind="Internal").ap()

    o_y = dout("o_y", [2048, D_MODEL])
    o_ys = dout("o_ys", [DEC_SEQ, D_MODEL])
    o_ckv = dout("o_ckv", [SEQ, 128])
    o_kpe = dout("o_kpe", [SEQ, 32])
    o_convp = dout("o_convp", [128, 4, 2])
    o_ckv_s = dout("o_ckv_s", [DEC_SEQ, 128])
    o_kpe_s = dout("o_kpe_s", [DEC_SEQ, 32])
    o_convs = dout("o_convs", [128, 4, 2])

    with ExitStack() as es:
        def sb(name, shape, dt):
            return es.enter_context(nc.sbuf_tensor(name, list(shape), dt))

        ckvT_all = sb("ckvT_all", [128, SEQ], BF)
        kpeT_all = sb("kpeT_all", [128, 2048], BF)
        V_all = sb("V_all", [128, 64, 130], BF)
        w_out_sb = sb("w_out_sb", [128, 8, 1024], BF)
        w_o_mla_sb = sb("w_o_mla_sb", [128, 4, 1024], BF)
        w_o_conv_sb = sb("w_o_conv_sb", [128, 4, 1024], BF)
        W_abs_sb = sb("W_abs_sb", [128, 2, 8, 128], BF)
        w_uq_rope_sb = sb("w_uq_rope_sb", [128, 2, 256], BF)
        w_uv_pad = sb("w_uv_pad", [128, 8, 128], BF)
        w_kv_sb = sb("w_kv_sb", [128, 8, 160], BF)
        scrA = sb("scrA", [128, 2048], F32)
        scrB = sb("scrB", [128, 1024], F32)
        w_uqT_sb = scrA[0:64, :].rearrange("p (h m) -> p h m", h=8)
        w_ukT_sb = scrB[0:64, :].rearrange("p (h c) -> p h c", h=8)
        post_g_bc = sb("post_g_bc", [128, 1024], F32)
        kv_g_bc = sb("kv_g_bc", [128, 128], F32)
        ident_bf = sb("ident_bf", [128, 128], BF)
        ones_bf = sb("ones_bf", [128, 128], BF)
        eps_t = sb("eps_t", [128, 1], F32)
        pre_g_sb = sb("pre_g_sb", [128, 8], F32)
        q_g_sb = sb("q_g_sb", [128, 2], F32)
        conv_w_sb = sb("conv_w_sb", [128, 4, 3], F32)
        ck_ring = [sb(f"ck{i}", [128, 2, 16], F32) for i in range(2)]
        sk_ring = [sb(f"sk{i}", [128, 2, 16], F32) for i in range(2)]
        cos_q = sb("cos_q_sb", [128, 16, 16], F32)
        sin_q = sb("sin_q_sb", [128, 16, 16], F32)
        cos_s = sb("cos_s_sb", [16, 16], F32)
        sin_s = sb("sin_s_sb", [16, 16], F32)
        stateT = sb("stateT_sb", [128, 4, 2], F32)
        convst = sb("convst", [128, 4, 2], F32)

        xT_ring = [sb(f"xT{i}", [128, 8, TGP], F32) for i in range(2)]
        xsq_b = [sb(f"xsq{i}", [128, 8, TGP], BF) for i in range(2)]
        lnt_b = [sb(f"lnt{i}", [128, TGP], F32) for i in range(2)]
        rstd_b = [sb(f"rstd{i}", [128, TGP], F32) for i in range(2)]
        hT_p = [sb(f"hTp{i}", [128, 8, TP], BF) for i in range(2)]
        hT_gs = [sb(f"hTg{i}", [128, 8, TGP], BF) for i in range(2)]
        WR = 6
        w_ring = [sb(f"wr{i}", [128, 1024], BF) for i in range(WR)]
        stg_ckv = [sb(f"stgc{i}", [128, 2, 128], F32) for i in range(2)]
        stg_kpe = [sb(f"stgk{i}", [128, 2, 32], F32) for i in range(2)]
        kpe_bf = [sb(f"kpebf{i}", [128, 128], BF) for i in range(2)]
        junkk_b = [sb(f"junkk{i}", [128, 128], BF) for i in range(2)]
        smallk_b = [sb(f"smallk{i}", [128, 8], F32) for i in range(2)]
        kropeA_b = [sb(f"kropeA{i}", [128, 32], F32) for i in range(2)]
        kropeB_b = [sb(f"kropeB{i}", [128, 32], F32) for i in range(2)]
        small = sb("small", [128, 16], F32)
        ropeA = scrB[:, 0:256]
        ropeB = scrB[:, 256:512]
        qlat = scrB[:, 512:1024].rearrange("p (k t) -> p k t", k=2)
        qsq = sb("qsq", [128, 2, 256], BF)
        lnq = sb("lnq", [128, 256], F32)
        rstdq = sb("rstdq", [128, 256], F32)
        qnT = sb("qnT", [128, 2, 256], BF)
        q_absT = sb("q_absT", [128, 8, 256], BF)
        siluG = sb("siluG", [128, 4, 256], BF)
        convres = sb("convres", [128, 4, 256], BF)
        ogT = sb("ogT", [128, 4, 256], BF)
        mergedT = sb("mergedT", [128, 8, 256], BF)
        cc_sb = [sb(f"ccsb{i}", [128, TGP], F32) for i in range(1)] * 2
        uext = [sb(f"uext{i}", [128, 2, 130], F32) for i in range(2)]
        cvt = [sb(f"cvt{i}", [128, 256], F32) for i in range(1)] * 2
        sgc = [sb(f"sgc{i}", [128, 256], F32) for i in range(2)]
        sga = [sb(f"sga{i}", [128, 256], F32) for i in range(1)] * 2
        sgb = [sb(f"sgb{i}", [128, 256], F32) for i in range(1)] * 2
        t1 = [sb(f"t1_{i}", [128, 256], F32) for i in range(1)] * 2
        t2 = [sb(f"t2_{i}", [128, 256], F32) for i in range(1)] * 2
        orep = sb("orep", [128, 8, 4, 32], BF)
        q_peT = [sb(f"qpeT{i}", [128, 8, 128], BF) for i in range(2)]
        mbias_sb = sb("mbias_sb", [128, 128], F32)
        qpm = [sb(f"qpm{i}", [128, 8, 128], BF) for i in range(2)]
        maskr = sb("maskr", [128, 4], F32)
        PT = [sb(f"PT{i}", [128, 8, 128], BF) for i in range(3)]
        rden = sb("rden", [128, 8], F32)
        olat = sb("olat", [128, 8, 128], BF)
        olatT = sb("olatT", [128, 8, 128], BF)
        xtok = scrA[:, 0:1024]
        yt = scrA[:, 1024:2048]

        g01 = es.enter_context(nc.psum_tensor("g01", [128, 2, 512], F32))
        g23 = es.enter_context(nc.psum_tensor("g23", [128, 2, 512], F32))
        o456 = es.enter_context(nc.psum_tensor("o456", [128, 3, 512], F32))
        m7 = es.enter_context(nc.psum_tensor("m7", [128, 512], F32))
        gbank_t = [g01, g01, g23, g23]
        G_res = [Res(True) for _ in range(4)]
        O_res = [Res(True) for _ in range(3)]
        M_res = Res(True)
        gctr = {"s": 0, "p": 0}

        def g_single():
            b = gctr["s"] % 3
            gctr["s"] += 1
            assert not (G_res[b].writers and not G_res[b].readers), "PSUM bank handed out while still live"
            return gbank_t[b][:, b % 2, :], G_res[b]

        def g_pair():
            p = gctr["p"] % 2
            gctr["p"] += 1
            return (g01, (G_res[0], G_res[1])) if p == 0 else (g23, (G_res[2], G_res[3]))

        def k_bank(sel=3):
            if sel == 7:
                return m7[:, :], M_res
            return g23[:, sel - 2, :], G_res[sel]

        R = {}

        def res(name):
            if name not in R:
                R[name] = Res()
            return R[name]

        def MM(out, lhsT, rhs, start, stop, reads=(), writes=(), wadd=False, tp=None, sgcheck=False, deps=()):
            def fn(e):
                kw = {}
                if tp is not None:
                    kw["tile_position"] = tp
                if sgcheck:
                    kw["skip_group_check"] = True
                return e.matmul(out, lhsT=lhsT, rhs=rhs, start=start, stop=stop, **kw)
            return I("tensor", fn, reads, writes, deps, wadd)

        def TR(out, in_, n, reads=(), writes=(), wadd=False, deps=()):
            return I("tensor", lambda e: e.transpose(out, in_, ident_bf[:n, :n]), reads, writes, deps, wadd)

        def ACT(out, in_, func, reads=(), writes=(), wadd=False, scale=None, bias=None, accum=None, deps=()):
            def fn(e):
                kw = {}
                if scale is not None:
                    kw["scale"] = scale
                if bias is not None:
                    kw["bias"] = bias
                if accum is not None:
                    kw["accum_out"] = accum
                return e.activation(out=out, in_=in_, func=func, **kw)
            return I("scalar", fn, reads, writes, deps, wadd)

        def TT(eng, out, in0, in1, op, reads=(), writes=(), wadd=False, deps=()):
            return I(eng, lambda e: e.tensor_tensor(out=out, in0=in0, in1=in1, op=op), reads, writes, deps, wadd)

        def STT(out, in0, scalar, in1, op0, op1, reads=(), writes=(), wadd=False, deps=()):
            return I("vector", lambda e: e.scalar_tensor_tensor(out=out, in0=in0, scalar=scalar, in1=in1, op0=op0, op1=op1),
                     reads, writes, deps, wadd)

        def TS(eng, out, in0, s1, op0, reads=(), writes=(), wadd=False, deps=()):
            return I(eng, lambda e: e.tensor_scalar(out=out, in0=in0, scalar1=s1, scalar2=None, op0=op0),
                     reads, writes, deps, wadd)

        def CP(eng, out, in_, reads=(), writes=(), wadd=False, deps=()):
            if eng == "scalar":
                return ACT(out, in_, AF.Copy, reads, writes, wadd, deps=deps)
            return I(eng, lambda e: e.tensor_copy(out=out, in_=in_), reads, writes, deps, wadd)

        def DMA(eng, out, in_, key, reads=(), writes=(), deps=(), group=False, wadd=False):
            return I(eng, lambda e: e.dma_start(out=out, in_=in_), reads, writes, deps, wadd, dma_key=key, group=group)

        def MEMSET(out, val, writes=(), wadd=False, deps=()):
            return I("gpsimd", lambda e: e.memset(out, val), (), writes, deps, wadd)

        c_res = res("consts")
        MEMSET(eps_t[:], EPS, [c_res])
        MEMSET(ones_bf[:], 1.0, [c_res], wadd=True)
        MEMSET(V_all[:, :, 128:130], 1.0, [c_res], wadd=True)
        ms_uv = MEMSET(w_uv_pad[:], 0.0, [c_res], wadd=True)
        ms_kpe = MEMSET(kpeT_all[:], 0.0, [c_res], wadd=True)
        for i in range(2):
            MEMSET(qpm[i][:], 0.0, [c_res], wadd=True)
        ms_mr = MEMSET(maskr[:], 0.0, [c_res], wadd=True)
        for r_ in range(4):
            MEMSET(maskr[32 * r_:32 * r_ + 32, r_:r_ + 1], 1.0, [c_res], wadd=True, deps=[ms_mr])
        for i in range(2):
            MEMSET(kpe_bf[i][:], 0.0, [c_res], wadd=True)

        gq = "gpsimd"
        iw = dict(key="initw", group=True, writes=[c_res], wadd=True)
        DMA(gq, ident_bf[:], d_ident, **iw)
        DMA(gq, w_kv_sb[:], d_w_kv, **iw)
        DMA(gq, w_uq_rope_sb[:], d_w_uq_rope, **iw)
        uvp = w_uv_pad[:].rearrange("p (j two) c -> p j two c", two=2)
        uvd = d_w_uv.rearrange("p (j two) c -> p j two c", two=2)
        DMA(gq, uvp[:, :, 0, 0:64], uvd[:, :, 0, :], deps=[ms_uv], **iw)
        DMA(gq, uvp[:, :, 1, 64:128], uvd[:, :, 1, :], deps=[ms_uv], **iw)
        wprep = []
        def emit_wprep(deps=()):
            for i in range(NCT // 2):
                wprep.append(P.dma(gq, (lambda i: (lambda e: e.dma_start(
                    out=d_scr[2 * i:2 * i + 2].rearrange("a p n -> (a p) n"),
                    in_=d_w_in_t[2 * i:2 * i + 2].rearrange("a p n -> (a p) n"))))(i), f"wp{i}", list(deps)))

        cw2_res = res("consts2")

        def emit_initw2(deps=()):
            iw2 = dict(key="initw2", group=True, writes=[cw2_res], wadd=True, deps=list(deps))
            DMA(gq, w_o_conv_sb[:], d_w_o_conv.rearrange("(k p) n -> p k n", p=128), **iw2)
            DMA(gq, w_o_mla_sb[:], d_w_o_mla.rearrange("(k p) n -> p k n", p=128), **iw2)
            DMA(gq, w_out_sb[:], d_w_out.rearrange("(k p) n -> p k n", p=128), **iw2)


        cw3_res = res("consts3")

        def emit_cache(deps):
            iw3 = dict(key="initw3", group=True, writes=[cw3_res], wadd=True, deps=deps)
            DMA(gq, ckvT_all[:, 0:PAST], d_cache_kvT, **iw3)
            DMA(gq, kpeT_all[0:32, 0:PAST], d_cache_kpeT, **iw3)
            DMA(gq, V_all[:, 0:8, 0:128], d_cache_kv.rearrange("(b p) c -> p b c", p=128), **iw3)

        ic = dict(key="initc", group=True, writes=[c_res], wadd=True)
        DMA("sync", pre_g_sb[:], d_pre_g, **ic)
        DMA("sync", q_g_sb[:], d_q_g, **ic)
        DMA("sync", conv_w_sb[:], d_conv_w, **ic)
        DMA("sync", kv_g_bc[:], d_kv_g[0:1, :].broadcast_to([128, 128]), **ic)
        DMA("sync", post_g_bc[:], d_post_g[0:1, :].broadcast_to([128, 1024]), **ic)
        DMA("sync", mbias_sb[:], d_mbias, **ic)
        DMA("sync", cos_s[:], d_cos_s, **ic)
        DMA("sync", sin_s[:], d_sin_s, **ic)
        DMA("sync", stateT[:], d_stateT, **ic)
        DMA("sync", w_uqT_sb, d_w_uqT, **ic)
        DMA("sync", w_ukT_sb, d_w_ukT, **ic)
        DMA("sync", cos_q[:], d_cos_q, **ic)
        DMA("sync", sin_q[:], d_sin_q, **ic)

        wabs_mm = []
        for kc in range(2):
            for hq in range(2):
                bank, br = g_single()
                for hh in range(4):
                    h = hq * 4 + hh
                    wabs_mm.append(MM(bank[:, hh * 128:(hh + 1) * 128], w_uqT_sb[0:64, h, kc * 128:(kc + 1) * 128],
                                      w_ukT_sb[0:64, h, :], True, True, reads=[c_res], writes=[br], wadd=(hh > 0)))
                CP("vector", W_abs_sb[:, kc, hq * 4:hq * 4 + 4, :],
                   bank.rearrange("p (h c) -> p h c", h=4), reads=[br], writes=[c_res], wadd=True)

        wq = {"issued": 0, "used": 0, "total": 0}
        w_res = [Res() for _ in range(WR)]

        def w_issue():
            if record:
                return
            i = wq["issued"]
            if i >= wq["total"]:
                return
            ct = wseq[i]
            s = i % WR
            DMA("sync", w_ring[s][:], d_scr[ct], key=f"w{s}", writes=[w_res[s]], deps=[wprep[ct // 2]])
            wq["issued"] += 1

        def w_take(ct):
            i = wq["used"]
            if record:
                wseq.append(ct)
            else:
                assert wseq[i] == ct, (i, wseq[i], ct)
            wq["used"] += 1
            s = i % WR
            return w_ring[s], w_res[s]

        wseq = [] if record else list(wseq_in)
        wq["total"] = len(wseq)

        xctr = {"i": 0}
        last_x = {"op": None}
        xT_res = [Res(), Res()]

        def norm_tile(src, T, hT, hT_r, kbsel=3, spaced=False):
            s = xctr["i"] % 2
            xctr["i"] += 1
            xt, xr = xT_ring[s], xT_res[s]
            xsq, lnt, rstd = xsq_b[s], lnt_b[s], rstd_b[s]
            xsq_r, lnt_r, rstd_r = res(f"xsq{s}"), res(f"lnt{s}"), res(f"rstd{s}")
            last_x["op"] = DMA("sync", xt[:, :, 0:T], src, key=f"xT{s}", writes=[xr])
            TT("vector", xsq[:, :, 0:T], xt[:, :, 0:T], xt[:, :, 0:T], ALU.mult, reads=[xr], writes=[xsq_r])
            if spaced:
                for _ in range(12 if spaced is True else int(spaced)):
                    yield
            bank, br = k_bank(kbsel)
            for k in range(8):
                MM(bank[:, 0:T], ones_bf[:, :], xsq[:, k, 0:T], k == 0, k == 7,
                   reads=[xsq_r, c_res], writes=[br], wadd=(k > 0))
            ACT(lnt[:, 0:T], bank[:, 0:T], AF.Ln, reads=[br], writes=[lnt_r], scale=1.0 / D_MODEL, bias=eps_t[:, 0:1])
            ACT(rstd[:, 0:T], lnt[:, 0:T], AF.Exp, reads=[lnt_r], writes=[rstd_r], scale=-0.5)
            for k in range(8):
                STT(hT[:, k, 0:T], xt[:, k, 0:T], pre_g_sb[:, k:k + 1], rstd[:, 0:T], ALU.mult, ALU.mult,
                    reads=[xr, rstd_r, c_res], writes=[hT_r], wadd=(k > 0))
            yield
            if spaced is True:
                for _ in range(12):
                    yield

        kv_ready = {}
        kctr = {"i": 0}
        kpcol_res = [Res() for _ in range(16)]
        small_r = res("small")
        ropeA_r, ropeB_r = res("ropeA"), res("ropeB")
        for nm in ("ropeA", "ropeB", "qlat", "xtok", "yt"):
            res(nm).readers = list(wabs_mm[-1:])

        def key_tile(hT, hT_r, blocks, cos_of, sin_of, out_ckv, out_kpe, extra_deps=(), kbsel=3, spaced=False, tab_res=()):
            s = kctr["i"] % 2
            kctr["i"] += 1
            stc, stk, kb = stg_ckv[s], stg_kpe[s], kpe_bf[s]
            stc_r, stk_r, kb_r = res(f"stgc{s}"), res(f"stgk{s}"), res(f"kpebf{s}")
            junkk, smallk, kropeA, kropeB = junkk_b[s], smallk_b[s], kropeA_b[s], kropeB_b[s]
            smallk_r, junkk_r = res(f"smallk{s}"), res(f"junkk{s}")
            kropeA_r, kropeB_r = res(f"kropeA{s}"), res(f"kropeB{s}")
            bank, br = k_bank(kbsel)
            for bi, (off, n, blk) in enumerate(blocks):
                for k in range(8):
                    MM(bank[:n, bi * 160:(bi + 1) * 160], hT[:, k, off:off + n], w_kv_sb[:, k, :], k == 0, k == 7,
                       reads=[hT_r, c_res], writes=[br], wadd=(bi > 0 or k > 0))
            yield
            tb = bank[:, 320:448].bitcast(BF)
            tr = br
            for bi, (off, n, blk) in enumerate(blocks):
                r = blk // 16
                kvr = Res()
                kv_ready[blk] = kvr
                kvp = bank[:n, bi * 160:bi * 160 + 128]
                rp = bank[:n, bi * 160 + 128:bi * 160 + 160].rearrange("p (a b) -> p a b", a=2)
                ACT(junkk[:n, 0:128], kvp, AF.Square, reads=[br], writes=[junkk_r, smallk_r], accum=smallk[:n, 0:1])
                ACT(smallk[:n, 1:2], smallk[:n, 0:1], AF.Ln, reads=[smallk_r], writes=[smallk_r],
                    scale=1.0 / KV_LORA, bias=eps_t[:n, 0:1])
                ACT(smallk[:n, 2:3], smallk[:n, 1:2], AF.Exp, reads=[smallk_r], writes=[smallk_r], scale=-0.5)
                cb = cos_of(blk, n).unsqueeze(1).broadcast_to([n, 2, 16])
                sbb = sin_of(blk, n).unsqueeze(1).broadcast_to([n, 2, 16])
                A3 = kropeA[:n, 0:32].rearrange("p (a b) -> p a b", a=2)
                B3 = kropeB[:n, 0:32].rearrange("p (a b) -> p a b", a=2)
                TT("vector", A3, rp, cb, ALU.mult, reads=[br, c_res] + list(tab_res), writes=[kropeA_r])
                TT("vector", B3, rp, sbb, ALU.mult, reads=[br, c_res] + list(tab_res), writes=[kropeB_r])
                TT("vector", stk[:n, bi, 0:16], kropeA[:n, 0:16], kropeB[:n, 16:32], ALU.subtract,
                   reads=[kropeA_r, kropeB_r], writes=[stk_r], wadd=(bi > 0))
                TT("vector", stk[:n, bi, 16:32], kropeA[:n, 16:32], kropeB[:n, 0:16], ALU.add,
                   reads=[kropeA_r, kropeB_r], writes=[stk_r], wadd=True)
                CP("gpsimd", kb[:n, 32 * r:32 * r + 32], stk[:n, bi, :], reads=[stk_r], writes=[kb_r])
                STT(stc[:n, bi, :], kvp, smallk[:n, 2:3], kv_g_bc[:n, :], ALU.mult, ALU.mult,
                    reads=[br, smallk_r, c_res], writes=[stc_r], wadd=(bi > 0))
                CP("gpsimd", V_all[:n, blk, 0:128], stc[:n, bi, :], reads=[stc_r], writes=[kvr], deps=extra_deps)
                yield
                if spaced:
                    for _ in range(9):
                        yield
                TR(tb[:, 0:n], V_all[:n, blk, 0:128], n, reads=[kvr, c_res], writes=[tr], wadd=True)
                TR(tb[:, 128:128 + n], kb[:n, :], n, reads=[kb_r, c_res], writes=[tr], wadd=True)
                yield
                CP("vector", ckvT_all[:, blk * 128:blk * 128 + n], tb[:, 0:n],
                   reads=[tr], writes=[kvr], wadd=True, deps=extra_deps)
                c0 = (blk % 16) * 128
                CP("vector", kpeT_all[32 * r:32 * r + 32, c0:c0 + n], tb[32 * r:32 * r + 32, 128:128 + n],
                   reads=[tr], writes=[kvr, kpcol_res[blk % 16]], wadd=True, deps=extra_deps)
            nb = len(blocks)
            n0 = blocks[0][1]
            outs = []
            outs.append(DMA("sync", out_ckv, stc[:n0, 0:nb, :] if nb > 1 else stc[:n0, 0, :], key=f"stc{s}", reads=[stc_r]))
            outs.append(DMA("sync", out_kpe, stk[:n0, 0:nb, :] if nb > 1 else stk[:n0, 0, :], key=f"stk{s}", reads=[stk_r]))
            P.final.extend(outs)
            yield

        BG = []
        bgc = {"left": 0}

        def run(gen):
            for _ in gen:
                pass

        def bg_add(gen, nchunks):
            BG.append(gen)
            bgc["left"] += nchunks

        def bg_step(n):
            while n > 0 and BG:
                k_ = bgc.get("rr", 0) % min(2, len(BG))
                bgc["rr"] = bgc.get("rr", 0) + 1
                try:
                    next(BG[k_])
                    n -= 1
                    bgc["left"] = max(bgc["left"] - 1, 0)
                except StopIteration:
                    BG.pop(k_)

        def bg_drain():
            while BG:
                bg_step(1000)
            bgc["left"] = 0

        qlat_r, qsq_r, lnq_r, rstdq_r, qnT_r = res("qlat"), res("qsq"), res("lnq"), res("rstdq"), res("qnT")
        qabs_r, siluG_r, convres_r, ogT_r, merged_r = res("qabs"), res("siluG"), res("convres"), res("ogT"), res("merged")
        hTg_rs = [res("hTg0"), res("hTg1")]
        orep_r, rden_r, olat_r, olatT_r, xtok_r, yt_r = res("orep"), res("rden"), res("olat"), res("olatT"), res("xtok"), res("yt")
        convst_r = res("convst")
        ring2 = {}

        def nxt(name):
            i = ring2.get(name, 0)
            ring2[name] = i + 1
            return i % 2

        ptctr = {"i": 0}
        PT_res = [Res() for _ in range(3)]
        qpeT_res = [Res(), Res()]
        qpm_res = [Res(), Res()]
        qpmctr = {"i": 0}
        qpm_state = [None, None]
        attn_last = {}

        def inproj_ct(ct, N, hT_g, hTg_r):
            wt, wr = w_take(ct)
            bank, br = g_single()
            for k in range(8):
                MM(bank[:, 0:N], wt[:, k * 128:(k + 1) * 128], hT_g[:, k, 0:N], k == 0, k == 7,
                   reads=[wr, hTg_r], writes=[br], wadd=(k > 0))
            w_issue()
            if BG and bgc.get("inproj", False):
                bg_step(1)
            return bank, br

        def part1_gen(T, n_halo, qblocks, conv_state_src, conv_out, name, hT_g, hTg_r, xsrc, spaced_norm=False):
            yield from norm_tile(xsrc, T + n_halo, hT_g, hTg_r, spaced=spaced_norm)
            TA = T + n_halo
            nqb = len(qblocks)
            nq = qblocks[0]["nq"]
            for kc in range(2):
                bank, br = inproj_ct(kc, T, hT_g, hTg_r)
                CP("vector", qlat[:, kc, 0:T], bank[:, 0:T], reads=[br], writes=[qlat_r], wadd=(kc > 0))
                ACT(qsq[:, kc, 0:T], bank[:, 0:T], AF.Square, reads=[br], writes=[qsq_r], wadd=(kc > 0))
                yield
            bank, br = g_single()
            for kc in range(2):
                MM(bank[:, 0:T], ones_bf[:, :], qsq[:, kc, 0:T], kc == 0, kc == 1, reads=[qsq_r, c_res], writes=[br], wadd=(kc > 0))
            ACT(lnq[:, 0:T], bank[:, 0:T], AF.Ln, reads=[br], writes=[lnq_r], scale=1.0 / Q_LORA, bias=eps_t[:, 0:1])
            ACT(rstdq[:, 0:T], lnq[:, 0:T], AF.Exp, reads=[lnq_r], writes=[rstdq_r], scale=-0.5)
            for kc in range(2):
                STT(qnT[:, kc, 0:T], qlat[:, kc, 0:T], q_g_sb[:, kc:kc + 1], rstdq[:, 0:T], ALU.mult, ALU.mult,
                    reads=[qlat_r, rstdq_r, c_res], writes=[qnT_r], wadd=(kc > 0))
            yield
            for h in range(8):
                bank, br = g_single()
                for kc in range(2):
                    MM(bank[:, 0:T], W_abs_sb[:, kc, h, :], qnT[:, kc, 0:T], kc == 0, kc == 1,
                       reads=[qnT_r, c_res], writes=[br], wadd=(kc > 0))
                CP("scalar" if h % 2 == 0 else "vector", q_absT[:, h, 0:T], bank[:, 0:T], reads=[br], writes=[qabs_r], wadd=(h > 0))
                if h == 3:
                    yield
            KSUB = 99
            for qi, qb in enumerate(qblocks):
                qoff = qb["off"]
                for kc in range(2):
                    MM(m7[:nq, 0:256], qnT[:, kc, qoff:qoff + nq], w_uq_rope_sb[:, kc, :], kc == 0, kc == 1,
                       reads=[qnT_r, c_res], writes=[M_res], wadd=(kc > 0))
                q4 = lambda ap: ap.rearrange("p (h a b) -> p h a b", h=8, a=2)
                cb = qb["cos"].unsqueeze(1).unsqueeze(1).broadcast_to([nq, 8, 2, 16])
                sbb = qb["sin"].unsqueeze(1).unsqueeze(1).broadcast_to([nq, 8, 2, 16])
                TT("vector", q4(ropeA[:nq, :]), q4(m7[:nq, 0:256]), cb, ALU.mult, reads=[M_res, c_res], writes=[ropeA_r])
                TT("vector", q4(ropeB[:nq, :]), q4(m7[:nq, 0:256]), sbb, ALU.mult, reads=[M_res, c_res], writes=[ropeB_r])
                TT("vector", orep[:nq, :, 0, 0:16], q4(ropeA[:nq, :])[:, :, 0, :], q4(ropeB[:nq, :])[:, :, 1, :], ALU.subtract,
                   reads=[ropeA_r, ropeB_r], writes=[orep_r])
                TT("vector", orep[:nq, :, 0, 16:32], q4(ropeA[:nq, :])[:, :, 1, :], q4(ropeB[:nq, :])[:, :, 0, :], ALU.add,
                   reads=[ropeA_r, ropeB_r], writes=[orep_r], wadd=True)
                CP("vector", orep[:nq, :, 1:4, :], orep[:nq, :, 0:1, :].broadcast_to([nq, 8, 3, 32]),
                   reads=[orep_r], writes=[orep_r])
                tb_f, tr = g_single()
                tb = tb_f.bitcast(BF)
                for h in range(8):
                    TR(tb[:, h * 128:h * 128 + nq], orep[:nq, h, :, :].rearrange("p a b -> p (a b)"), nq,
                       reads=[orep_r, c_res], writes=[tr], wadd=(h > 0))
                CP("scalar", q_peT[qi][:, :, 0:nq], tb.rearrange("p (h q) -> p h q", h=8)[:, :, 0:nq],
                   reads=[tr], writes=[qpeT_res[qi]])
                yield
            yield
            for i in range(4):
                bank, br = inproj_ct(2 + i, T, hT_g, hTg_r)
                ACT(siluG[:, i, 0:T], bank[:, 0:T], AF.Silu, reads=[br], writes=[siluG_r], wadd=(i > 0))
                if i % 2 == 1:
                    yield
            yield
            for jc in range(4):
                if jc > 0:
                    yield
                s = nxt("conv")
                cc, cc_r = cc_sb[s], res("cc0")
                ue, ue_r = uext[s], res(f"ue{s}")
                cv, cv_r = cvt[s], res("cv0")
                sg, sg_r = sgc[s], res(f"sgc{s}")
                v3 = lambda ap: ap.rearrange("p (q t) -> p q t", q=nqb)
                b_cc, r_cc = inproj_ct(6 + 4 * jc, TA, hT_g, hTg_r)
                CP("scalar", cc[:, 0:TA], b_cc[:, 0:TA], reads=[r_cc], writes=[cc_r])
                b_cx, r_cx = inproj_ct(7 + 4 * jc, TA, hT_g, hTg_r)
                TT("vector", ue[:, 0:nqb, 2:2 + nq], v3(b_cx[:, 0:T]), v3(cc[:, 0:T]), ALU.mult,
                   reads=[r_cx, cc_r], writes=[ue_r])
                if n_halo:
                    TT("vector", ue[:, 0:nqb, 0:2], v3(b_cx[:, T:TA]), v3(cc[:, T:TA]), ALU.mult,
                       reads=[r_cx, cc_r], writes=[ue_r], wadd=True)
                else:
                    CP("gpsimd", ue[:, 0, 0:2], conv_state_src[:, jc, :], reads=[c_res], writes=[ue_r], wadd=True)
                b_gc, r_gc = inproj_ct(8 + 4 * jc, T, hT_g, hTg_r)
                ACT(sg[:, 0:T], b_gc[:, 0:T], AF.Silu, reads=[r_gc], writes=[sg_r])
                cv3 = cv[:, 0:T].rearrange("p (q t) -> p q t", q=nqb)
                TS("vector", cv3, ue[:, 0:nqb, 0:nq], conv_w_sb[:, jc, 0:1], ALU.mult, reads=[ue_r, c_res], writes=[cv_r])
                STT(cv3, ue[:, 0:nqb, 1:1 + nq], conv_w_sb[:, jc, 1:2], cv3, ALU.mult, ALU.add, reads=[ue_r, cv_r], writes=[cv_r])
                STT(cv3, ue[:, 0:nqb, 2:2 + nq], conv_w_sb[:, jc, 2:3], cv3, ALU.mult, ALU.add, reads=[ue_r, cv_r], writes=[cv_r])
                b_cb, r_cb = inproj_ct(9 + 4 * jc, T, hT_g, hTg_r)
                TT("vector", cv[:, 0:T], cv[:, 0:T], b_cb[:, 0:T], ALU.mult, reads=[cv_r, r_cb], writes=[cv_r])
                TT("vector", convres[:, jc, 0:T], cv[:, 0:T], sg[:, 0:T], ALU.mult, reads=[cv_r, sg_r], writes=[convres_r], wadd=(jc > 0))
                if conv_out is not None:
                    CP("gpsimd", convst[:, jc, :], ue[:, nqb - 1, nq:nq + 2], reads=[ue_r], writes=[convst_r], wadd=(jc > 0))
            if conv_out is not None:
                P.final.append(DMA("sync", conv_out, convst[:], key="stconv", reads=[convst_r]))


        def attention(qblocks, name):
            nq = qblocks[0]["nq"]
            slots_left = {"n": sum(len(q_["slots"]) + 1 for q_ in qblocks)}

            def post_A(qi):
                for ob, hs in enumerate([(0, 3), (3, 6), (6, 8)]):
                    nh = hs[1] - hs[0]
                    ov = o456[:nq, ob, 0:nh * 129].rearrange("p (h c) -> p h c", c=129)
                    I("vector", (lambda ov=ov, hs=hs: (lambda e: e.reciprocal(rden[:nq, hs[0]:hs[1]].unsqueeze(2), ov[:, :, 128:129])))(),
                      reads=[O_res[ob]], writes=[rden_r], wadd=(ob > 0))
                    TT("vector", olat[:nq, hs[0]:hs[1], :], ov[:, :, 0:128],
                       rden[:nq, hs[0]:hs[1]].unsqueeze(2).broadcast_to([nq, nh, 128]), ALU.mult,
                       reads=[O_res[ob], rden_r], writes=[olat_r], wadd=(ob > 0))

            def post_B(qi):
                qoff = qblocks[qi]["off"]
                tb_f, tr = g_single()
                tb = tb_f.bitcast(BF)
                for h in range(8):
                    TR(tb[:, h * 128:h * 128 + nq], olat[:nq, h, :], nq, reads=[olat_r, c_res], writes=[tr], wadd=(h > 0))
                CP("scalar", olatT[:, :, 0:nq], tb.rearrange("p (h q) -> p h q", h=8)[:, :, 0:nq], reads=[tr], writes=[olatT_r])
                ob_, or_ = g_single()
                for j in range(4):
                    MM(ob_[:, j * 128:j * 128 + nq], w_uv_pad[:, 2 * j, :], olatT[:, 2 * j, 0:nq], True, False,
                       reads=[olatT_r, c_res], writes=[or_], wadd=(j > 0))
                    MM(ob_[:, j * 128:j * 128 + nq], w_uv_pad[:, 2 * j + 1, :], olatT[:, 2 * j + 1, 0:nq], False, True,
                       reads=[olatT_r, c_res], writes=[or_], wadd=True)
                TT("vector", ogT[:, :, qoff:qoff + nq], ob_.rearrange("p (j q) -> p j q", j=4)[:, :, 0:nq],
                   siluG[:, :, qoff:qoff + nq], ALU.mult, reads=[or_, siluG_r], writes=[ogT_r], wadd=(qi > 0))

            pending_post = []
            for qi, qb in enumerate(qblocks):
                qoff = qb["off"]
                qp, qp_r = q_peT[qi], qpeT_res[qi]
                slots = qb["slots"]
                ns = len(slots)
                last_pv = None
                sl_state = {}

                def get_qvar(r):
                    key = (name, qi, r)
                    for i in range(2):
                        if qpm_state[i] is not None and qpm_state[i][0] == key:
                            return qpm[i], qpm_res[i]
                    i = qpmctr["i"] % 2
                    qpmctr["i"] += 1
                    if qpm_state[i] is not None and qpm_state[i][1] != r:
                        ro = qpm_state[i][1]
                        I("vector", (lambda i=i, ro=ro: (lambda e: e.memset(qpm[i][32 * ro:32 * ro + 32, :, :], 0.0)))(),
                          (), [qpm_res[i]])
                        CP("vector", qpm[i][32 * r:32 * r + 32, :, 0:nq], qp[32 * r:32 * r + 32, :, 0:nq],
                           reads=[qp_r], writes=[qpm_res[i]], wadd=True)
                    else:
                        CP("vector", qpm[i][32 * r:32 * r + 32, :, 0:nq], qp[32 * r:32 * r + 32, :, 0:nq],
                           reads=[qp_r, c_res], writes=[qpm_res[i]])
                    qpm_state[i] = (key, r)
                    return qpm[i], qpm_res[i]

                get_qvar(slots[0][0] // 16)

                def emit_S(si):
                    blk, nk, midx = slots[si]
                    r = blk // 16
                    kcol = blk * 128
                    kpcol = (blk % 16) * 128
                    kvr = kv_ready[blk]
                    pt_i = ptctr["i"] % 3
                    ptctr["i"] += 1
                    pt, pt_r = PT[pt_i], PT_res[pt_i]
                    qm, qm_r = get_qvar(r)
                    first_w = True
                    for hh in range(2):
                        sbank, brr = g_single()
                        S = sbank[:nk, 0:4 * nq].rearrange("p (h q) -> p h q", h=4)
                        MM(S, ckvT_all[:, kcol:kcol + nk], q_absT[:, 4 * hh:4 * hh + 4, qoff:qoff + nq], True, False,
                           reads=[kvr, qabs_r], writes=[brr])
                        MM(S, kpeT_all[:, kpcol:kpcol + nk], qm[:, 4 * hh:4 * hh + 4, 0:nq],
                           False, True, reads=[kvr, qm_r, c_res, kpcol_res[blk % 16]], writes=[brr], wadd=True)
                        if midx is None:
                            ACT(pt[:nk, 4 * hh:4 * hh + 4, 0:nq], S, AF.Exp, reads=[brr], writes=[pt_r], wadd=not first_w, scale=SM_SCALE)
                            first_w = False
                        else:
                            for hq in range(2):
                                col = qb["kidx"] * 8 + midx * 2 + hq
                                ACT(pt[:nk, 4 * hh:4 * hh + 4, 64 * hq:64 * hq + 64], S[:, :, 64 * hq:64 * hq + 64], AF.Exp,
                                    reads=[brr, c_res], writes=[pt_r], wadd=not first_w, scale=SM_SCALE,
                                    bias=mbias_sb[:nk, col:col + 1])
                                first_w = False
                    sl_state[si] = (pt, pt_r, kvr)

                def emit_PV(si):
                    blk, nk, midx = slots[si]
                    pt, pt_r, kvr = sl_state.pop(si)
                    lp = None
                    for h in range(8):
                        ob, oc = h // 3, h % 3
                        first = (si == 0 and oc == 0)
                        lp = MM(o456[:nq, ob, oc * 129:oc * 129 + 129], pt[:nk, h, 0:nq], V_all[:nk, blk, 0:129],
                                first, si == ns - 1, reads=[pt_r, kvr], writes=[O_res[ob]],
                                wadd=not first, sgcheck=True)
                    return lp

                for si in range(ns + 1):
                    if si + 2 < ns:
                        get_qvar(slots[si + 2][0] // 16)
                    if si < ns:
                        emit_S(si)
                    if si >= 1:
                        last_pv = emit_PV(si - 1)
                    if si == 2 and pending_post:
                        post_B(pending_post.pop(0))
                    if BG:
                        sleft = max(slots_left["n"], 1)
                        bg_step(-(-bgc["left"] // sleft))
                    slots_left["n"] -= 1
                attn_last[name] = last_pv
                while pending_post:
                    post_B(pending_post.pop(0))
                post_A(qi)
                pending_post.append(qi)
            while pending_post:
                post_B(pending_post.pop(0))


        def part3_gen(T, qblocks, hT_g, hTg_r):
            nq = qblocks[0]["nq"]
            for oc in range(8):
                if oc > 0:
                    yield
                s = nxt("merge")
                b_ma, r_ma = inproj_ct(22 + 2 * oc, T, hT_g, hTg_r)
                ACT(sga[s][:, 0:T], b_ma[:, 0:T], AF.Sigmoid, reads=[r_ma], writes=[res("sga0")])
                b_mb, r_mb = inproj_ct(23 + 2 * oc, T, hT_g, hTg_r)
                ACT(sgb[s][:, 0:T], b_mb[:, 0:T], AF.Sigmoid, reads=[r_mb], writes=[res("sgb0")])
                b_a, r_a = g_single()
                for kc in range(4):
                    MM(b_a[:, 0:T], w_o_mla_sb[:, kc, oc * 128:(oc + 1) * 128], ogT[:, kc, 0:T], kc == 0, kc == 3,
                       reads=[ogT_r, cw2_res], writes=[r_a], wadd=(kc > 0))
                TT("vector", t1[s][:, 0:T], b_a[:, 0:T], sga[s][:, 0:T], ALU.mult, reads=[r_a, res("sga0")], writes=[res("t1_0")])
                b_b, r_b = g_single()
                for kc in range(4):
                    MM(b_b[:, 0:T], w_o_conv_sb[:, kc, oc * 128:(oc + 1) * 128], convres[:, kc, 0:T], kc == 0, kc == 3,
                       reads=[convres_r, cw2_res], writes=[r_b], wadd=(kc > 0))
                TT("vector", t2[s][:, 0:T], b_b[:, 0:T], sgb[s][:, 0:T], ALU.mult, reads=[r_b, res("sgb0")], writes=[res("t2_0")])
                TT("gpsimd", mergedT[:, oc, 0:T], t1[s][:, 0:T], t2[s][:, 0:T], ALU.add,
                   reads=[res("t1_0"), res("t2_0")], writes=[merged_r], wadd=(oc > 0))
            for qb in qblocks:
                yield
                qoff = qb["off"]
                DMA("sync", xtok[:nq, :], qb["x_rows"], key="xtok", writes=[xtok_r])
                pr_t, (pr0, pr1) = g_pair()
                for half in range(2):
                    brr = pr0 if half == 0 else pr1
                    for k in range(8):
                        MM(pr_t[:nq, half, :], mergedT[:, k, qoff:qoff + nq], w_out_sb[:, k, half * 512:(half + 1) * 512],
                           k == 0, k == 7, reads=[merged_r, cw2_res], writes=[brr], wadd=(k > 0))
                z = pr_t[:nq, :, :].rearrange("p a b -> p (a b)")
                ACT(PT[0][:nq, :, :].rearrange("p a b -> p (a b)"), z, AF.Square, reads=[pr0, pr1],
                    writes=[PT_res[0], small_r], accum=small[:nq, 4:5])
                ACT(small[:nq, 5:6], small[:nq, 4:5], AF.Ln, reads=[small_r], writes=[small_r], scale=1.0 / D_MODEL, bias=eps_t[:nq, 0:1])
                ACT(small[:nq, 6:7], small[:nq, 5:6], AF.Exp, reads=[small_r], writes=[small_r], scale=-0.5)
                STT(yt[:nq, :], z, small[:nq, 6:7], post_g_bc[:nq, :], ALU.mult, ALU.mult,
                    reads=[pr0, pr1, small_r, c_res], writes=[yt_r])
                TT("gpsimd", yt[:nq, :], yt[:nq, :], xtok[:nq, :], ALU.add, reads=[yt_r, xtok_r], writes=[yt_r])
                P.final.append(DMA("sync", qb["y_rows"], yt[:nq, :], key="sty", reads=[yt_r]))


        STAGE = 99
        xT_seq_v = d_xT_seq.rearrange("(k p) t -> p k t", p=128)
        xT_own_v = d_xT_own.rearrange("(k p) t -> p k t", p=128)
        smp_done = []

        def prefix_tile(tl, kbsel=None, spaced=False):
            s = tl % 2
            if kbsel is None:
                kbsel = 3 if s == 0 else 7
            ck, sk = ck_ring[s], sk_ring[s]
            ck_r, sk_r = res(f"ck{s}"), res(f"sk{s}")
            DMA("sync", ck[:], d_cos_k[:, 2 * tl:2 * tl + 2, :], key=f"ck{s}", writes=[ck_r])
            DMA("sync", sk[:], d_sin_k[:, 2 * tl:2 * tl + 2, :], key=f"sk{s}", writes=[sk_r])
            yield from norm_tile(xT_seq_v[:, :, tl * TP:(tl + 1) * TP], TP, hT_p[s], res(f"hTp{s}"), kbsel=kbsel, spaced=spaced)
            yield from key_tile(hT_p[s], res(f"hTp{s}"), [(0, 128, 2 * tl), (128, 128, 2 * tl + 1)],
                                lambda blk, n: ck[:n, blk - 2 * tl, :], lambda blk, n: sk[:n, blk - 2 * tl, :],
                                o_ckv[tl * TP:(tl + 1) * TP, :].rearrange("(b p) c -> p b c", p=128),
                                o_kpe[tl * TP:(tl + 1) * TP, :].rearrange("(b p) c -> p b c", p=128),
                                kbsel=kbsel, spaced=spaced, tab_res=[ck_r, sk_r])
        CH_TILE = 8

        def run_rr(gens):
            live = list(gens)
            while live:
                for g_ in list(live):
                    try:
                        next(g_)
                    except StopIteration:
                        live.remove(g_)

        def pair_qbs(g):
            qbs = []
            for a in range(2):
                k = 2 * g + a
                nslot = 8 * g + 4 + 4 * a
                slots = [(b, 128, (b - (nslot - 4)) if b >= nslot - 4 else None) for b in range(nslot)]
                qbs.append(dict(off=128 * a, nq=128, slots=slots, cos=cos_q[:, k, :], sin=sin_q[:, k, :],
                                x_rows=d_x_own[k * 128:(k + 1) * 128, :], y_rows=o_y[k * 128:(k + 1) * 128, :],
                                kidx=k))
            return qbs

        def pair_part1(g, spaced_norm=False):
            return part1_gen(256, 4, pair_qbs(g), None, o_convp if g == 7 else None, f"p{g}",
                             hT_gs[g % 2], hTg_rs[g % 2], xT_own_v[:, :, g * TGP:(g + 1) * TGP], spaced_norm=spaced_norm)

        NUP = 4
        t0_, t1_ = prefix_tile(0), prefix_tile(1)
        next(t0_)
        x0_ = last_x["op"]
        next(t1_)
        x1_ = last_x["op"]
        emit_wprep([x0_, x1_])
        run_rr([t0_, t1_])
        run_rr([prefix_tile(2), prefix_tile(3)])
        g1 = pair_part1(0)
        next(g1)
        emit_initw2([last_x["op"]])
        CH_TILE = 52
        for _ in range(WR):
            w_issue()
        run(g1)
        for g in range(8):
            for tl in range(4 * g + NUP, min(4 * g + NUP + 4, 32)):
                bg_add(prefix_tile(tl, spaced=True), CH_TILE)
            attention(pair_qbs(g), f"p{g}")
            bg_drain()
            p3 = part3_gen(256, pair_qbs(g), hT_gs[g % 2], hTg_rs[g % 2])
            if g < 7:
                run_rr([pair_part1(g + 1, spaced_norm=3), p3])
            else:
                p3_last = p3
        p_done = [attn_last["p7"]]
        emit_cache(p_done)
        for b in range(SB0, SB0 + 8):
            kv_ready[b] = cw3_res
        smp_slots = [(b, 128, None) for b in range(SB0, SB0 + 8)] + [(SB0 + 8, DEC_SEQ, None)]
        smp_qbs = [dict(off=0, nq=DEC_SEQ, slots=smp_slots, cos=cos_s[:, :], sin=sin_s[:, :],
                        x_rows=d_x_smp, y_rows=o_ys, kidx=0)]
        g1 = part1_gen(DEC_SEQ, 0, smp_qbs, stateT, o_convs, "smp", hT_gs[0], hTg_rs[0],
                       d_xT_smp.rearrange("(k p) t -> p k t", p=128))
        next(g1)
        run(key_tile(hT_gs[0], hTg_rs[0], [(0, DEC_SEQ, SB0 + 8)],
                     lambda blk, n: cos_s[:n, :], lambda blk, n: sin_s[:n, :], o_ckv_s, o_kpe_s, extra_deps=p_done))
        run_rr([g1, p3_last])
        attention(smp_qbs, "smp")
        run(part3_gen(DEC_SEQ, smp_qbs, hT_gs[0], hTg_rs[0]))
        if record:
            return wseq
        assert STAGE < 99 or wq["used"] == wq["total"], (wq["used"], wq["total"])
        P.emit(nc, es)
    return nc


def _own_blocks(j):
    out = []
    for g in range(8):
        out += [8 * g + j, 8 * g + 7 - j]
    return out


def _rope_tables(pos):
    half = QK_ROPE // 2
    inv = (np.float32(10000.0) ** (-np.arange(half, dtype=np.float32) / np.float32(half))).astype(np.float32)
    ang = pos.astype(np.float32)[:, None] * inv[None, :]
    return np.cos(ang).astype(np.float32), np.sin(ang).astype(np.float32)


_NC_CACHE = {}


def kernel(x_prompt, x_sample, cache_kv_latent, cache_k_rope, state_conv, pre_norm, w_in, q_norm, w_uq, kv_norm,
           w_uk, w_uv, w_o_mla, conv_w, w_o_conv, w_out, post_norm):
    f = lambda a: np.ascontiguousarray(np.asarray(a, dtype=np.float32))
    x_prompt, x_sample = f(x_prompt), f(x_sample)
    W = f(w_in)[0]
    q0, kv0, kr0, gm0, cb0, cc0, cx0, gc0, mm0, mc0 = 0, 256, 384, 416, 928, 1440, 1952, 2464, 2976, 4000
    starts = [q0, q0 + 128] + [gm0 + 128 * i for i in range(4)]
    for jc in range(4):
        starts += [cc0 + 128 * jc, cx0 + 128 * jc, gc0 + 128 * jc, cb0 + 128 * jc]
    for oc in range(8):
        starts += [mm0 + 128 * oc, mc0 + 128 * oc]
    assert len(starts) == NCT
    cols = np.concatenate([np.arange(s, s + 128) for s in starts])
    w_in_t = f(W[:, cols].reshape(8, 128, NCT, 128).transpose(2, 1, 0, 3).reshape(NCT, 128, 1024))
    w_kv = f(W[:, kv0:kv0 + 160].reshape(8, 128, 160).transpose(1, 0, 2))
    wuq = f(w_uq)[0].reshape(Q_LORA, N_HEADS, QK_NOPE + QK_ROPE)
    w_uqT_nope = f(wuq[:, :, :QK_NOPE].transpose(2, 1, 0))
    w_ukT = f(f(w_uk)[0].transpose(2, 1, 0))
    w_uq_rope = f(wuq[:, :, QK_NOPE:].reshape(2, 128, N_HEADS * QK_ROPE).transpose(1, 0, 2))
    common = {
        "w_in_t": w_in_t, "w_kv": w_kv, "w_uqT_nope": w_uqT_nope, "w_ukT": w_ukT, "w_uq_rope": w_uq_rope,
        "w_uv": f(f(w_uv)[0]), "w_o_mla": f(f(w_o_mla)[0]), "w_o_conv": f(f(w_o_conv)[0]), "w_out": f(f(w_out)[0]),
        "pre_g": f(f(pre_norm)[0].reshape(8, 128).T), "q_g": f(f(q_norm)[0].reshape(2, 128).T),
        "kv_g": f(f(kv_norm)[0].reshape(1, 128)), "post_g": f(f(post_norm)[0].reshape(1, 1024)),
        "conv_w_t": f(f(conv_w)[0].reshape(3, 4, 128).transpose(2, 1, 0)),
        "ident": np.eye(128, dtype=np.float32),
    }
    ck, sk = _rope_tables(np.arange(SEQ))
    common["cos_k"] = f(ck.reshape(64, 128, 16).transpose(1, 0, 2))
    common["sin_k"] = f(sk.reshape(64, 128, 16).transpose(1, 0, 2))
    cs, ss = _rope_tables(PAST + np.arange(DEC_SEQ))
    common["cos_s"], common["sin_s"] = f(cs), f(ss)
    ckv_c, kpe_c, st_c = f(cache_kv_latent)[0], f(cache_k_rope)[0], f(state_conv)[0]

    in_maps = []
    own = []
    for c in range(8):
        b, j = c // 4, c % 4
        blks = _own_blocks(j)
        own.append(blks)
        xb = x_prompt[b]
        xT = f(xb.T)
        parts = []
        for g in range(8):
            A, B = blks[2 * g], blks[2 * g + 1]
            tok = list(range(128 * A, 128 * A + 128)) + list(range(128 * B, 128 * B + 128))
            main = xT[:, tok]
            halo = np.zeros((D_MODEL, 4), np.float32)
            for a, blk in enumerate((A, B)):
                if blk > 0:
                    halo[:, 2 * a:2 * a + 2] = xT[:, 128 * blk - 2:128 * blk]
            parts += [main, halo]
        xT_own = f(np.concatenate(parts, axis=1))
        x_own = f(np.concatenate([xb[128 * k:128 * k + 128] for k in blks], axis=0))
        pos_q = np.concatenate([np.arange(128 * k, 128 * k + 128) for k in blks])
        cq, sq = _rope_tables(pos_q)
        mbias = np.zeros((128, 128), np.float32)
        pidx = np.arange(128)
        for k, blk in enumerate(blks):
            g, a = k // 2, k % 2
            nslot = 8 * g + 4 + 4 * a
            for m in range(4):
                kb = nslot - 4 + m
                for hq in range(2):
                    vis = (2 * kb + pidx // 64) <= (2 * blk + hq)
                    mbias[:, k * 8 + m * 2 + hq] = np.where(vis, 0.0, -30000.0)
        d = dict(common)
        d.update({
            "xT_seq": xT, "xT_own": xT_own, "x_own": x_own,
            "xT_smp": f(x_sample[c].T), "x_smp": f(x_sample[c]),
            "cache_kvT": f(ckv_c[c].T), "cache_kv": f(ckv_c[c]), "cache_kpeT": f(kpe_c[c].T),
            "stateT": f(st_c[c].reshape(2, 4, 128).transpose(2, 1, 0)),
            "mbias": f(mbias),
            "cos_q": f(cq.reshape(16, 128, 16).transpose(1, 0, 2)), "sin_q": f(sq.reshape(16, 128, 16).transpose(1, 0, 2)),
        })
        in_maps.append(d)

    if "nc" not in _NC_CACHE:
        _NC_CACHE["nc"] = build_program(build_program(None))
    nc = _NC_CACHE["nc"]
    res = run_bass_kernel_spmd(nc, in_maps, core_ids=list(range(8)))
    R = res.results

    y_prompt = np.zeros((2, SEQ, D_MODEL), np.float32)
    y_sample = np.zeros((8, DEC_SEQ, D_MODEL), np.float32)
    ckv_p = np.zeros((1, 2, SEQ, 128), np.float32)
    kpe_p = np.zeros((1, 2, SEQ, 32), np.float32)
    cv_p = np.zeros((1, 2, 2, CONV_W), np.float32)
    ckv_s = np.zeros((1, 8, DEC_SEQ, 128), np.float32)
    kpe_s = np.zeros((1, 8, DEC_SEQ, 32), np.float32)
    cv_s = np.zeros((1, 8, 2, CONV_W), np.float32)
    for c in range(8):
        b, j = c // 4, c % 4
        r = R[c]
        oy = np.asarray(r["o_y"])
        for k, blk in enumerate(own[c]):
            y_prompt[b, 128 * blk:128 * blk + 128] = oy[128 * k:128 * k + 128]
        y_sample[c] = np.asarray(r["o_ys"])
        ckv_s[0, c] = np.asarray(r["o_ckv_s"])
        kpe_s[0, c] = np.asarray(r["o_kpe_s"])
        cv_s[0, c] = np.asarray(r["o_convs"]).transpose(2, 1, 0).reshape(2, CONV_W)
        if j == 0:
            ckv_p[0, b] = np.asarray(r["o_ckv"])
            kpe_p[0, b] = np.asarray(r["o_kpe"])
            cv_p[0, b] = np.asarray(r["o_convp"]).transpose(2, 1, 0).reshape(2, CONV_W)
    return (y_prompt, y_sample, ckv_p, kpe_p, cv_p, ckv_s, kpe_s, cv_s)
```
